# Optimizing a Trainium2 kernel written in Bass

```python
import jax, jax.numpy as jnp
from jax import lax
import numpy as np

D_MODEL = 1024
BATCH = 16
SEQ = 256
DEPTH = 2
DEC_BATCH = 8
DEC_SEQ = 4096
PAST_LEN = 256

GRID_W = 64
DH = 64
NA_HEADS = 8
NA_WIN_ROWS = 8
NA_WIN_COLS = 16
GQA_HEADS = 8
GQA_KV = 2
SWA_HEADS = 8
SWA_KV = 2
SWA_WINDOW = 128
ML_HEADS = 4
ML_DH = 128
ML_CHUNK = 64
Q_BLOCK = 128
N_BRANCH = 4
BRANCH_W = NA_HEADS * DH
D_FF = ((8 * D_MODEL // 3 + 255) // 256) * 256
ROPE_BASE = 10000.0
NORM_EPS = 1e-6
IN_SIZES = (NA_HEADS * DH, NA_HEADS * DH, NA_HEADS * DH,
            GQA_HEADS * DH, GQA_KV * DH, GQA_KV * DH,
            SWA_HEADS * DH, SWA_KV * DH, SWA_KV * DH,
            ML_HEADS * ML_DH, ML_HEADS * ML_DH, ML_HEADS * ML_DH, ML_HEADS * ML_DH, 2 * ML_HEADS, 2 * ML_HEADS,
            N_BRANCH * D_MODEL)
N_IN = sum(IN_SIZES)
ML_F_OFFSET = sum(IN_SIZES[:14])

kernel_name = 'hybrid_flow_trunk_step'


def _rmsnorm(x, g):
    xf = x.astype(jnp.float32)
    y = xf * lax.rsqrt(jnp.mean(xf * xf, axis=-1, keepdims=True) + NORM_EPS)
    return (y * g.astype(jnp.float32)).astype(x.dtype)


def _headnorm(t, g):
    return t * lax.rsqrt(jnp.mean(t * t, axis=-1, keepdims=True) + NORM_EPS) * g.astype(jnp.float32)


def _adaln(cvec, w, b):
    m = jnp.dot(jax.nn.silu(cvec), w) + b
    return jnp.split(m, 6, axis=-1)


def _modulate(xn, shift, scale):
    return xn * (1 + scale) + shift


def _rope2d(x):
    S = x.shape[1]
    half = DH // 2
    t = jnp.arange(S)
    freqs = ROPE_BASE ** (-jnp.arange(0, half, 2, dtype=jnp.float32) / half)

    def rot(xa, pos):
        ang = pos.astype(jnp.float32)[:, None] * freqs
        cos = jnp.cos(ang)[None, :, None, :]
        sin = jnp.sin(ang)[None, :, None, :]
        x1, x2 = jnp.split(xa, 2, axis=-1)
        return jnp.concatenate([x1 * cos - x2 * sin, x2 * cos + x1 * sin], axis=-1)

    return jnp.concatenate([rot(x[..., :half], t // GRID_W), rot(x[..., half:], t % GRID_W)], axis=-1)


def _branch_inputs(h, w_in, b_in, q_g, k_g, latent):
    z = (jnp.dot(h, w_in) + b_in).astype(jnp.float32)
    parts = []
    off = 0
    for size in IN_SIZES:
        parts.append(z[..., off:off + size])
        off += size
    (na_q, na_k, na_v, g_q, g_k, g_v, s_q, s_k, s_v, m_q, m_k, m_v, m_o, m_i, m_f, gates) = parts

    def heads(t, n):
        return t.reshape(t.shape[:-1] + (n, t.shape[-1] // n))

    g_q = _headnorm(heads(g_q, GQA_HEADS), q_g)
    g_k = _headnorm(heads(g_k, GQA_KV), k_g)
    s_q, s_k = heads(s_q, SWA_HEADS), heads(s_k, SWA_KV)
    if latent:
        g_q, g_k, s_q, s_k = _rope2d(g_q), _rope2d(g_k), _rope2d(s_q), _rope2d(s_k)
    lead = z.shape[:-1]
    return (heads(na_q, NA_HEADS), heads(na_k, NA_HEADS), heads(na_v, NA_HEADS),
            g_q, g_k, heads(g_v, GQA_KV),
            s_q, s_k, heads(s_v, SWA_KV),
            heads(m_q, ML_HEADS), heads(m_k, ML_HEADS) * ML_DH ** -0.5, heads(m_v, ML_HEADS), heads(m_o, ML_HEADS),
            m_i.reshape(lead + (2, ML_HEADS)),
            jax.nn.log_sigmoid(m_f).reshape(lead + (2, ML_HEADS)),
            jax.nn.sigmoid(gates).reshape(lead + (N_BRANCH, D_MODEL)))


def _attn_full(q, k, v, sink):
    B, L, HQ, _ = q.shape
    KV = k.shape[2]
    G = HQ // KV
    s = jnp.einsum('bqkgd,blkd->bkgql', q.reshape(B, L, KV, G, DH), k) * DH ** -0.5
    if sink is not None:
        s = jnp.concatenate([s, jnp.broadcast_to(sink.astype(jnp.float32).reshape(1, KV, G, 1, 1), s.shape[:-1] + (1,))], axis=-1)
    p = jax.nn.softmax(s, axis=-1)[..., :L]
    o = jnp.einsum('bkgql,blkd->bqkgd', p, v)
    return o.reshape(B, L, HQ * DH)


def _natten_latent(q, k, v, kc, vc, rpb):
    B, S, H, _ = q.shape
    rows = S // GRID_W
    wr = min(NA_WIN_ROWS, rows)
    wc = NA_WIN_COLS
    scale = DH ** -0.5
    qg = jnp.moveaxis(q.reshape(B, rows, GRID_W, H, DH), 1, 0)
    kg = k.reshape(B, rows, GRID_W, H, DH)
    vg = v.reshape(B, rows, GRID_W, H, DH)
    r_idx = jnp.arange(rows)
    row_start = jnp.clip(r_idx - wr // 2, 0, rows - wr)
    c_idx = jnp.arange(GRID_W)
    col_keys = jnp.clip(c_idx - wc // 2, 0, GRID_W - wc)[:, None] + jnp.arange(wc)[None, :]
    col_bias_idx = col_keys - c_idx[:, None] + (NA_WIN_COLS - 1)
    rpb = rpb.astype(jnp.float32)

    def row_block(args):
        q_r, r, rs = args
        k_sel = lax.dynamic_slice_in_dim(kg, rs, wr, axis=1)[:, :, col_keys]
        v_sel = lax.dynamic_slice_in_dim(vg, rs, wr, axis=1)[:, :, col_keys]
        row_bias_idx = rs + jnp.arange(wr) - r + (NA_WIN_ROWS - 1)
        bias = rpb[:, row_bias_idx[:, None, None], col_bias_idx[None, :, :]]
        s_nb = jnp.einsum('bqhd,brqchd->bhqrc', q_r, k_sel) * scale + jnp.transpose(bias, (0, 2, 1, 3))[None]
        s_nb = s_nb.reshape(B, H, GRID_W, wr * wc)
        s_ctx = jnp.einsum('bqhd,blhd->bhql', q_r, kc) * scale
        p = jax.nn.softmax(jnp.concatenate([s_nb, s_ctx], axis=-1), axis=-1)
        p_nb = p[..., :wr * wc].reshape(B, H, GRID_W, wr, wc)
        return (jnp.einsum('bhqrc,brqchd->bqhd', p_nb, v_sel)
                + jnp.einsum('bhql,blhd->bqhd', p[..., wr * wc:], vc))

    out = lax.map(row_block, (qg, r_idx, row_start))
    return jnp.moveaxis(out, 0, 1).reshape(B, S, H * DH)


def _gqa_dense_latent(q, k, v, kc, vc):
    B, S, HQ, _ = q.shape
    KV = k.shape[2]
    G = HQ // KV
    keys = jnp.concatenate([kc, k], axis=1)
    vals = jnp.concatenate([vc, v], axis=1)
    qb = jnp.moveaxis(q.reshape(B, S // Q_BLOCK, Q_BLOCK, KV, G, DH), 1, 0)

    def block(qi):
        s = jnp.einsum('bqkgd,bnkd->bkgqn', qi, keys) * DH ** -0.5
        return jnp.einsum('bkgqn,bnkd->bqkgd', jax.nn.softmax(s, axis=-1), vals)

    o = lax.map(block, qb)
    return jnp.moveaxis(o, 0, 1).reshape(B, S, HQ * DH)


def _swa_latent(q, k, v, kc, vc, sink):
    B, S, HQ, _ = q.shape
    KV = k.shape[2]
    G = HQ // KV
    BLK = SWA_WINDOW
    nb = S // BLK
    L = kc.shape[1]
    scale = DH ** -0.5

    def band(t):
        tb = jnp.pad(t, ((0, 0), (BLK, BLK), (0, 0), (0, 0))).reshape(B, nb + 2, BLK, KV, DH)
        return jnp.concatenate([tb[:, :-2], tb[:, 1:-1], tb[:, 2:]], axis=2)

    kb, vb = band(k), band(v)
    qb = q.reshape(B, nb, BLK, KV, G, DH)
    s_band = jnp.einsum('bnqkgd,bnjkd->bnkgqj', qb, kb) * scale
    blk = jnp.arange(nb)[:, None, None] * BLK
    qpos = blk + jnp.arange(BLK)[None, :, None]
    kpos = blk - BLK + jnp.arange(3 * BLK)[None, None, :]
    valid = (jnp.abs(qpos - kpos) <= SWA_WINDOW) & (kpos >= 0) & (kpos < S)
    s_band = jnp.where(valid[None, :, None, None], s_band, -jnp.inf)
    s_ctx = jnp.einsum('bnqkgd,blkd->bnkgql', qb, kc) * scale
    s_sink = jnp.broadcast_to(sink.astype(jnp.float32).reshape(1, 1, KV, G, 1, 1), s_ctx.shape[:-1] + (1,))
    p = jax.nn.softmax(jnp.concatenate([s_band, s_ctx, s_sink], axis=-1), axis=-1)
    nbk = 3 * BLK
    o = (jnp.einsum('bnkgqj,bnjkd->bnqkgd', p[..., :nbk], vb)
         + jnp.einsum('bnkgql,blkd->bnqkgd', p[..., nbk:nbk + L], vc))
    return o.reshape(B, S, HQ * DH)


def _mlstm_scan(q, k, v, i_pre, log_f, C0, n0, m0):
    B, S, H, d = q.shape
    nc = S // ML_CHUNK

    def chunks(t):
        t = t.reshape((B, nc, ML_CHUNK) + t.shape[2:])
        return jnp.moveaxis(jnp.moveaxis(t, 1, 0), 2, 3)

    tril = jnp.tril(jnp.ones((ML_CHUNK, ML_CHUNK), dtype=bool))

    def step(carry, xs):
        C, n, m = carry
        qc, kc, vc, ic, fc = xs
        b = jnp.cumsum(fc, axis=-1)
        Dm = jnp.where(tril, b[..., :, None] - b[..., None, :] + ic[..., None, :], -jnp.inf)
        inter = b + m[..., None]
        m_t = jnp.maximum(inter, jnp.max(Dm, axis=-1))
        w_intra = jnp.exp(Dm - m_t[..., None])
        w_inter = jnp.exp(inter - m_t)
        s = jnp.einsum('bhtd,bhsd->bhts', qc, kc) * w_intra
        num = w_inter[..., None] * jnp.einsum('bhtd,bhde->bhte', qc, C) + jnp.einsum('bhts,bhse->bhte', s, vc)
        den = w_inter * jnp.einsum('bhtd,bhd->bht', qc, n) + jnp.sum(s, axis=-1)
        h = num / jnp.maximum(jnp.abs(den), jnp.exp(-m_t))[..., None]
        b_last = b[..., -1]
        g = b_last[..., None] - b + ic
        m_new = jnp.maximum(b_last + m, jnp.max(g, axis=-1))
        w_s = jnp.exp(g - m_new[..., None])
        decay = jnp.exp(b_last + m - m_new)
        C_new = decay[..., None, None] * C + jnp.einsum('bhs,bhsd,bhse->bhde', w_s, kc, vc)
        n_new = decay[..., None] * n + jnp.einsum('bhs,bhsd->bhd', w_s, kc)
        return (C_new, n_new, m_new), h

    (C, n, m), h = lax.scan(step, (C0, n0, m0), tuple(chunks(t) for t in (q, k, v, i_pre, log_f)))
    h = jnp.moveaxis(jnp.moveaxis(h, 0, 1), 2, 3).reshape(B, S, H, d)
    return h, (C, n, m)


def _mlstm_bidir(q, k, v, i_pre, log_f, state_f, state_b):
    h_f, st_f = _mlstm_scan(q, k, v, i_pre[:, :, 0], log_f[:, :, 0], *state_f)
    rev = lambda t: jnp.flip(t, axis=1)
    h_b, st_b = _mlstm_scan(rev(q), rev(k), rev(v), rev(i_pre[:, :, 1]), rev(log_f[:, :, 1]), *state_b)
    return h_f + rev(h_b), st_f, st_b


def _mlstm_out(h, o_pre, g):
    y = h * jax.nn.sigmoid(o_pre)
    y = _headnorm(y, g.reshape(ML_HEADS, ML_DH))
    return y.reshape(y.shape[:-2] + (ML_HEADS * ML_DH,))


def _merge(outs, gates, w_b, w_o, dtype):
    merged = gates[..., 0, :] * jnp.dot(outs[0], w_b[0].astype(jnp.float32))
    for i in range(1, N_BRANCH):
        merged = merged + gates[..., i, :] * jnp.dot(outs[i], w_b[i].astype(jnp.float32))
    return jnp.dot(merged, w_o.astype(jnp.float32)).astype(dtype)


def _swiglu(h, w1, w2):
    a = jnp.dot(h, w1)
    gt, up = jnp.split(a, 2, axis=-1)
    return jnp.dot(jax.nn.silu(gt) * up, w2)


def setup_inputs(seed: int = 0) -> dict:
    key = jax.random.key(seed)
    ks = jax.random.split(key, 32)
    f32 = jnp.float32

    def nrm(k, shape, s):
        return jax.random.normal(k, shape, f32) * s

    f_bias = jnp.linspace(3.0, 6.0, 2 * ML_HEADS, dtype=f32)
    b_in = nrm(ks[15], (DEPTH, N_IN), 0.02).at[:, ML_F_OFFSET:ML_F_OFFSET + 2 * ML_HEADS].add(f_bias)
    return {
        'x_prompt': nrm(ks[0], (BATCH, SEQ, D_MODEL), 1.0),
        'x_sample': nrm(ks[1], (DEC_BATCH, DEC_SEQ, D_MODEL), 1.0),
        'cache_na_kv': nrm(ks[2], (DEC_BATCH, DEPTH, 2, PAST_LEN, NA_HEADS, DH), 1.0),
        'cache_gqa_kv': nrm(ks[3], (DEC_BATCH, DEPTH, 2, PAST_LEN, GQA_KV, DH), 1.0),
        'cache_swa_kv': nrm(ks[4], (DEC_BATCH, DEPTH, 2, PAST_LEN, SWA_KV, DH), 1.0),
        'state_mlstm_C': nrm(ks[5], (DEC_BATCH, DEPTH, 2, ML_HEADS, ML_DH, ML_DH), 0.1),
        'state_mlstm_n': nrm(ks[6], (DEC_BATCH, DEPTH, 2, ML_HEADS, ML_DH), 0.1),
        'state_mlstm_m': nrm(ks[7], (DEC_BATCH, DEPTH, 2, ML_HEADS), 1.0),
        'c': nrm(ks[8], (DEC_BATCH, D_MODEL), 1.0),
        'c_ctx': nrm(ks[9], (D_MODEL,), 1.0),
        'w_mod': nrm(ks[10], (DEPTH, D_MODEL, 6 * D_MODEL), 0.5 * D_MODEL ** -0.5),
        'b_mod': nrm(ks[11], (DEPTH, 6 * D_MODEL), 0.02),
        'norm1_g': 1.0 + nrm(ks[12], (DEPTH, D_MODEL), 0.02),
        'norm2_g': 1.0 + nrm(ks[13], (DEPTH, D_MODEL), 0.02),
        'w_in': nrm(ks[14], (DEPTH, D_MODEL, N_IN), D_MODEL ** -0.5),
        'b_in': b_in,
        'na_rpb': nrm(ks[16], (DEPTH, NA_HEADS, 2 * NA_WIN_ROWS - 1, 2 * NA_WIN_COLS - 1), 0.1),
        'gqa_q_g': 1.0 + nrm(ks[17], (DEPTH, DH), 0.02),
        'gqa_k_g': 1.0 + nrm(ks[18], (DEPTH, DH), 0.02),
        'swa_sink': nrm(ks[19], (DEPTH, SWA_HEADS), 1.0),
        'mlstm_norm_g': 1.0 + nrm(ks[20], (DEPTH, ML_HEADS * ML_DH), 0.02),
        'w_branch': nrm(ks[21], (DEPTH, N_BRANCH, BRANCH_W, D_MODEL), BRANCH_W ** -0.5),
        'w_out': nrm(ks[22], (DEPTH, D_MODEL, D_MODEL), D_MODEL ** -0.5),
        'w_ffn_in': nrm(ks[23], (DEPTH, D_MODEL, 2 * D_FF), D_MODEL ** -0.5),
        'w_ffn_out': nrm(ks[24], (DEPTH, D_FF, D_MODEL), D_FF ** -0.5),
        'final_norm_g': 1.0 + nrm(ks[25], (D_MODEL,), 0.02),
    }


def reference(x_prompt, x_sample, cache_na_kv, cache_gqa_kv, cache_swa_kv, state_mlstm_C, state_mlstm_n,
              state_mlstm_m, c, c_ctx, w_mod, b_mod, norm1_g, norm2_g, w_in, b_in, na_rpb, gqa_q_g, gqa_k_g,
              swa_sink, mlstm_norm_g, w_branch, w_out, w_ffn_in, w_ffn_out, final_norm_g):
    f32 = jnp.float32
    Bp = x_prompt.shape[0]
    zero_state = (jnp.zeros((Bp, ML_HEADS, ML_DH, ML_DH), f32), jnp.zeros((Bp, ML_HEADS, ML_DH), f32),
                  jnp.zeros((Bp, ML_HEADS), f32))
    xp, xs = x_prompt, x_sample
    na_list, gqa_list, swa_list, C_list, n_list, m_list = [], [], [], [], [], []
    for l in range(DEPTH):
        sh1, sc1, gt1, sh2, sc2, gt2 = _adaln(c_ctx, w_mod[l], b_mod[l])
        h = _modulate(_rmsnorm(xp, norm1_g[l]), sh1, sc1)
        (na_q, na_k, na_v, g_q, g_k, g_v, s_q, s_k, s_v, m_q, m_k, m_v, m_o, m_i, m_f, gates) = \
            _branch_inputs(h, w_in[l], b_in[l], gqa_q_g[l], gqa_k_g[l], False)
        o_na = _attn_full(na_q, na_k, na_v, None)
        o_gqa = _attn_full(g_q, g_k, g_v, None)
        o_swa = _attn_full(s_q, s_k, s_v, swa_sink[l])
        h_ml, st_f, st_b = _mlstm_bidir(m_q, m_k, m_v, m_i, m_f, zero_state, zero_state)
        o_ml = _mlstm_out(h_ml, m_o, mlstm_norm_g[l])
        xp = xp + gt1 * _merge((o_na, o_gqa, o_swa, o_ml), gates, w_branch[l], w_out[l], xp.dtype)
        xp = xp + gt2 * _swiglu(_modulate(_rmsnorm(xp, norm2_g[l]), sh2, sc2), w_ffn_in[l], w_ffn_out[l])
        na_list.append(jnp.stack([na_k, na_v], axis=1))
        gqa_list.append(jnp.stack([g_k, g_v], axis=1))
        swa_list.append(jnp.stack([s_k, s_v], axis=1))
        C_list.append(jnp.stack([st_f[0], st_b[0]], axis=1))
        n_list.append(jnp.stack([st_f[1], st_b[1]], axis=1))
        m_list.append(jnp.stack([st_f[2], st_b[2]], axis=1))

        sh1, sc1, gt1, sh2, sc2, gt2 = [m[:, None, :] for m in _adaln(c, w_mod[l], b_mod[l])]
        h = _modulate(_rmsnorm(xs, norm1_g[l]), sh1, sc1)
        (na_q, na_k, na_v, g_q, g_k, g_v, s_q, s_k, s_v, m_q, m_k, m_v, m_o, m_i, m_f, gates) = \
            _branch_inputs(h, w_in[l], b_in[l], gqa_q_g[l], gqa_k_g[l], True)
        kv_na = cache_na_kv[:, l].astype(f32)
        kv_g = cache_gqa_kv[:, l].astype(f32)
        kv_s = cache_swa_kv[:, l].astype(f32)
        o_na = _natten_latent(na_q, na_k, na_v, kv_na[:, 0], kv_na[:, 1], na_rpb[l])
        o_gqa = _gqa_dense_latent(g_q, g_k, g_v, kv_g[:, 0], kv_g[:, 1])
        o_swa = _swa_latent(s_q, s_k, s_v, kv_s[:, 0], kv_s[:, 1], swa_sink[l])
        lat_f = (state_mlstm_C[:, l, 0].astype(f32), state_mlstm_n[:, l, 0].astype(f32), state_mlstm_m[:, l, 0].astype(f32))
        lat_b = (state_mlstm_C[:, l, 1].astype(f32), state_mlstm_n[:, l, 1].astype(f32), state_mlstm_m[:, l, 1].astype(f32))
        h_ml, _, _ = _mlstm_bidir(m_q, m_k, m_v, m_i, m_f, lat_f, lat_b)
        o_ml = _mlstm_out(h_ml, m_o, mlstm_norm_g[l])
        xs = xs + gt1 * _merge((o_na, o_gqa, o_swa, o_ml), gates, w_branch[l], w_out[l], xs.dtype)
        xs = xs + gt2 * _swiglu(_modulate(_rmsnorm(xs, norm2_g[l]), sh2, sc2), w_ffn_in[l], w_ffn_out[l])

    y_prompt = _rmsnorm(xp, final_norm_g)
    y_sample = _rmsnorm(xs, final_norm_g)
    new_na_kv = jnp.stack(na_list, axis=1)
    new_gqa_kv = jnp.stack(gqa_list, axis=1)
    new_swa_kv = jnp.stack(swa_list, axis=1)
    new_ml_C = jnp.stack(C_list, axis=1)
    new_ml_n = jnp.stack(n_list, axis=1)
    new_ml_m = jnp.stack(m_list, axis=1)
    return (y_prompt, y_sample, new_na_kv, new_gqa_kv, new_swa_kv, new_ml_C, new_ml_n, new_ml_m)
```

```python
import math
from contextlib import ExitStack

import numpy as np
import ml_dtypes

import concourse.bass as bass
import concourse.mybir as mybir
from concourse.alu_op_type import AluOpType as ALU
from concourse.bass_utils import run_bass_kernel_spmd

F32 = mybir.dt.float32
BF16 = mybir.dt.bfloat16
AF = mybir.ActivationFunctionType
AX = mybir.AxisListType

D = 1024
DEPTH = 2
NT = 36
NTS = 32
TOK = NT * 128
SEQ_S = 4096
GRID_W = 64
DH = 64
N_IN = 9232
D_FF = 2816
EPS = 1e-6
NEG = -30000.0
O_NAQ, O_NAK, O_NAV = 0, 512, 1024
O_GQ, O_GK, O_GV = 1536, 2048, 2176
O_SQ, O_SK, O_SV = 2304, 2816, 2944
O_MQ, O_MK, O_MV, O_MO, O_MI, O_MF = 3072, 3584, 4096, 4608, 5120, 5128
O_GATES = 5136

STREAMS = ("pe", "act", "dve", "pool", "sp")


class Res:
    __slots__ = ("name", "w", "r", "const")

    def __init__(self, name):
        self.name = name
        self.w = None
        self.r = []
        self.const = False


class Op:
    __slots__ = ("stream", "fn", "deps", "flag", "dma_key", "dma_val", "rank", "idx", "extra_waits")

    def __init__(self, stream, fn):
        self.stream = stream
        self.fn = fn
        self.deps = []
        self.flag = False
        self.dma_key = None
        self.dma_val = 0
        self.rank = 0
        self.idx = 0
        self.extra_waits = None


class Prog:
    def __init__(self):
        self.ops = []
        self.last = {s: None for s in STREAMS}
        self.dma_count = {}
        self.dma_last = {}
        self.key2sem = {}

    def _dep(self, o, p):
        if p is None or p is o:
            return
        if p.dma_key is None and o.dma_key is None and p.stream == "pe" and o.stream == "pe":
            return
        o.deps.append(p)
        p.flag = True

    def add(self, stream, fn, reads=(), writes=(), dma_key=None):
        o = Op(stream, fn)
        o.idx = len(self.ops)
        if dma_key is not None:
            ns = "sw" if stream == "pool" else "hw"
            if (ns, dma_key) not in self.key2sem:
                self.key2sem[(ns, dma_key)] = (ns, sum(1 for k in self.key2sem if k[0] == ns))
            dma_key = self.key2sem[(ns, dma_key)]
            o.dma_key = dma_key
            c = self.dma_count.get(dma_key, 0) + 16
            self.dma_count[dma_key] = c
            o.dma_val = c
            self.dma_last[dma_key] = o
        for r in reads:
            if r.const:
                continue
            self._dep(o, r.w)
            r.r.append(o)
        for w in writes:
            assert not w.const, w.name
            self._dep(o, w.w)
            for rd in w.r:
                self._dep(o, rd)
            w.w = o
            w.r = []
        self.ops.append(o)
        self.last[stream] = o
        return o

    def barrier(self):
        lasts = [self.last[s] for s in STREAMS if self.last[s] is not None]
        dmas = list(self.dma_last.values())
        for s in STREAMS:
            o = Op(s, None)
            o.idx = len(self.ops)
            for p in lasts + dmas:
                if p is not None:
                    o.deps.append(p)
                    p.flag = True
            self.ops.append(o)
        self.key2sem = {}

    def emit(self, nc, stack):
        cnt = {s: 0 for s in STREAMS}
        for o in self.ops:
            if o.dma_key is None and o.flag and o.fn is not None:
                cnt[o.stream] += 1
                o.rank = cnt[o.stream]
        esem = {s: stack.enter_context(nc.semaphore("e_" + s)) for s in STREAMS}
        dsem = {k: stack.enter_context(nc.semaphore("d_%d" % i)) for i, k in enumerate(self.dma_count)}
        self.n_sems = len(esem) + len(dsem)
        by_stream = {s: [] for s in STREAMS}
        for o in self.ops:
            by_stream[o.stream].append(o)
        key_prog = {}
        upgraded = {}
        for o in self.ops:
            for p in o.deps:
                if p.dma_key is not None:
                    upgraded[(o.idx, p.idx)] = max(p.dma_val, key_prog.get(p.dma_key, 0))
            if o.dma_key is not None:
                key_prog[o.dma_key] = o.dma_val

        def run_stream(s, eng):
            seen = {}
            for o in by_stream[s]:
                need = {}
                for p in o.deps:
                    if p.dma_key is not None:
                        sem = dsem[p.dma_key]
                        val = upgraded[(o.idx, p.idx)]
                        kk = ("d", p.dma_key)
                    else:
                        if p.fn is None:
                            continue
                        sem = esem[p.stream]
                        val = p.rank
                        kk = ("e", p.stream)
                    if seen.get(kk, 0) >= val:
                        continue
                    if kk not in need or need[kk][1] < val:
                        need[kk] = (sem, val)
                for kk, (sem, val) in need.items():
                    eng.wait_ge(sem, val)
                    seen[kk] = val
                if o.fn is None:
                    continue
                ins = o.fn(eng)
                if o.dma_key is not None:
                    ins.then_inc(dsem[o.dma_key], 16)
                elif o.flag:
                    ins.then_inc(esem[s], 1)

        block = stack.enter_context(nc.Block())

        @block.tensor
        def _(e):
            run_stream("pe", e)

        @block.scalar
        def _(e):
            run_stream("act", e)

        @block.vector
        def _(e):
            run_stream("dve", e)

        @block.gpsimd
        def _(e):
            run_stream("pool", e)

        @block.sync
        def _(e):
            run_stream("sp", e)


class Arena:
    def __init__(self, tensor, ncols):
        self.t = tensor
        self.ncols = ncols
        self.top = 0
        self.marks = []
        self.peak = 0

    def push(self):
        self.marks.append(self.top)

    def pop(self):
        self.top = self.marks.pop()

    def alloc(self, cols, dtype=F32, parts=128):
        n32 = cols if dtype == F32 else (cols + 1) // 2
        n32 = (n32 + 7) // 8 * 8
        off = self.top
        self.top += n32
        assert self.top <= self.ncols, "SBUF arena overflow: %d > %d" % (self.top, self.ncols)
        self.peak = max(self.peak, self.top)
        v = self.t[0:parts, off:off + n32]
        if dtype != F32:
            v = v.bitcast(dtype)[:, 0:cols]
        else:
            v = v[:, 0:cols]
        return v


class Rot:
    def __init__(self, items):
        self.items = items
        self.i = 0

    def next(self):
        it = self.items[self.i % len(self.items)]
        self.i += 1
        return it


def _h(ap, pattern, **kw):
    return ap.rearrange(pattern, **kw)


class Pipe:
    def __init__(self):
        self.active = []

    def push(self, gen):
        if gen is not None:
            self.active.insert(0, gen)
        self.step()

    def step(self):
        for g in list(self.active):
            try:
                next(g)
            except StopIteration:
                self.active.remove(g)

    def flush(self):
        while self.active:
            self.step()


def _na_kbs(a):
    rows = set()
    for r in (2 * a, 2 * a + 1):
        rs = min(max(r - 4, 0), 56)
        rows.update(range(rs, rs + 8))
    return sorted(set(r // 2 for r in rows))


def _na_class(a):
    return {0: 0, 1: 1, 30: 3, 31: 4}.get(a, 2)


def _na_tile_index():
    idx = {}
    for a in (0, 1, 2, 30, 31):
        for kb in reversed(_na_kbs(a)):
            key = (_na_class(a), kb - a)
            if key not in idx:
                idx[key] = len(idx)
    return idx


NA_TILE_IDX = _na_tile_index()
NA_NB = len(NA_TILE_IDX)


def _na_gather_indices():
    out = {}
    reps = {0: 0, 1: 1, 2: 2, 3: 30, 4: 31}
    for (cls, delta), tid in NA_TILE_IDX.items():
        a = reps[cls]
        kb = a + delta
        kk = np.arange(128)
        kr = 2 * kb + kk // 64
        kc = kk % 64
        qr = 2 * a + kk // 64
        qc = kk % 64
        rs = np.clip(qr - 4, 0, 56)
        cs = np.clip(qc - 8, 0, 48)
        valid = ((kr[:, None] >= rs[None, :]) & (kr[:, None] < rs[None, :] + 8) &
                 (kc[:, None] >= cs[None, :]) & (kc[:, None] < cs[None, :] + 16))
        dr = np.clip(kr[:, None] - qr[None, :] + 7, 0, 14)
        dc = np.clip(kc[:, None] - qc[None, :] + 15, 0, 30)
        out[tid] = (dr, dc, valid)
    return out


def _host_consts():
    c = {}
    c["identb"] = np.eye(128, dtype=np.float32).astype(ml_dtypes.bfloat16)
    c["identf"] = np.eye(128, dtype=np.float32)
    s = np.arange(128)
    le = (s[:, None] <= s[None, :]).astype(np.float32)
    ge = (s[:, None] >= s[None, :]).astype(np.float32)
    c["tri_f32"] = np.stack([le, ge, np.ones((128, 128), np.float32)], 0)
    c["tri_bf"] = np.stack([le, ge], 0).astype(ml_dtypes.bfloat16)
    c["swa_mask"] = np.stack([np.where(s[None, :] <= s[:, None], 0.0, NEG),
                              np.where(s[:, None] <= s[None, :], 0.0, NEG)], 0).astype(ml_dtypes.bfloat16)
    half = 32
    freqs = (10000.0 ** (-np.arange(0, half, 2, dtype=np.float32) / half)).astype(np.float32)
    t = np.arange(SEQ_S)
    ang_r = (t // GRID_W).astype(np.float32)[:, None] * freqs
    ang_c = (t % GRID_W).astype(np.float32)[:, None] * freqs
    cr, sr, cc, sc = np.cos(ang_r), np.sin(ang_r), np.cos(ang_c), np.sin(ang_c)
    c["rope_cos"] = np.concatenate([cr, cr, cc, cc], 1).astype(np.float32)
    c["rope_sin"] = np.concatenate([-sr, sr, -sc, sc], 1).astype(np.float32)
    return c


HOST_CONSTS = None


class KB:
    def __init__(self, debug=None, stop_after=None, layers=DEPTH):
        self.debug = debug or ()
        self.stop_after = stop_after
        self.layers = layers
        self.nc = bass.Bass("TRN2", target_bir_lowering=False)
        self.P = Prog()
        self.din = {}
        self.dout = {}
        self.scr = {}

    def _in(self, name, shape, dtype=F32):
        self.din[name] = self.nc.dram_tensor(name, list(shape), dtype, kind="ExternalInput").ap()

    def _out(self, name, shape, dtype=F32):
        self.dout[name] = self.nc.dram_tensor(name, list(shape), dtype, kind="ExternalOutput").ap()

    def _scr(self, name, shape, dtype):
        kind = "ExternalOutput" if name in self.debug else "Internal"
        self.scr[name] = self.nc.dram_tensor("scr_" + name, list(shape), dtype, kind=kind).ap()

    def declare(self):
        i = self._in
        i("xs", [4096, D]); i("xp", [512, D])
        i("cna", [2, 2, 256, 512]); i("cg", [2, 2, 256, 128]); i("cs", [2, 2, 256, 128])
        i("stC", [2, 2, 4, 128, 128]); i("stn", [2, 2, 4, 128]); i("stm", [2, 8])
        i("cvec", [2, D])
        i("w_mod", [2, D, 6 * D]); i("b_mod", [2, 6 * D]); i("norm1_g", [2, D]); i("norm2_g", [2, D])
        i("w_in", [2, D, N_IN]); i("b_in", [2, N_IN])
        i("gqa_q_g", [2, 64]); i("gqa_k_g", [2, 64]); i("swa_sink", [2, 8]); i("mlstm_norm_g", [2, 512])
        i("w_branch", [2, 4, 512, D]); i("w_out", [2, D, D]); i("w_ffn_in", [2, D, 2 * D_FF])
        i("w_ffn_out", [2, D_FF, D]); i("final_norm_g", [1, D])
        i("nabias", [2, 8, NA_NB, 128, 128])
        i("identb", [128, 128], BF16); i("identf", [128, 128]); i("tri_f32", [3, 128, 128])
        i("tri_bf", [2, 128, 128], BF16); i("swa_mask", [2, 128, 128], BF16)
        i("rope_cos", [4096, 64]); i("rope_sin", [4096, 64])
        o = self._out
        o("y_s", [4096, D]); o("y_p", [512, D])
        o("o_na", [2, 2, 2, 256, 512]); o("o_g", [2, 2, 2, 256, 128]); o("o_s", [2, 2, 2, 256, 128])
        o("o_C", [2, 2, 2, 4, 128, 128]); o("o_n", [2, 2, 2, 4, 128]); o("o_m", [2, 2, 2, 4])
        s = self._scr
        s("XRES", [TOK, D], F32)
        s("QT_na", [512, TOK], BF16); s("KT_na", [512, TOK], BF16); s("V_na", [TOK, 512], BF16)
        s("QT_g", [512, TOK], BF16); s("KT_g", [128, TOK], BF16); s("V_g", [TOK, 128], BF16)
        s("QT_s", [512, TOK], BF16); s("KT_s", [128, TOK], BF16); s("V_s", [TOK, 128], BF16)
        s("QT_m", [512, TOK], BF16); s("KT_m", [512, TOK], BF16); s("K_m", [TOK, 512], BF16)
        s("V_m", [TOK, 4 * 129], BF16); s("O_m", [TOK, 512], F32)
        s("GT", [4096, TOK], BF16)
        s("HF", [TOK, 512], F32)
        s("OT", [4, 512, TOK], BF16)
        s("H2T", [D, TOK], BF16)
        if "HT" in self.debug:
            s("HT", [D, TOK], BF16)
        s("MOD", [2, 128, 6 * D], F32)
        if "IFD" in self.debug:
            s("IFD", [128, NT * 16], F32)

    def dma(self, out, in_, reads=(), writes=(), key=None, q="sp", slow=False):
        assert key is not None
        if slow:
            fn = lambda e: e.dma_start(out=out, in_=in_, allow_slow_non_contiguous=True)
        else:
            fn = lambda e: e.dma_start(out=out, in_=in_)
        return self.P.add(q, fn, reads=reads, writes=writes, dma_key=key)

    def mm(self, out, lhsT, rhs, start, stop, reads=(), writes=(), **kw):
        return self.P.add("pe", lambda e: e.matmul(out, lhsT=lhsT, rhs=rhs, start=start, stop=stop, **kw),
                          reads=reads, writes=writes)

    def tr(self, out, in_, ident, reads=(), writes=()):
        return self.P.add("pe", lambda e: e.transpose(out=out, in_=in_, identity=ident), reads=reads, writes=writes)

    def act(self, out, in_, func, reads=(), writes=(), **kw):
        return self.P.add("act", lambda e: e.activation(out=out, in_=in_, func=func, **kw), reads=reads, writes=writes)

    def tt(self, out, in0, in1, op, reads=(), writes=(), eng="dve"):
        return self.P.add(eng, lambda e: e.tensor_tensor(out=out, in0=in0, in1=in1, op=op), reads=reads, writes=writes)

    def ts(self, out, in0, s1, op0, s2=None, op1=None, reads=(), writes=(), eng="dve"):
        if op1 is None:
            fn = lambda e: e.tensor_scalar(out=out, in0=in0, scalar1=s1, scalar2=None, op0=op0)
        else:
            fn = lambda e: e.tensor_scalar(out=out, in0=in0, scalar1=s1, scalar2=s2, op0=op0, op1=op1)
        return self.P.add(eng, fn, reads=reads, writes=writes)

    def stt(self, out, in0, scalar, in1, op0, op1, reads=(), writes=()):
        return self.P.add("dve", lambda e: e.scalar_tensor_tensor(out=out, in0=in0, scalar=scalar, in1=in1, op0=op0, op1=op1),
                          reads=reads, writes=writes)

    def cp(self, out, in_, reads=(), writes=(), eng="dve"):
        if eng == "act":
            return self.act(out, in_, AF.Copy, reads=reads, writes=writes)
        return self.P.add(eng, lambda e: e.tensor_copy(out=out, in_=in_), reads=reads, writes=writes)

    def red(self, out, in_, op, reads=(), writes=()):
        return self.P.add("dve", lambda e: e.tensor_reduce(out=out, in_=in_, axis=AX.X, op=op), reads=reads, writes=writes)

    def recip(self, out, in_, reads=(), writes=()):
        return self.P.add("dve", lambda e: e.reciprocal(out=out, in_=in_), reads=reads, writes=writes)

    def recip_fast(self, out, in_, reads=(), writes=()):
        return self.P.add("dve", lambda e: e.reciprocal_approx_fast(out=out, in_=in_), reads=reads, writes=writes)

    def memset(self, ap, val, writes=(), eng="dve"):
        return self.P.add(eng, lambda e: e.memset(ap, val), writes=writes)

    def rstd(self, out, ss, scale, reads=(), writes=(), tmp=None):
        self.act(tmp, ss, AF.Ln, reads=reads, writes=writes, scale=scale, bias=self.eps_col[0:ss.shape[0], :])
        self.act(out, tmp, AF.Exp, reads=writes, writes=writes, scale=-0.5)

    def alloc(self, cols, dtype=F32, name="t", parts=128):
        return self.A.alloc(cols, dtype, parts), Res(name)

    def warm(self, reads=(), n=24, bank=7):
        for i in range(n):
            self.mm(self.psb(bank), self.identb, self.wz, True, True, reads=list(reads), writes=[self.r_ps[bank]])

    def rot(self, n, cols, dtype=F32, name="r"):
        return Rot([self.alloc(cols, dtype, "%s%d" % (name, i)) for i in range(n)])

    def tile_rows(self, ti):
        if ti < NTS:
            return self.din["xs"][ti * 128:(ti + 1) * 128, :]
        return self.din["xp"][(ti - NTS) * 128:(ti - NTS + 1) * 128, :]

    def build(self):
        nc = self.nc
        self.declare()
        st = ExitStack()
        with st:
            arena_t = st.enter_context(nc.sbuf_tensor("arena", [128, 52000], F32))
            self.ps_t = st.enter_context(nc.psum_tensor("psum", [128, 8, 512], F32))
            self.A = Arena(arena_t, 52000)
            self.r_ps = [Res("ps%d" % i) for i in range(8)]
            self.r_xres = [Res("xres%d" % i) for i in range(NT)]
            self.body()
            self.P.barrier()
            self.P.emit(nc, st)
        return nc

    def psb(self, b):
        return self.ps_t[:, b, :]

    def psb_bf(self, b):
        return self.ps_t[:, b, :].bitcast(BF16)

    def done(self, name):
        return self.stop_after == name

    def body(self):
        self.preamble()
        for l in range(self.layers):
            self.A.push()
            self.A.push()
            self.phase_A(l)
            self.A.pop()
            if self.done("A%d" % l):
                return
            self.IF, self.r_IF = self.alloc(NT * 16, F32, "IF")
            self.A.push()
            self.hT, _ = self.alloc(8 * TOK, BF16, "hT")
            self.r_hT = [Res("hT%d" % i) for i in range(NT)]
            self.phase_B(l)
            if self.done("B%d" % l):
                return
            self.phase_C(l)
            self.P.barrier()
            self.A.pop()
            if self.done("C%d" % l):
                return
            if not getattr(self, "skip_D", False):
                self.phase_D(l)
            if self.done("D%d" % l):
                return
            if not getattr(self, "skip_E", False):
                self.phase_E(l)
            if self.done("E%d" % l):
                return
            self.phase_F(l)
            if self.done("F%d" % l):
                return
            self.phase_G(l, l == self.layers - 1)
            if self.done("G%d" % l):
                return
            self.A.pop()

    def preamble(self):
        A = self.A
        d = self.din
        self.identb, r = self.alloc(128, BF16, "identb")
        self.dma(self.identb, d["identb"][:, :], writes=[r], key="c0")
        self.r_const = r
        self.identf, _ = self.alloc(128, F32, "identf")
        self.dma(self.identf, d["identf"][:, :], writes=[r], key="c0")
        self.tri = []
        for k in range(3):
            t, _ = self.alloc(128, F32, "tri")
            self.dma(t, d["tri_f32"][k], writes=[r], key="c0")
            self.tri.append(t)
        self.trib = []
        for k in range(2):
            t, _ = self.alloc(128, BF16, "trib")
            self.dma(t, d["tri_bf"][k], writes=[r], key="c0")
            self.trib.append(t)
        self.swam = []
        for k in range(2):
            t, _ = self.alloc(128, BF16, "swam")
            self.dma(t, d["swa_mask"][k], writes=[r], key="c0")
            self.swam.append(t)
        self.wz, _ = self.alloc(512, BF16, "wz")
        self.memset(self.wz, 0.0, writes=[r])
        self.eps_col, _ = self.alloc(1, F32, "eps")
        self.memset(self.eps_col, EPS, writes=[r])
        self.one_col, _ = self.alloc(1, F32, "one")
        self.memset(self.one_col, 1.0, writes=[r])
        cvT, rc = self.alloc(16, F32, "cvT")
        self.dma(_h(cvT, "p (g k) -> p g k", g=2), _h(d["cvec"], "g (k p) -> p g k", p=128), writes=[rc], key="c1", slow=True)
        sv, rs = self.alloc(16, F32, "sv")
        self.act(sv, cvT, AF.Silu, reads=[rc], writes=[rs])
        self.cb = []
        for g in range(2):
            t, _ = self.alloc(8 * 128, BF16, "cb")
            self.cp(_h(t, "p (k m) -> p k m", k=8), sv[:, g * 8:(g + 1) * 8].unsqueeze(2).broadcast_to([128, 8, 128]),
                    reads=[rs], writes=[r])
            self.cb.append(t)
        self.gfin, _ = self.alloc(D, F32, "gfin")
        self.dma(self.gfin, d["final_norm_g"].broadcast_to([128, D]), writes=[r], key="c0")
        self.P.barrier()
        r.const = True

    def phase_A(self, l):
        d = self.din
        rc = self.r_const
        self.MOD = []
        self.r_MOD = []
        for g in range(2):
            t, r = self.alloc(6 * D, F32, "MOD%d" % g)
            self.MOD.append(t)
            self.r_MOD.append(r)
        self.A.push()
        wsl = self.rot(2, 8 * 512, BF16, "wmod")
        bsl = self.rot(2, 512, F32, "bmod")
        pr = Rot([0, 1, 2, 3])
        for cg in range(12):
            w, rw = wsl.next()
            b, rb = bsl.next()
            self.dma(_h(w, "p (k c) -> p k c", k=8), _h(d["w_mod"][l][:, cg * 512:(cg + 1) * 512], "(k p) c -> p k c", p=128),
                     writes=[rw], key=("wmod", id(rw)), q="pool")
            self.dma(b, d["b_mod"][l:l + 1, cg * 512:(cg + 1) * 512].broadcast_to([128, 512]), writes=[rb], key=("bmod", id(rb)))
            for g in range(2):
                pb = pr.next()
                for k in range(8):
                    self.mm(self.psb(pb), self.cb[g][:, k * 128:(k + 1) * 128], w[:, k * 512:(k + 1) * 512], k == 0, k == 7,
                            reads=[rw, rc], writes=[self.r_ps[pb]])
                self.tt(self.MOD[g][:, cg * 512:(cg + 1) * 512], self.psb(pb), b, ALU.add,
                        reads=[self.r_ps[pb], rb], writes=[self.r_MOD[g]])
        gt, rg = self.alloc(D, F32, "gtmp")
        for (col, nm) in ((1, "norm1_g"), (4, "norm2_g")):
            self.dma(gt, d[nm][l:l + 1, :].broadcast_to([128, D]), writes=[rg], key="gtmp")
            for g in range(2):
                v = self.MOD[g][:, col * D:(col + 1) * D]
                self.stt(v, v, 1.0, gt, ALU.add, ALU.mult, reads=[rg, self.r_MOD[g]], writes=[self.r_MOD[g]])
        for g in range(2):
            self.dma(self.scr["MOD"][g], self.MOD[g], reads=[self.r_MOD[g]], key="modst")
        self.P.barrier()
        self.A.pop()

    def load_mod(self, idxs):
        self.modt = {}
        self.r_modt, = [Res("modt")]
        for g in range(2):
            for idx in idxs:
                t, _ = self.alloc(D, F32, "mod%d_%d" % (g, idx))
                self.dma(t, self.scr["MOD"][g][:, idx * D:(idx + 1) * D], writes=[self.r_modt], key="modld")
                self.modt[(g, idx)] = t

    def mod(self, g, idx):
        return self.modt[(g, idx)]

    def norm_mod_T(self, xt, rx, g, gi, si, dst_fn, bufs):
        junk, rj = bufs["junk"].next()
        ssb, rss = bufs["ss"].next()
        t, rt = bufs["t"].next()
        hb, rhb = bufs["hb"].next()
        self.act(junk, xt, AF.Square, reads=[rx], writes=[rj, rss], accum_out=ssb[:, 0:1])
        self.rstd(ssb[:, 2:3], ssb[:, 0:1], 1.0 / D, reads=[rss], writes=[rss], tmp=ssb[:, 1:2])
        self.stt(t, xt, ssb[:, 2:3], self.mod(g, gi), ALU.mult, ALU.mult, reads=[rx, rss, self.r_modt], writes=[rt])
        self.tt(hb, t, self.mod(g, si), ALU.add, reads=[rt, self.r_modt], writes=[rhb], eng="pool")
        yield
        pb = bufs["pt"].next()
        pv = self.psb_bf(pb)
        for k in range(8):
            self.tr(pv[:, k * 128:(k + 1) * 128], hb[:, k * 128:(k + 1) * 128], self.identb, reads=[rhb], writes=[self.r_ps[pb]])
        yield
        dst_fn(pv, self.r_ps[pb])

    def phase_B(self, l):
        self.A.push()
        self.load_mod((0, 1))
        xr = self.rot(5, D, F32, "xt")
        bufs = {"junk": self.rot(2, D, BF16, "junk"), "ss": self.rot(4, 4, F32, "ss"), "t": self.rot(2, D, F32, "t"),
                "hb": self.rot(3, D, BF16, "hb"), "pt": Rot([5, 6, 7])}
        pipe = Pipe()
        hT3 = _h(self.hT, "p (k t) -> p k t", k=8)
        loaded = {}

        def load(ti):
            xt, rx = xr.next()
            src = self.tile_rows(ti) if l == 0 else self.scr["XRES"][ti * 128:(ti + 1) * 128, :]
            self.dma(xt, src, reads=[self.r_xres[ti]], writes=[rx], key=("xt", id(rx)))
            loaded[ti] = (xt, rx)

        load(0)
        load(1)
        load(2)
        for ti in range(NT):
            if ti + 3 < NT:
                load(ti + 3)
            xt, rx = loaded.pop(ti)
            g = 0 if ti < NTS else 1

            def dst(pv, rp, ti=ti):
                self.cp(hT3[:, :, ti * 128:(ti + 1) * 128], _h(pv, "p (k t) -> p k t", k=8), reads=[rp],
                        writes=[self.r_hT[ti]], eng="act")
            pipe.push(self.norm_mod_T(xt, rx, g, 1, 0, dst, bufs))
        pipe.flush()
        if "HT" in self.debug and l == 0:
            self.dma(_h(self.scr["HT"], "(k p) t -> p k t", p=128), hT3, reads=self.r_hT, key="dbg")
        self.P.barrier()
        self.A.pop()

    def phase_C(self, l):
        d = self.din
        S = self.scr
        rc = self.r_const
        self.A.push()
        wsl = self.rot(2, 8 * 512, BF16, "wsl")
        brs = self.rot(2, 512, F32, "brep")
        t1 = self.rot(10, 512, F32, "t1")
        t2 = self.rot(3, 512, F32, "t2")
        t3 = self.rot(3, 512, F32, "t3")
        junkr = self.rot(3, 512, F32, "junkc")
        ssm = self.rot(5, 32, F32, "ssm")
        qb = self.rot(3, 512, BF16, "qb")
        stT = self.rot(3, 512, BF16, "stT")
        stv = self.rot(3, 516, BF16, "stv")
        stg = self.rot(3, 512, F32, "stg")
        stgb = self.rot(3, 512, BF16, "stgb")
        stf = self.rot(3, 512, BF16, "stf")
        cs_r = self.rot(6, 128, F32, "cossin")
        gq_rep, rgq = self.alloc(64, F32, "gq")
        gk_rep, rgk = self.alloc(64, F32, "gk")
        bcol, rbc = self.alloc(72, F32, "bcol")
        mmr = Rot([0, 1])
        mmf = Rot([2, 3])
        wslf = self.rot(2, 8 * 512, BF16, "wslf")
        ptr = Rot([4, 5, 6])
        hT3 = _h(self.hT, "p (k t) -> p k t", k=8)
        IF3 = _h(self.IF, "p (c j) -> p c j", j=16)

        self.dma(gq_rep, d["gqa_q_g"][l:l + 1, :].broadcast_to([128, 64]), writes=[rgq], key="gq")
        self.ts(gq_rep, gq_rep, 0.125, ALU.mult, reads=[rgq], writes=[rgq])
        self.dma(gk_rep, d["gqa_k_g"][l:l + 1, :].broadcast_to([128, 64]), writes=[rgk], key="gk")
        self.dma(bcol[:, 0:40], _h(d["b_in"][l, 0:5120], "(j p) -> p j", p=128), writes=[rbc], key="bcol", slow=True)
        self.dma(bcol[:, 40:72], _h(d["b_in"][l, O_GATES:O_GATES + 4096], "(j p) -> p j", p=128), writes=[rbc], key="bcol", slow=True)

        def load_slab(c0, width):
            w, rw = wsl.next()
            self.dma(_h(w, "p (k c) -> p k c", k=8)[:, :, 0:width], _h(d["w_in"][l][:, c0:c0 + width], "(k p) c -> p k c", p=128),
                     writes=[rw], key=("wsl", id(rw)), q="pool")
            return w, rw

        def load_bias(c0, width, scale=None):
            b, rb = brs.next()
            self.dma(b[:, 0:width], d["b_in"][l:l + 1, c0:c0 + width].broadcast_to([128, width]), writes=[rb], key=("brep", id(rb)))
            if scale is not None:
                self.ts(b[:, 0:width], b[:, 0:width], scale, ALU.mult, reads=[rb], writes=[rb])
            return b, rb

        def load_cs(ti):
            t, r = cs_r.next()
            self.dma(t[:, 0:64], d["rope_cos"][ti * 128:(ti + 1) * 128, :], writes=[r], key=("cs", id(r)))
            self.dma(t[:, 64:128], d["rope_sin"][ti * 128:(ti + 1) * 128, :], writes=[r], key=("cs", id(r)))
            return t, r

        def rope(out, x, rx, H, cs, rcs, ro):
            W = H * 64
            a, ra = t2.next()
            b, rb = t3.next()
            x3 = _h(x[:, 0:W], "p (h e) -> p h e", h=H)
            cosb = cs[:, 0:64].unsqueeze(1).broadcast_to([128, H, 64])
            self.tt(_h(a[:, 0:W], "p (h e) -> p h e", h=H), x3, cosb, ALU.mult, reads=[rx, rcs], writes=[ra])
            x5 = _h(x[:, 0:W], "p (h f x j) -> p h f x j", h=H, f=2, x=2)
            b5 = _h(b[:, 0:W], "p (h f x j) -> p h f x j", h=H, f=2, x=2)
            s4 = _h(cs[:, 64:128], "p (f x j) -> p f x j", f=2, x=2)
            for xi in range(2):
                sb_ = s4[:, :, xi, :].unsqueeze(1).broadcast_to([128, H, 2, 16])
                self.tt(b5[:, :, :, xi, :], x5[:, :, :, 1 - xi, :], sb_, ALU.mult, reads=[rx, rcs], writes=[rb], eng="pool")
            self.tt(out[:, 0:W], a[:, 0:W], b[:, 0:W], ALU.add, reads=[ra, rb], writes=[ro])

        def headnorm_g(out, ro, t, rt, H, grep, rg):
            W = H * 64
            ss, rss = ssm.next()
            jk, rjk = junkr.next()
            self.act(jk[:, 0:W], t[:, 0:W], AF.Square, reads=[rt], writes=[rjk])
            yield
            self.red(ss[:, 0:H], _h(jk[:, 0:W], "p (h e) -> p h e", h=H), ALU.add, reads=[rjk], writes=[rss])
            self.rstd(ss[:, 16:16 + H], ss[:, 0:H], 1.0 / 64, reads=[rss], writes=[rss], tmp=ss[:, 8:8 + H])
            yield
            o3 = _h(out[:, 0:W], "p (h e) -> p h e", h=H)
            self.tt(o3, _h(t[:, 0:W], "p (h e) -> p h e", h=H), ss[:, 16:16 + H].unsqueeze(2).broadcast_to([128, H, 64]), ALU.mult,
                    reads=[rt, rss], writes=[ro])
            self.tt(o3, o3, grep.unsqueeze(1).broadcast_to([128, H, 64]), ALU.mult, reads=[ro, rg], writes=[ro])

        def to_fm_g(src, rsrc, W, dst, ti):
            nj = W // 128
            pb = ptr.next()
            pv = self.psb_bf(pb)
            for j in range(nj):
                self.tr(pv[:, j * 128:(j + 1) * 128], src[:, j * 128:(j + 1) * 128], self.identb, reads=[rsrc], writes=[self.r_ps[pb]])
            yield
            sg, rsg = stT.next()
            self.cp(sg[:, 0:W], pv[:, 0:W], reads=[self.r_ps[pb]], writes=[rsg], eng="act")
            self.dma(_h(dst[0:W, ti * 128:(ti + 1) * 128], "(j p) t -> p j t", p=128), _h(sg[:, 0:W], "p (j t) -> p j t", j=nj),
                     reads=[rsg], key=("stT", id(rsg)), q="act")

        def prompt_out(name, kv, ti, c0, width, src, rsrc):
            sq = (ti - NTS) // 2
            p0 = ((ti - NTS) % 2) * 128
            self.dma(self.dout[name][sq, l, kv, p0:p0 + 128, c0:c0 + width], src, reads=[rsrc], key=("po", id(rsrc)))

        def h_na_v(ps, rp, ti, b, rb):
            sv, rsv = stv.next()
            self.tt(sv[:, 0:512], ps, b, ALU.add, reads=[rp, rb], writes=[rsv])
            if ti >= NTS:
                f, rf = t1.next()
                self.tt(f, ps, b, ALU.add, reads=[rp, rb], writes=[rf])
            yield
            self.dma(S["V_na"][ti * 128:(ti + 1) * 128, :], sv[:, 0:512], reads=[rsv], key=("stv", id(rsv)))
            if ti >= NTS:
                prompt_out("o_na", 1, ti, 0, 512, f, rf)

        def h_na_k_prompt(ps, rp, ti, b, rb):
            f, rf = t1.next()
            self.tt(f, ps, b, ALU.add, reads=[rp, rb], writes=[rf])
            yield
            prompt_out("o_na", 0, ti, 0, 512, f, rf)

        def h_g_q(ps, rp, ti, b, rb):
            t, rt = t1.next()
            self.tt(t, ps, b, ALU.add, reads=[rp, rb], writes=[rt])
            if ti < NTS:
                cs, rcs = load_cs(ti)
            n, rn = t1.next()
            yield from headnorm_g(n, rn, t, rt, 8, gq_rep, rgq)
            q, rq = qb.next()
            if ti < NTS:
                rope(q, n, rn, 8, cs, rcs, rq)
            else:
                self.cp(q, n, reads=[rn], writes=[rq])
            yield
            yield from to_fm_g(q, rq, 512, S["QT_g"], ti)

        def kv_common(ps, rp, ti, b, rb, normed, kt_name, v_name, oname):
            t, rt = t1.next()
            self.tt(t[:, 0:256], ps, b[:, 0:256], ALU.add, reads=[rp, rb], writes=[rt])
            if ti < NTS:
                cs, rcs = load_cs(ti)
            sv, rsv = stv.next()
            self.cp(sv[:, 0:128], t[:, 128:256], reads=[rt], writes=[rsv], eng="pool")
            if normed:
                n, rn = t1.next()
                yield from headnorm_g(n, rn, t, rt, 2, gk_rep, rgk)
            else:
                n, rn = t, rt
                yield
            self.dma(S[v_name][ti * 128:(ti + 1) * 128, :], sv[:, 0:128], reads=[rsv], key=("stv", id(rsv)))
            q, rq = qb.next()
            if ti < NTS:
                rope(q, n, rn, 2, cs, rcs, rq)
            else:
                prompt_out(oname, 0, ti, 0, 128, n[:, 0:128], rn)
                prompt_out(oname, 1, ti, 0, 128, t[:, 128:256], rt)
                self.cp(q[:, 0:128], n[:, 0:128], reads=[rn], writes=[rq])
            yield
            yield from to_fm_g(q, rq, 128, S[kt_name], ti)

        def h_g_kv(ps, rp, ti, b, rb):
            yield from kv_common(ps, rp, ti, b, rb, True, "KT_g", "V_g", "o_g")

        def h_s_kv(ps, rp, ti, b, rb):
            yield from kv_common(ps, rp, ti, b, rb, False, "KT_s", "V_s", "o_s")

        def h_s_q(ps, rp, ti, b, rb):
            t, rt = t1.next()
            self.stt(t, ps, 0.125, b, ALU.mult, ALU.add, reads=[rp, rb], writes=[rt])
            if ti < NTS:
                cs, rcs = load_cs(ti)
            yield
            q, rq = qb.next()
            if ti < NTS:
                rope(q, t, rt, 8, cs, rcs, rq)
            else:
                self.cp(q, t, reads=[rt], writes=[rq])
            yield
            yield from to_fm_g(q, rq, 512, S["QT_s"], ti)

        def h_m_k(ps, rp, ti, b, rb):
            f, rf = stf.next()
            self.tt(f, ps, b, ALU.add, reads=[rp, rb], writes=[rf])
            yield
            self.dma(S["K_m"][ti * 128:(ti + 1) * 128, :], f, reads=[rf], key=("stf", id(rf)))

        def h_m_v(ps, rp, ti, b, rb):
            sv, rsv = stv.next()
            s3 = _h(sv[:, 0:516], "p (h e) -> p h e", h=4)
            self.tt(s3[:, :, 0:128], _h(ps, "p (h e) -> p h e", h=4), _h(b, "p (h e) -> p h e", h=4), ALU.add,
                    reads=[rp, rb], writes=[rsv])
            self.memset(s3[:, :, 128:129], 1.0, writes=[rsv], eng="pool")
            yield
            self.dma(S["V_m"][ti * 128:(ti + 1) * 128, :], sv[:, 0:516], reads=[rsv], key=("stv", id(rsv)))

        def h_m_o(ps, rp, ti, b, rb):
            t, rt = t1.next()
            self.tt(t, ps, b, ALU.add, reads=[rp, rb], writes=[rt])
            yield
            g, rg = stg.next()
            self.act(g, t, AF.Sigmoid, reads=[rt], writes=[rg])
            yield
            self.dma(S["O_m"][ti * 128:(ti + 1) * 128, :], g, reads=[rg], key=("stg", id(rg)), q="act")

        def h_m_if(ps, rp, ti, b, rb):
            self.tt(IF3[:, ti, :], ps, b[:, 0:16], ALU.add, reads=[rp, rb], writes=[self.r_IF])
            yield

        fm_groups = [(O_NAQ, 4, "q8", S["QT_na"], 0), (O_NAK, 4, "k", S["KT_na"], 4),
                     (O_MQ, 4, "k", S["QT_m"], 24), (O_MK, 4, "k", S["KT_m"], 28)]
        for i in range(8):
            fm_groups.append((O_GATES + i * 512, 4, "gate", S["GT"][i * 512:(i + 1) * 512, :], 40 + i * 4))
        onlyf = getattr(self, "only_fm", None)

        def fm_stream():
            for gi, (c0, nb, kind, dst, bc0) in enumerate(fm_groups):
                if onlyf is not None and gi not in onlyf:
                    continue
                w, rw = wslf.next()
                self.dma(_h(w, "p (k c) -> p k c", k=8), _h(d["w_in"][l][:, c0:c0 + 512], "(k p) c -> p k c", p=128),
                         writes=[rw], key=("wslf", id(rw)), q="pool")
                for cbl in range(nb):
                    bc = bcol[:, bc0 + cbl:bc0 + cbl + 1]
                    for tc in range(TOK // 512):
                        pb = mmf.next()
                        ps = self.psb(pb)
                        for k in range(8):
                            self.mm(ps, w[:, k * 512 + cbl * 128:k * 512 + (cbl + 1) * 128], hT3[:, k, tc * 512:(tc + 1) * 512],
                                    k == 0, k == 7, reads=[rw] + self.r_hT[tc * 4:(tc + 1) * 4], writes=[self.r_ps[pb]])
                        dsl = dst[cbl * 128:(cbl + 1) * 128, tc * 512:(tc + 1) * 512]
                        if kind == "gate":
                            g, rg = stgb.next()
                            self.act(g, ps, AF.Sigmoid, reads=[self.r_ps[pb], rbc], writes=[rg], bias=bc)
                            self.dma(dsl, g, reads=[rg], key=("stgb", id(rg)), q="act")
                        else:
                            f, rf = stf.next()
                            if kind == "q8":
                                self.ts(f, ps, bc, ALU.add, 0.125, ALU.mult, reads=[self.r_ps[pb], rbc], writes=[rf])
                            else:
                                self.ts(f, ps, bc, ALU.add, reads=[self.r_ps[pb], rbc], writes=[rf])
                            self.dma(dsl, f, reads=[rf], key=("stf", id(rf)))
                        yield

        all_t = list(range(NT))
        tm_groups = [
            (O_NAV, 512, h_na_v, all_t, None),
            (O_GQ, 512, h_g_q, all_t, None),
            (O_GK, 256, h_g_kv, all_t, None),
            (O_SQ, 512, h_s_q, all_t, 0.125),
            (O_SK, 256, h_s_kv, all_t, None),
            (O_MK, 512, h_m_k, all_t, None),
            (O_MV, 512, h_m_v, all_t, None),
            (O_MO, 512, h_m_o, all_t, None),
            (O_MI, 16, h_m_if, all_t, None),
            (O_NAK, 512, h_na_k_prompt, list(range(NTS, NT)), None),
        ]
        only = getattr(self, "only_groups", None)
        fm = fm_stream()
        n_tm = 0
        for gi, (c0, width, handler, tiles, bscale) in enumerate(tm_groups):
            if only is not None and gi not in only:
                continue
            w, rw = load_slab(c0, width)
            b, rb = load_bias(c0, width, bscale)
            if gi == 0:
                self.warm(reads=[rw, rb])
            pipe = Pipe()
            for ti in tiles:
                pb = mmr.next()
                ps = self.psb(pb)[:, 0:width]
                for k in range(8):
                    self.mm(ps, hT3[:, k, ti * 128:(ti + 1) * 128], w[:, k * 512:k * 512 + width], k == 0, k == 7,
                            reads=[rw, self.r_hT[ti]], writes=[self.r_ps[pb]])
                pipe.push(handler(ps, self.r_ps[pb], ti, b, rb))
                n_tm += 1
                for _ in range(2 if n_tm % 3 == 0 else 1):
                    next(fm, None)
            pipe.flush()

        if "IFD" in self.debug and l == 0:
            self.dma(S["IFD"], self.IF, reads=[self.r_IF], key="dbg")

        for _ in fm:
            pass
        self.A.pop()


def _core_inputs(inputs, core, consts, nab):
    f = np.ascontiguousarray
    m = {
        "xs": f(inputs["x_sample"][core]),
        "xp": f(inputs["x_prompt"][2 * core:2 * core + 2].reshape(512, D)),
        "cna": f(inputs["cache_na_kv"][core].reshape(2, 2, 256, 512)),
        "cg": f(inputs["cache_gqa_kv"][core].reshape(2, 2, 256, 128)),
        "cs": f(inputs["cache_swa_kv"][core].reshape(2, 2, 256, 128)),
        "stC": f(inputs["state_mlstm_C"][core]),
        "stn": f(inputs["state_mlstm_n"][core]),
        "stm": f(inputs["state_mlstm_m"][core].reshape(2, 8)),
        "cvec": f(np.stack([inputs["c"][core], inputs["c_ctx"]], 0)),
        "final_norm_g": f(inputs["final_norm_g"].reshape(1, D)),
        "nabias": nab,
    }
    for k in ("w_mod", "b_mod", "norm1_g", "norm2_g", "w_in", "b_in", "gqa_q_g", "gqa_k_g", "swa_sink",
              "mlstm_norm_g", "w_branch", "w_out", "w_ffn_in", "w_ffn_out"):
        m[k] = f(inputs[k])
    m.update(consts)
    return m


def _na_bias_tiles(na_rpb):
    gi = _na_gather_indices()
    out = np.empty((2, 8, NA_NB, 128, 128), np.float32)
    for tid, (dr, dc, valid) in gi.items():
        g = na_rpb[:, :, dr, dc]
        out[:, :, tid] = np.where(valid[None, None], g, np.float32(NEG))
    return out


def _attn_setup(self):
    self.at_srot = Rot([(0, 1), (2, 3)])
    self.at_obanks = (4, 5)
    self.at_pt = self.rot(3, 1024, BF16, "pt")
    self.at_osb = self.rot(4, 1024, F32, "osb")
    self.at_rec = self.rot(6, 512, F32, "rec")
    self.at_tmp = self.rot(6, 512, F32, "dtmp")
    self.at_pending = []


def _attn_job(self, *a, **kw):
    for _ in _attn_job_gen(self, *a, **kw):
        pass


def _attn_job_gen(self, kt, rkt, qt, rqt, nq, kbl, stage, rstage, sinks=None, rsink=None, filler=0, act_recip=True, after=None,
                  obanks=None, ptrot=None):
    oA, oB = obanks if obanks is not None else self.at_obanks
    psoA, psoB = self.psb(oA), self.psb(oB)
    nk = len(kbl)
    pend = None

    def pv(item):
        i, ptile, rpt = item
        kb = kbl[i]
        qlo, qhi = kb["qlo"], kb["qhi"]
        self.mm(psoA[:, qlo:qhi], kb["vx"][0], ptile[:, qlo:qhi], i == 0, i == nk - 1, reads=[rpt, kb["rvx"]], writes=[self.r_ps[oA]])
        self.mm(psoB[:, qlo:qhi], kb["vx"][1], ptile[:, 512 + qlo:512 + qhi], i == 0, i == nk - 1, reads=[rpt, kb["rvx"]],
                writes=[self.r_ps[oB]])
        if filler:
            self.warm(n=filler, bank=6)

    for i, kb in enumerate(kbl):
        banks = self.at_srot.next()
        ptile, rpt = (ptrot or self.at_pt).next()
        qlo, qhi = kb["qlo"], kb["qhi"]
        exs = kb.get("extras", ((), ()))
        for hh in range(2):
            pss = self.psb(banks[hh])
            self.mm(pss[:, qlo:qhi], kt[hh * 64:hh * 64 + 64, kb["kcol"]:kb["kcol"] + 128], qt[hh * 64:hh * 64 + 64, qlo:qhi],
                    True, len(exs[hh]) == 0, reads=[rkt, rqt], writes=[self.r_ps[banks[hh]]])
        for hh in range(2):
            pss = self.psb(banks[hh])
            ex = exs[hh]
            for ei, (col, tile, rtile) in enumerate(ex):
                self.mm(pss[:, col:col + tile.shape[1]], self.identb, tile, False, ei == len(ex) - 1,
                        reads=[rtile], writes=[self.r_ps[banks[hh]]], skip_group_check=True)
        if qlo == 0 and qhi == 512:
            self.act(ptile[:, 0:1024], self.ps_t[:, banks[0]:banks[0] + 2, :], AF.Exp,
                     reads=[self.r_ps[banks[0]], self.r_ps[banks[1]]], writes=[rpt])
        else:
            self.act(_h(ptile, "p (b q) -> p b q", b=2)[:, :, qlo:qhi], self.ps_t[:, banks[0]:banks[0] + 2, qlo:qhi], AF.Exp,
                     reads=[self.r_ps[banks[0]], self.r_ps[banks[1]]], writes=[rpt])
        if pend is not None:
            pv(pend)
        pend = (i, ptile, rpt)
        if i == min(1, nk - 2):
            while self.at_pending:
                self.at_pending.pop(0)()
        yield
    pv(pend)
    osb, rosb = self.at_osb.next()
    self.cp(osb[:, 0:nq], psoA[:, 0:nq], reads=[self.r_ps[oA]], writes=[rosb])
    self.cp(osb[:, 512:512 + nq], psoB[:, 0:nq], reads=[self.r_ps[oB]], writes=[rosb])

    def fin():
      for hh in range(2):
          o_ = osb[:, hh * 512:hh * 512 + nq]
          rec, rrec = self.at_rec.next()
          if sinks is not None:
              tmp, rtmp = self.at_tmp.next()
              self.ts(tmp[64:128, 0:nq], o_[64:128, :], sinks[hh][64:128, :], ALU.add, reads=[rosb, rsink], writes=[rtmp])
              den = tmp[64:128, 0:nq]
              rden = rtmp
          else:
              den = o_[64:128, :]
              rden = rosb
          if act_recip:
              self.act(rec[0:64, 0:nq], den, AF.Ln, reads=[rden], writes=[rrec])
              self.act(rec[0:64, 0:nq], rec[0:64, 0:nq], AF.Exp, reads=[rrec], writes=[rrec], scale=-1.0)
          else:
              self.recip(rec[0:64, 0:nq], den, reads=[rden], writes=[rrec])
          self.tt(stage[hh * 64:hh * 64 + 64, 0:nq], o_[0:64, :], rec[0:64, 0:nq], ALU.mult, reads=[rosb, rrec], writes=[rstage])
      if after is not None:
          after()
    self.at_pending.append(fin)


def _load_ctx_kT(self, src, ncols, dsts, l):
    tmpb, rtb = self.at_ctxb.next()
    self.dma(_h(tmpb[:, 0:256], "p (b e) -> p b e", b=2), _h(src, "(b p) e -> p b e", p=128), writes=[rtb],
             key=("ctxb", id(rtb)), q="pool")
    pb_ = self.at_ptr.next()
    pv = self.psb_bf(pb_)
    for b in range(2):
        self.tr(pv[:, b * 128:(b + 1) * 128], tmpb[:, b * 128:(b + 1) * 128], self.identb, reads=[rtb], writes=[self.r_ps[pb_]])
    for (dst, rdst, moves) in dsts:
        for (dlo, slo) in moves:
            self.cp(_h(dst[dlo:dlo + 64, 0:256], "p (b t) -> p b t", b=2), _h(pv[slo:slo + 64, 0:256], "p (b t) -> p b t", b=2),
                    reads=[self.r_ps[pb_]], writes=[rdst], eng="act")


def _phase_D(self, l):
    d = self.din
    S = self.scr
    self.A.push()
    _attn_setup(self)
    self.at_ctxb = self.rot(2, 256, BF16, "ctxb")
    self.at_ptr = Rot([6, 7])
    sinkexp, rsk = self.alloc(8, F32, "sinkexp")
    self.dma(sinkexp, d["swa_sink"][l:l + 1, :].broadcast_to([128, 8]), writes=[rsk], key="sink")
    self.act(sinkexp, sinkexp, AF.Exp, reads=[rsk], writes=[rsk])
    which = getattr(self, "only_attn", ("prompt", "gqa", "swa", "na"))

    if "prompt" in which:
        self.A.push()
        ktp = self.rot(4, 256, BF16, "ktp")
        qtp = self.rot(4, 256, BF16, "qtp")
        vxp = self.rot(4, 2 * 2 * 128, BF16, "vxp")
        for (vx, rvx) in vxp.items:
            self.memset(vx, 1.0, writes=[rvx])
        stg = self.rot(4, 256, BF16, "stgp")
        for sq in range(2):
            c0 = 4096 + sq * 256
            for (bi, qn, kn, vn, nkv) in ((0, "QT_na", "KT_na", "V_na", 8), (1, "QT_g", "KT_g", "V_g", 2), (2, "QT_s", "KT_s", "V_s", 2)):
                for hp in range(4):
                    qt, rqt = qtp.next()
                    self.dma(qt, S[qn][hp * 128:(hp + 1) * 128, c0:c0 + 256], writes=[rqt], key=("qtp", id(rqt)))
                    kt, rkt = ktp.next()
                    vx, rvx = vxp.next()
                    vx4 = _h(vx, "p (b h e) -> p b h e", b=2, h=2)
                    if nkv == 8:
                        self.dma(kt, S[kn][hp * 128:(hp + 1) * 128, c0:c0 + 256], writes=[rkt], key=("ktp", id(rkt)))
                        for hh in range(2):
                            self.dma(vx4[:, :, hh, 0:64], _h(S[vn][c0:c0 + 256, hp * 128 + hh * 64:hp * 128 + (hh + 1) * 64], "(b p) e -> p b e", p=128),
                                     writes=[rvx], key=("vxp", id(rvx)))
                    else:
                        j = hp // 2
                        for half in range(2):
                            self.dma(kt[half * 64:(half + 1) * 64, :], S[kn][j * 64:(j + 1) * 64, c0:c0 + 256], writes=[rkt], key=("ktp", id(rkt)))
                            self.dma(vx4[:, :, half, 0:64], _h(S[vn][c0:c0 + 256, j * 64:(j + 1) * 64], "(b p) e -> p b e", p=128),
                                     writes=[rvx], key=("vxp", id(rvx)))
                    st_, rst = stg.next()
                    if hp == 0:
                        self.warm(reads=[rkt, rqt, rvx], bank=6)
                    kbl = [dict(kcol=kb * 128, vx=(vx4[:, kb, 0, :], vx4[:, kb, 1, :]), rvx=rvx, qlo=0, qhi=256) for kb in range(2)]
                    def after(st_=st_, rst=rst, bi=bi, hp=hp, c0=c0):
                        self.dma(S["OT"][bi, hp * 128:(hp + 1) * 128, c0:c0 + 256], st_, reads=[rst], key=("stgp", id(rst)))
                    if bi == 2:
                        _attn_job(self, kt, rkt, qt, rqt, 256, kbl, st_, rst,
                                  sinks=(sinkexp[:, 2 * hp:2 * hp + 1], sinkexp[:, 2 * hp + 1:2 * hp + 2]), rsink=rsk, after=after)
                    else:
                        _attn_job(self, kt, rkt, qt, rqt, 256, kbl, st_, rst, after=after)
        while self.at_pending:
            self.at_pending.pop(0)()
        self.P.barrier()
        self.A.pop()

    if "gqa" in which or "swa" in which:
        self.A.push()
        mix = {}
        for (name, bi, qn, kn, vn, cn) in (("gqa", 1, "QT_g", "KT_g", "V_g", "cg"), ("swa", 2, "QT_s", "KT_s", "V_s", "cs")):
            if name not in which:
                continue
            ktd = [self.alloc(4352, BF16, "ktd%d" % j) for j in range(2)]
            vxs = [self.alloc(34 * 128, BF16, "vx%d" % j) for j in range(2)]
            qts = self.rot(2, 4096, BF16, "qts")
            stg = self.rot(2, 4096, BF16, "stgs")
            _load_ctx_kT(self, d[cn][l, 0], 128, [(ktd[0][0], ktd[0][1], [(0, 0), (64, 0)]), (ktd[1][0], ktd[1][1], [(0, 64), (64, 64)])], l)
            for j in range(2):
                kt, rkt = ktd[j]
                vx, rvx = vxs[j]
                for half in range(2):
                    self.dma(kt[half * 64:(half + 1) * 64, 256:4352], S[kn][j * 64:(j + 1) * 64, 0:4096], writes=[rkt], key=("ktd", id(rkt)))
                self.memset(vx, 1.0, writes=[rvx])
                vx3 = _h(vx, "p (b e) -> p b e", b=34)
                self.dma(vx3[:, 0:2, 0:64], _h(d[cn][l, 1][:, j * 64:(j + 1) * 64], "(b p) e -> p b e", p=128), writes=[rvx],
                         key=("vxs", id(rvx)), q="pool")
                for q4 in range(4):
                    self.dma(vx3[:, 2 + q4 * 8:2 + (q4 + 1) * 8, 0:64],
                             _h(S[vn][q4 * 1024:(q4 + 1) * 1024, j * 64:(j + 1) * 64], "(b p) e -> p b e", p=128), writes=[rvx], key=("vxs", id(rvx)))
            mix[name] = (bi, qn, ktd, vxs, qts, stg)

        ptrots = {"gqa": self.at_pt, "swa": self.rot(3, 1024, BF16, "pt_swa")}

        def stream(name, obanks, filler):
            bi, qn, ktd, vxs, qts, stg = mix[name]
            for hp in range(4):
                j = hp // 2
                kt, rkt = ktd[j]
                vx, rvx = vxs[j]
                vx3 = _h(vx, "p (b e) -> p b e", b=34)
                qt, rqt = qts.next()
                self.dma(qt, S[qn][hp * 128:(hp + 1) * 128, 0:4096], writes=[rqt], key=("qts", id(rqt)))
                st_, rst = stg.next()
                if name == "gqa":
                    self.warm(reads=[rkt, rqt, rvx], bank=6 if "swa" not in mix else obanks[0])
                for qc in range(8):
                    if name == "gqa":
                        kbl = [dict(kcol=kb * 128, vx=(vx3[:, kb, :], vx3[:, kb, :]), rvx=rvx, qlo=0, qhi=512) for kb in range(34)]
                        sk = None
                    else:
                        kbl = [dict(kcol=0, vx=(vx3[:, 0, :], vx3[:, 0, :]), rvx=rvx, qlo=0, qhi=512)]
                        for jb in range(4 * qc - 1, 4 * qc + 5):
                            if jb < 0 or jb > 31:
                                continue
                            qbs = [qb for qb in (jb - 1, jb, jb + 1) if 4 * qc <= qb <= 4 * qc + 3]
                            ex = []
                            if jb + 1 in qbs:
                                ex.append(((jb + 1 - 4 * qc) * 128, self.swam[0], self.r_const))
                            if jb - 1 in qbs:
                                ex.append(((jb - 1 - 4 * qc) * 128, self.swam[1], self.r_const))
                            kbl.append(dict(kcol=256 + jb * 128, vx=(vx3[:, 2 + jb, :], vx3[:, 2 + jb, :]), rvx=rvx,
                                            qlo=(min(qbs) - 4 * qc) * 128, qhi=(max(qbs) + 1 - 4 * qc) * 128, extras=(ex, ex)))
                        kbl.append(dict(kcol=128, vx=(vx3[:, 1, :], vx3[:, 1, :]), rvx=rvx, qlo=0, qhi=512))
                        sk = (sinkexp[:, 2 * hp:2 * hp + 1], sinkexp[:, 2 * hp + 1:2 * hp + 2])
                    aft = None
                    if qc == 7:
                        def aft(st_=st_, rst=rst, hp=hp, bi=bi):
                            self.dma(S["OT"][bi, hp * 128:(hp + 1) * 128, 0:4096], st_, reads=[rst], key=("stgs", id(rst)))
                    yield from _attn_job_gen(self, kt, rkt, qt[:, qc * 512:(qc + 1) * 512], rqt, 512, kbl,
                                             st_[:, qc * 512:(qc + 1) * 512], rst, sinks=sk, rsink=rsk, filler=filler,
                                             act_recip=(name != "gqa"), after=aft, obanks=obanks, ptrot=ptrots[name])

        if "gqa" in mix and "swa" in mix:
            g_ = stream("gqa", (4, 5), getattr(self, "mix_filler", 0))
            w_ = stream("swa", (6, 7), 0)
            n_g = 0
            ratio = getattr(self, "mix_ratio", 4)
            w_alive = True
            for _ in g_:
                n_g += 1
                if w_alive and n_g % ratio == 0:
                    if next(w_, "done") == "done":
                        w_alive = False
            if w_alive:
                for _ in w_:
                    pass
        else:
            nm = "gqa" if "gqa" in mix else "swa"
            for _ in stream(nm, (4, 5), 1 if nm == "gqa" else 0):
                pass
        while self.at_pending:
            self.at_pending.pop(0)()
        self.P.barrier()
        self.A.pop()

    if "na" in which:
        self.A.push()
        ktn = self.rot(2, 4352, BF16, "ktn")
        qtn = self.rot(2, 4096, BF16, "qtn")
        vxn = self.rot(2, 34 * 2 * 128, BF16, "vxn")
        for (vx, rvx) in vxn.items:
            self.memset(vx, 1.0, writes=[rvx])
        bia = self.rot(2, 2 * NA_NB * 128, BF16, "nabias")
        stg = self.rot(2, 4096, BF16, "stgn")
        for hp in range(4):
            kt, rkt = ktn.next()
            qt, rqt = qtn.next()
            vx, rvx = vxn.next()
            bt, rbt = bia.next()
            vx4 = _h(vx, "p (b h e) -> p b h e", b=34, h=2)
            bt4 = _h(bt, "p (h n q) -> p h n q", h=2, n=NA_NB)
            _load_ctx_kT(self, d["cna"][l, 0][:, hp * 128:(hp + 1) * 128], 128, [(kt, rkt, [(0, 0), (64, 64)])], l)
            self.dma(kt[:, 256:4352], S["KT_na"][hp * 128:(hp + 1) * 128, 0:4096], writes=[rkt], key=("ktn", id(rkt)))
            self.dma(qt, S["QT_na"][hp * 128:(hp + 1) * 128, 0:4096], writes=[rqt], key=("qtn", id(rqt)))
            for hh in range(2):
                c_lo = hp * 128 + hh * 64
                self.dma(vx4[:, 0:2, hh, 0:64], _h(d["cna"][l, 1][:, c_lo:c_lo + 64], "(b p) e -> p b e", p=128),
                         writes=[rvx], key=("vxn", id(rvx)), q="pool")
                for q4 in range(4):
                    self.dma(vx4[:, 2 + q4 * 8:2 + (q4 + 1) * 8, hh, 0:64],
                             _h(S["V_na"][q4 * 1024:(q4 + 1) * 1024, c_lo:c_lo + 64], "(b p) e -> p b e", p=128),
                             writes=[rvx], key=("vxn", id(rvx)))
            for hh in range(2):
                self.dma(bt4[:, hh, :, :], _h(d["nabias"][l, hp * 2 + hh], "n k q -> k n q"), writes=[rbt], key=("nab", id(rbt)), q="pool")
            st_, rst = stg.next()
            self.warm(reads=[rkt, rqt, rvx, rbt], bank=6)
            fill = getattr(self, "filler", {"gqa": 1, "swa": 0, "na": 0})
            for qc in range(8):
                tiles = [4 * qc + i for i in range(4)]
                kbl = [dict(kcol=0, vx=(vx4[:, 0, 0, :], vx4[:, 0, 1, :]), rvx=rvx, qlo=0, qhi=512)]
                jbs = sorted(set(jb for a in tiles for jb in _na_kbs(a)))
                for jb in jbs:
                    as_ = [a for a in tiles if jb in _na_kbs(a)]
                    ids = [NA_TILE_IDX[(_na_class(a), jb - a)] for a in as_]
                    runs = []
                    for a, tid in zip(as_, ids):
                        if runs and runs[-1][1] + runs[-1][2] == tid:
                            runs[-1][2] += 1
                        else:
                            runs.append([a, tid, 1])
                    exs = tuple([((a0 - 4 * qc) * 128, _h(bt4[:, hh, t0:t0 + n, :], "p n q -> p (n q)"), rbt) for (a0, t0, n) in runs]
                                for hh in range(2))
                    kbl.append(dict(kcol=256 + jb * 128, vx=(vx4[:, 2 + jb, 0, :], vx4[:, 2 + jb, 1, :]), rvx=rvx,
                                    qlo=(min(as_) - 4 * qc) * 128, qhi=(max(as_) + 1 - 4 * qc) * 128, extras=exs))
                kbl.append(dict(kcol=128, vx=(vx4[:, 1, 0, :], vx4[:, 1, 1, :]), rvx=rvx, qlo=0, qhi=512))
                aft = None
                if qc == 7:
                    def aft(st_=st_, rst=rst, hp=hp):
                        self.dma(S["OT"][0, hp * 128:(hp + 1) * 128, 0:4096], st_, reads=[rst], key=("stgn", id(rst)))
                _attn_job(self, kt, rkt, qt[:, qc * 512:(qc + 1) * 512], rqt, 512, kbl,
                          st_[:, qc * 512:(qc + 1) * 512], rst, filler=fill["na"], after=aft)
        while self.at_pending:
            self.at_pending.pop(0)()
        self.P.barrier()
        self.A.pop()
    self.A.pop()


KB.phase_D = _phase_D


def _phase_E(self, l):
    d = self.din
    S = self.scr
    rc = self.r_const
    self.A.push()
    IF3 = _h(self.IF, "p (c j) -> p c j", j=16)
    NC4 = NT * 4
    lnk, rlnk = self.alloc(1, F32, "lnk")
    self.memset(lnk, -0.5 * math.log(128.0), writes=[rlnk])
    e1, re1 = self.alloc(NT * 8, F32, "e1")
    e13 = _h(e1, "p (c j) -> p c j", j=8)
    self.act(e13, IF3[:, :, 8:16], AF.Exp, reads=[self.r_IF], writes=[re1], scale=-1.0)
    self.act(e1, e1, AF.Ln, reads=[re1], writes=[re1], bias=self.one_col)
    nb, ntot, uu, eb, EG, ww, ug = [], [], [], [], [], [], []
    for dr in range(2):
        lfd, rl = self.alloc(NC4, F32, "lfd")
        self.cp(_h(lfd, "p (c j) -> p c j", j=4), e13[:, :, dr * 4:(dr + 1) * 4], reads=[re1], writes=[rl])
        self.mm(self.psb(dr)[:, 0:NC4], self.tri[dr], lfd, True, True, reads=[rl], writes=[self.r_ps[dr]])
        self.mm(self.psb(2 + dr)[:, 0:NC4], self.tri[2], lfd, True, True, reads=[rl], writes=[self.r_ps[2 + dr]])
        t_nb, r_nb = self.alloc(NC4, F32, "nb")
        t_nt, r_nt = self.alloc(NC4, F32, "ntot")
        self.cp(t_nb, self.psb(dr)[:, 0:NC4], reads=[self.r_ps[dr]], writes=[r_nb])
        self.cp(t_nt, self.psb(2 + dr)[:, 0:NC4], reads=[self.r_ps[2 + dr]], writes=[r_nt])
        dd, rdd = self.alloc(NC4, F32, "dd")
        self.tt(_h(dd, "p (c j) -> p c j", j=4), IF3[:, :, dr * 4:(dr + 1) * 4], _h(t_nb, "p (c j) -> p c j", j=4), ALU.add,
                reads=[self.r_IF, r_nb], writes=[rdd])
        t_u, r_u = self.alloc(NC4, F32, "u")
        self.act(t_u, dd, AF.Exp, reads=[rdd, rlnk], writes=[r_u], bias=lnk)
        t_eb, r_eb = self.alloc(NC4, F32, "eb")
        self.act(t_eb, t_nb, AF.Exp, reads=[r_nb], writes=[r_eb], scale=-1.0)
        t_eg, r_eg = self.alloc(NC4, F32, "EG")
        self.act(t_eg, t_nt, AF.Exp, reads=[r_nt], writes=[r_eg], scale=-1.0)
        t_w, r_w = self.alloc(NC4, F32, "w")
        self.tt(t_w, dd, t_nt, ALU.subtract, reads=[rdd, r_nt], writes=[r_w])
        t_ug, r_ug = self.alloc(NC4, F32, "ug")
        self.tt(t_ug, t_u, t_eg, ALU.mult, reads=[r_u, r_eg], writes=[r_ug])
        ug.append((t_ug, r_ug))
        nb.append((t_nb, r_nb)); ntot.append((t_nt, r_nt)); uu.append((t_u, r_u)); eb.append((t_eb, r_eb))
        EG.append((t_eg, r_eg)); ww.append((t_w, r_w))
    em0, rem0 = self.alloc(8, F32, "em0")
    self.dma(em0, d["stm"][l:l + 1, :].broadcast_to([128, 8]), writes=[rem0], key="em0")
    self.act(em0, em0, AF.Exp, reads=[rem0], writes=[rem0])
    gml, rgml = self.alloc(512, F32, "gml")
    self.dma(gml, d["mlstm_norm_g"][l:l + 1, :].broadcast_to([128, 512]), writes=[rgml], key="gml")
    self.P.barrier()

    Cst, rC = self.alloc(4 * 129, F32, "Cst")
    Cn, rCn = self.alloc(4 * 129, F32, "Cn")
    Cbf, rCb = self.alloc(4 * 129, BF16, "Cbf")
    qTr = self.rot(4, 512, BF16, "qTc")
    kTr = self.rot(4, 512, BF16, "kTc")
    kr = self.rot(4, 512, BF16, "kc")
    vr = self.rot(4, 516, BF16, "vc")
    hfr = self.rot(4, 512, F32, "hfc")
    ocr = self.rot(5, 512, F32, "oc")
    sTr = self.rot(3, 512, BF16, "sT")
    ktr = self.rot(3, 512, BF16, "kt")
    hst = self.rot(3, 512, F32, "hst")
    yr = self.rot(3, 512, F32, "y")
    ybr = self.rot(2, 512, BF16, "yb")
    stT = self.rot(2, 512, BF16, "stTm")
    sm = self.rot(6, 32, F32, "smm")
    junkr = self.rot(2, 128, F32, "junkm")
    psS = Rot([0, 1])
    ptr = Rot([6, 7])
    fin, rfin = self.alloc(64, F32, "fin")
    fco, rfco = self.alloc(4 * 129, F32, "fco")

    def Nv(h):
        return self.psb(2 + h // 2)[:, (h % 2) * 129:(h % 2) * 129 + 129], self.r_ps[2 + h // 2]

    def Uv(h):
        return self.psb(4 + h // 2)[:, (h % 2) * 129:(h % 2) * 129 + 129], self.r_ps[4 + h // 2]

    def run_pass(dr, tiles, sample, sq):
        t_u, r_u = uu[dr]
        t_eb, r_eb = eb[dr]
        t_eg, r_eg = EG[dr]
        if sample:
            C3 = _h(Cst, "p (h e) -> p h e", h=4)
            for h in range(4):
                self.dma(C3[:, h, 0:128], d["stC"][l, dr, h], writes=[rC], key="cinit")
                self.dma(C3[:, h, 128:129], _h(d["stn"][l, dr, h], "(p o) -> p o", o=1), writes=[rC], key="cinit", slow=True)
            for h in range(4):
                self.ts(C3[:, h, :], C3[:, h, :], em0[:, dr * 4 + h:dr * 4 + h + 1], ALU.mult, reads=[rC, rem0], writes=[rC])
        else:
            self.memset(Cst, 0.0, writes=[rC])
        self.cp(Cbf, Cst, reads=[rC], writes=[rCb], eng="act")
        loaded = {}

        def load(ti):
            qT, rq = qTr.next(); kT, rk = kTr.next(); kc, rkc = kr.next(); vc, rv = vr.next()
            cs = slice(ti * 128, (ti + 1) * 128)
            self.dma(_h(qT, "p (h t) -> p h t", h=4), _h(S["QT_m"][:, cs], "(h p) t -> p h t", p=128), writes=[rq], key=("qTc", id(rq)))
            self.dma(_h(kT, "p (h t) -> p h t", h=4), _h(S["KT_m"][:, cs], "(h p) t -> p h t", p=128), writes=[rk], key=("kTc", id(rk)))
            self.dma(kc, S["K_m"][cs, :], writes=[rkc], key=("kc", id(rkc)))
            self.dma(vc, S["V_m"][cs, :], writes=[rv], key=("vc", id(rv)))
            ex = None
            if dr == 1:
                hf, rhf = hfr.next(); oc, roc = ocr.next()
                self.dma(hf, S["HF"][cs, :], writes=[rhf], key=("hfc", id(rhf)))
                self.dma(oc, S["O_m"][cs, :], writes=[roc], key=("oc", id(roc)))
                ex = (hf, rhf, oc, roc)
            loaded[ti] = (qT, rq, kT, rk, kc, rkc, vc, rv, ex)

        t_ug, r_ug = ug[dr]

        def chunk(ti):
            qT, rq, kT, rk, kc, rkc, vc, rv, ex = loaded.pop(ti)
            sb = psS.next()
            pS = self.psb(sb)
            for h in range(4):
                self.mm(pS[:, h * 128:(h + 1) * 128], kT[:, h * 128:(h + 1) * 128], qT[:, h * 128:(h + 1) * 128], True, True,
                        reads=[rk, rq], writes=[self.r_ps[sb]])
            sT, rsT = sTr.next()
            kt, rkt = ktr.next()
            for h in range(4):
                ucol = t_u[:, ti * 4 + h:ti * 4 + h + 1]
                self.stt(sT[:, h * 128:(h + 1) * 128], pS[:, h * 128:(h + 1) * 128], ucol, self.trib[dr], ALU.mult, ALU.mult,
                         reads=[self.r_ps[sb], r_u], writes=[rsT])
                self.act(kt[:, h * 128:(h + 1) * 128], kc[:, h * 128:(h + 1) * 128], AF.Copy, reads=[rkc, r_ug], writes=[rkt],
                         scale=t_ug[:, ti * 4 + h:ti * 4 + h + 1])
            yield
            for h in range(4):
                nv, rn = Nv(h)
                self.mm(nv, sT[:, h * 128:(h + 1) * 128], vc[:, h * 129:(h + 1) * 129], True, False, reads=[rsT, rv], writes=[rn])
                self.mm(nv, qT[:, h * 128:(h + 1) * 128], Cbf[:, h * 129:(h + 1) * 129], False, True, reads=[rq, rCb], writes=[rn])
            for h in range(4):
                uv, ru = Uv(h)
                self.mm(uv, kt[:, h * 128:(h + 1) * 128], vc[:, h * 129:(h + 1) * 129], True, True, reads=[rkt, rv], writes=[ru])
            for h in range(4):
                uv, ru = Uv(h)
                gcol = t_eg[:, ti * 4 + h:ti * 4 + h + 1]
                cs_ = slice(h * 129, (h + 1) * 129)
                self.stt(Cst[:, cs_], Cst[:, cs_], gcol, uv, ALU.mult, ALU.add, reads=[ru, r_eg, rC], writes=[rC])
            self.cp(Cbf, Cst, reads=[rC], writes=[rCb], eng="act")
            s_, rs_ = sm.next()
            for hp in range(2):
                dv = _h(self.psb(2 + hp)[:, 0:258], "p (h e) -> p h e", e=129)[:, :, 128]
                self.tt(s_[:, 2 * hp:2 * hp + 2], dv, t_eb[:, ti * 4 + 2 * hp:ti * 4 + 2 * hp + 2], ALU.mult,
                        reads=[self.r_ps[2 + hp], r_eb], writes=[rs_])
            self.ts(s_[:, 4:8], s_[:, 0:4], -1.0, ALU.mult, reads=[rs_], writes=[rs_])
            self.stt(s_[:, 8:12], s_[:, 0:4], 1.0, s_[:, 4:8], ALU.max, ALU.max, reads=[rs_], writes=[rs_])
            self.recip(s_[:, 12:16], s_[:, 8:12], reads=[rs_], writes=[rs_])
            self.tt(s_[:, 16:20], s_[:, 12:16], t_eb[:, ti * 4:ti * 4 + 4], ALU.mult, reads=[rs_, r_eb], writes=[rs_])
            hs, rhs = hst.next()
            for h in range(4):
                nv, rn = Nv(h)
                self.act(hs[:, h * 128:(h + 1) * 128], nv[:, 0:128], AF.Copy, reads=[rn, rs_], writes=[rhs], scale=s_[:, 16 + h:17 + h])
            if dr == 1:
                self.tt(hs, hs, ex[0], ALU.add, reads=[rhs, ex[1]], writes=[rhs])
            if dr == 0:
                self.dma(S["HF"][ti * 128:(ti + 1) * 128, :], hs, reads=[rhs], key=("hst", id(rhs)), q="act")
                return
            yield
            y, ry = yr.next()
            self.tt(y, hs, ex[2], ALU.mult, reads=[rhs, ex[3]], writes=[ry])
            s2, rs2 = sm.next()
            jk, rjk = junkr.next()
            for h in range(4):
                self.act(jk, y[:, h * 128:(h + 1) * 128], AF.Square, reads=[ry], writes=[rjk, rs2], accum_out=s2[:, h:h + 1])
            self.rstd(s2[:, 8:12], s2[:, 0:4], 1.0 / 128, reads=[rs2], writes=[rs2], tmp=s2[:, 4:8])
            yield
            y3 = _h(y, "p (h e) -> p h e", h=4)
            self.tt(y3, y3, s2[:, 8:12].unsqueeze(2).broadcast_to([128, 4, 128]), ALU.mult, reads=[ry, rs2], writes=[ry])
            yb, ryb = ybr.next()
            self.tt(yb, y, gml, ALU.mult, reads=[ry, rgml], writes=[ryb])
            pb_ = ptr.next()
            pv = self.psb_bf(pb_)
            for j in range(4):
                self.tr(pv[:, j * 128:(j + 1) * 128], yb[:, j * 128:(j + 1) * 128], self.identb, reads=[ryb], writes=[self.r_ps[pb_]])
            yield
            sg, rsg = stT.next()
            self.cp(sg, pv[:, 0:512], reads=[self.r_ps[pb_]], writes=[rsg], eng="act")
            self.dma(_h(S["OT"][3][:, ti * 128:(ti + 1) * 128], "(j p) t -> p j t", p=128), _h(sg, "p (j t) -> p j t", j=4),
                     reads=[rsg], key=("stTm", id(rsg)), q="act")

        pipe = Pipe()
        load(tiles[0])
        if len(tiles) > 1:
            load(tiles[1])
        for idx, ti in enumerate(tiles):
            if idx + 2 < len(tiles):
                load(tiles[idx + 2])
            pipe.push(chunk(ti))
        pipe.flush()
        if not sample:
            t_w, r_w = ww[dr]
            t_nt, r_nt = ntot[dr]
            c0, c1 = tiles[0], tiles[1]
            self.cp(fin[:, 0:4], t_w[:, c0 * 4:c0 * 4 + 4], reads=[r_w], writes=[rfin])
            self.cp(fin[:, 4:8], t_w[:, c1 * 4:c1 * 4 + 4], reads=[r_w], writes=[rfin])
            self.ts(fin[:, 8:12], t_nt[:, c0 * 4:c0 * 4 + 4], -1.0, ALU.mult, reads=[r_nt], writes=[rfin])
            self.ts(fin[:, 12:16], t_nt[:, c1 * 4:c1 * 4 + 4], -1.0, ALU.mult, reads=[r_nt], writes=[rfin])
            pb_ = ptr.next()
            pf = self.psb(pb_)
            self.tr(pf[0:16, 0:128], fin[:, 0:16], self.identf, reads=[rfin], writes=[self.r_ps[pb_]])
            self.red(fin[0:16, 16:17], pf[0:16, 0:128], ALU.max, reads=[self.r_ps[pb_]], writes=[rfin])
            self.tr(pf[0:1, 128:144], fin[0:16, 16:17], self.identf[0:16, 0:16], reads=[rfin], writes=[self.r_ps[pb_]])
            row = fin[0:1, 32:48]
            self.cp(row, pf[0:1, 128:144], reads=[self.r_ps[pb_]], writes=[rfin])
            self.tt(fin[0:1, 48:52], fin[0:1, 40:44], fin[0:1, 44:48], ALU.add, reads=[rfin], writes=[rfin])
            self.tt(fin[0:1, 52:56], fin[0:1, 44:48], fin[0:1, 32:36], ALU.add, reads=[rfin], writes=[rfin])
            self.tt(fin[0:1, 48:52], fin[0:1, 48:52], fin[0:1, 52:56], ALU.max, reads=[rfin], writes=[rfin])
            self.tt(fin[0:1, 48:52], fin[0:1, 48:52], fin[0:1, 36:40], ALU.max, reads=[rfin], writes=[rfin])
            self.dma(self.dout["o_m"][sq, l, dr:dr + 1, :], fin[0:1, 48:52], reads=[rfin], key="fin")
            self.mm(pf[:, 256:260], self.tri[2][0:1, :], fin[0:1, 48:52], True, True, reads=[rfin], writes=[self.r_ps[pb_]])
            self.act(fin[:, 56:60], pf[:, 256:260], AF.Exp, reads=[self.r_ps[pb_]], writes=[rfin], scale=-1.0)
            for h in range(4):
                self.ts(fco[:, h * 129:(h + 1) * 129], Cst[:, h * 129:(h + 1) * 129], fin[:, 56 + h:57 + h], ALU.mult,
                        reads=[rC, rfin], writes=[rfco])
            f3 = _h(fco, "p (h e) -> p h e", h=4)
            for h in range(4):
                self.dma(self.dout["o_C"][sq, l, dr, h], f3[:, h, 0:128], reads=[rfco], key="fco")
                self.dma(_h(self.dout["o_n"][sq, l, dr, h], "(p o) -> p o", o=1), f3[:, h, 128:129], reads=[rfco], key="fco", slow=True)

    seqs = [(True, list(range(0, NTS)), None), (False, [NTS, NTS + 1], 0), (False, [NTS + 2, NTS + 3], 1)]
    for dr in range(2):
        for (sample, tiles, sq) in seqs:
            run_pass(dr, tiles if dr == 0 else tiles[::-1], sample, sq)
        self.P.barrier()
    self.A.pop()


KB.phase_E = _phase_E


def _phase_F(self, l):
    d = self.din
    S = self.scr
    self.A.push()
    wb, rwb = self.alloc(4 * 4 * D, BF16, "wb")
    wb4 = _h(wb, "p (i k c) -> p i k c", i=4, k=4)
    for i in range(4):
        self.dma(wb4[:, i, :, :], _h(d["w_branch"][l, i], "(k p) c -> p k c", p=128), writes=[rwb], key="wb", q="pool")
    wo, rwo = self.alloc(8 * D, BF16, "wo")
    wo3 = _h(wo, "p (k c) -> p k c", k=8)
    self.dma(wo3, _h(d["w_out"][l], "(k p) c -> p k c", p=128), writes=[rwo], key="wo", q="pool")
    otr = self.rot(6, 4 * 512, BF16, "ot")
    gtr = self.rot(4, 512, BF16, "gt")
    gbr = self.rot(6, 512, BF16, "gated")
    tmpr = self.rot(3, 512, F32, "tmpf")
    mTr = self.rot(2, 8 * 512, BF16, "mT")
    self.load_mod((2, 3, 4))
    xr = self.rot(2, D, F32, "xF")
    x1r = self.rot(4, D, F32, "x1F")
    bufs = {"junk": self.rot(2, D, BF16, "junkF"), "ss": self.rot(4, 4, F32, "ssF"), "t": self.rot(2, D, F32, "tF"),
            "hb": self.rot(3, D, BF16, "hbF"), "pt": Rot([6, 7])}
    h2s = self.rot(2, D, BF16, "h2s")
    mmr = Rot([0, 1])
    mmo = Rot([2, 3])
    accb = Rot([4, 5])
    pipe = Pipe()
    pending = []

    def tile_part2(tc, tt_, mT3_, rmT_):
        ti = tc * 4 + tt_
        g = 0 if ti < NTS else 1
        xt, rx = xr.next()
        src = self.tile_rows(ti) if l == 0 else S["XRES"][ti * 128:(ti + 1) * 128, :]
        self.dma(xt, src, reads=[self.r_xres[ti]], writes=[rx], key=("xF", id(rx)))
        x1, rx1 = x1r.next()
        for nh in range(2):
            hs = slice(nh * 512, (nh + 1) * 512)
            pb = mmo.next()
            ps = self.psb(pb)
            for k in range(8):
                self.mm(ps, mT3_[:, k, tt_ * 128:(tt_ + 1) * 128], wo3[:, k, hs], k == 0, k == 7, reads=[rmT_, rwo], writes=[self.r_ps[pb]])
            tmp, rtmp = tmpr.next()
            self.tt(tmp, ps, self.mod(g, 2)[:, hs], ALU.mult, reads=[self.r_ps[pb], self.r_modt], writes=[rtmp])
            self.tt(x1[:, hs], tmp, xt[:, hs], ALU.add, reads=[rtmp, rx], writes=[rx1], eng="pool")
        self.dma(S["XRES"][ti * 128:(ti + 1) * 128, :], x1, reads=[rx1], writes=[self.r_xres[ti]], key=("x1F", id(rx1)), q="pool")

        def dst(pv, rp, ti=ti):
            sg, rsg = h2s.next()
            self.cp(sg, pv, reads=[rp], writes=[rsg], eng="act")
            self.dma(_h(S["H2T"][:, ti * 128:(ti + 1) * 128], "(k p) t -> p k t", p=128), _h(sg, "p (k t) -> p k t", k=8),
                     reads=[rsg], key=("h2s", id(rsg)), q="act")
        pipe.push(self.norm_mod_T(x1, rx1, g, 4, 3, dst, bufs))

    for tc in range(TOK // 512):
        cs = slice(tc * 512, (tc + 1) * 512)
        mT_, rmT_ = mTr.next()
        mT3_ = _h(mT_, "p (k t) -> p k t", k=8)
        ots = []
        for i in range(4):
            ot, rot_ = otr.next()
            self.dma(_h(ot, "p (k t) -> p k t", k=4), _h(S["OT"][i][:, cs], "(k p) t -> p k t", p=128), writes=[rot_], key=("ot", id(rot_)))
            ots.append((_h(ot, "p (k t) -> p k t", k=4), rot_))
        if tc == 0:
            self.warm(reads=[rwb, rwo] + [o[1] for o in ots])
        for db in range(8):
            ab = accb.next()
            pacc = self.psb(ab)
            gated = []
            for i in range(4):
                pb = mmr.next()
                ps = self.psb(pb)
                for kc in range(4):
                    self.mm(ps, wb4[:, i, kc, db * 128:(db + 1) * 128], ots[i][0][:, kc, :], kc == 0, kc == 3,
                            reads=[rwb, ots[i][1]], writes=[self.r_ps[pb]])
                g, rg = gtr.next()
                self.dma(g, S["GT"][i * 1024 + db * 128:i * 1024 + (db + 1) * 128, cs], writes=[rg], key=("gt", id(rg)))
                tb_, rtb = gbr.next()
                self.tt(tb_, ps, g, ALU.mult, reads=[self.r_ps[pb], rg], writes=[rtb])
                gated.append((tb_, rtb))
                if i >= 1:
                    pt_, rpt_ = gated[i - 1]
                    self.mm(pacc, self.identb, pt_, i == 1, False, reads=[rpt_], writes=[self.r_ps[ab]])
            pt_, rpt_ = gated[3]
            self.mm(pacc, self.identb, pt_, False, True, reads=[rpt_], writes=[self.r_ps[ab]])
            self.cp(mT3_[:, db, :], pacc, reads=[self.r_ps[ab]], writes=[rmT_], eng="act")
            if db % 2 == 1 and pending:
                pending.pop(0)()
        while pending:
            pending.pop(0)()
        for tt_ in range(4):
            pending.append(lambda tc=tc, tt_=tt_, m3=mT3_, rm=rmT_: tile_part2(tc, tt_, m3, rm))
    while pending:
        pending.pop(0)()
    pipe.flush()
    self.P.barrier()
    self.A.pop()


def _phase_G(self, l, last):
    d = self.din
    S = self.scr
    self.A.push()
    NJ = D_FF // 128
    TB = 9
    JG = 4
    self.load_mod((5,))
    w2, rw2 = self.alloc(NJ * D, BF16, "w2")
    w23 = _h(w2, "p (j c) -> p j c", j=NJ)
    for q in range(2):
        self.dma(w23[:, q * 11:(q + 1) * 11, :], _h(d["w_ffn_out"][l][q * 1408:(q + 1) * 1408, :], "(j p) c -> p j c", p=128),
                 writes=[rw2], key="w2", q="pool")
    h2r = self.rot(1, 8 * TB * 128, BF16, "h2blk")
    actT, raT = self.alloc(NJ * TB * 128, BF16, "actT")
    a3 = _h(actT, "p (j t) -> p j t", j=NJ)
    wsr = self.rot(2, 8 * 2 * JG * 128, BF16, "w1sl")
    sgr = self.rot(3, 384, F32, "sg")
    x1r = self.rot(2, D, F32, "x1G")
    x2r = self.rot(2, D, F32, "x2G")
    tmpr = self.rot(2, 512, F32, "tmpG")
    junk, rj = self.alloc(D, BF16, "junkG")
    ssr = self.rot(2, 4, F32, "ssG")
    mmr = Rot([0, 1, 2, 3, 4, 5, 6, 7])
    jgroups = [list(range(j0, min(j0 + JG, NJ))) for j0 in range(0, NJ, JG)]
    for blk in range(NT // TB):
        t0 = blk * TB * 128
        h2, rh2 = h2r.next()
        h23 = _h(h2, "p (k t) -> p k t", k=8)
        self.dma(h23, _h(S["H2T"][:, t0:t0 + TB * 128], "(k p) t -> p k t", p=128), writes=[rh2], key=("h2blk", id(rh2)))
        for js in jgroups:
            nj = len(js)
            ws, rws = wsr.next()
            ws4 = _h(ws, "p (k u c) -> p k u c", k=8, u=2)
            c0 = js[0] * 128
            self.dma(ws4[:, :, 0, 0:nj * 128], _h(d["w_ffn_in"][l][:, c0:c0 + nj * 128], "(k p) c -> p k c", p=128), writes=[rws],
                     key=("w1sl", id(rws)), q="pool")
            self.dma(ws4[:, :, 1, 0:nj * 128], _h(d["w_ffn_in"][l][:, D_FF + c0:D_FF + c0 + nj * 128], "(k p) c -> p k c", p=128),
                     writes=[rws], key=("w1sl", id(rws)), q="pool")
            for ji, j in enumerate(js):
                for sub in range(TB * 128 // 384):
                    ss_ = slice(sub * 384, (sub + 1) * 384)
                    pg = mmr.next(); pu = mmr.next()
                    for k in range(8):
                        self.mm(self.psb(pg)[:, 0:384], ws4[:, k, 0, ji * 128:(ji + 1) * 128], h23[:, k, ss_], k == 0, k == 7,
                                reads=[rws, rh2], writes=[self.r_ps[pg]])
                    for k in range(8):
                        self.mm(self.psb(pu)[:, 0:384], ws4[:, k, 1, ji * 128:(ji + 1) * 128], h23[:, k, ss_], k == 0, k == 7,
                                reads=[rws, rh2], writes=[self.r_ps[pu]])
                    sg, rsg = sgr.next()
                    self.act(sg, self.psb(pg)[:, 0:384], AF.Silu, reads=[self.r_ps[pg]], writes=[rsg])
                    self.tt(a3[:, j, ss_], sg, self.psb(pu)[:, 0:384], ALU.mult, reads=[rsg, self.r_ps[pu]], writes=[raT])
        for tt_ in range(TB):
            ti = blk * TB + tt_
            g = 0 if ti < NTS else 1
            x1, rx1 = x1r.next()
            self.dma(x1, S["XRES"][ti * 128:(ti + 1) * 128, :], reads=[self.r_xres[ti]], writes=[rx1], key=("x1G", id(rx1)))
            x2, rx2 = x2r.next()
            for nh in range(2):
                hs = slice(nh * 512, (nh + 1) * 512)
                pb = mmr.next()
                ps = self.psb(pb)
                for j in range(NJ):
                    self.mm(ps, a3[:, j, tt_ * 128:(tt_ + 1) * 128], w23[:, j, hs], j == 0, j == NJ - 1, reads=[raT, rw2], writes=[self.r_ps[pb]])
                tmp, rtmp = tmpr.next()
                self.tt(tmp, ps, self.mod(g, 5)[:, hs], ALU.mult, reads=[self.r_ps[pb], self.r_modt], writes=[rtmp])
                self.tt(x2[:, hs], tmp, x1[:, hs], ALU.add, reads=[rtmp, rx1], writes=[rx2], eng="pool")
            if not last:
                self.dma(S["XRES"][ti * 128:(ti + 1) * 128, :], x2, reads=[rx2], writes=[self.r_xres[ti]], key=("x2G", id(rx2)))
            else:
                ssb, rss = ssr.next()
                self.act(junk, x2, AF.Square, reads=[rx2], writes=[rj, rss], accum_out=ssb[:, 0:1])
                self.rstd(ssb[:, 2:3], ssb[:, 0:1], 1.0 / D, reads=[rss], writes=[rss], tmp=ssb[:, 1:2])
                self.stt(x2, x2, ssb[:, 2:3], self.gfin, ALU.mult, ALU.mult, reads=[rx2, rss], writes=[rx2])
                if ti < NTS:
                    dsto = self.dout["y_s"][ti * 128:(ti + 1) * 128, :]
                else:
                    dsto = self.dout["y_p"][(ti - NTS) * 128:(ti - NTS + 1) * 128, :]
                self.dma(dsto, x2, reads=[rx2], key=("x2G", id(rx2)))
    self.P.barrier()
    self.A.pop()


KB.phase_F = _phase_F
KB.phase_G = _phase_G


_CACHE = {}


def _get_program():
    if "nc" not in _CACHE:
        kb = KB()
        _CACHE["nc"] = kb.build()
        _CACHE["kb"] = kb
    return _CACHE["nc"]


def kernel(**inputs):
    inputs = {k: np.asarray(v) for k, v in inputs.items()}
    nc = _get_program()
    consts = _host_consts()
    nab = _na_bias_tiles(inputs["na_rpb"].astype(np.float32))
    in_maps = [_core_inputs(inputs, c, consts, nab) for c in range(8)]
    res = run_bass_kernel_spmd(nc, in_maps, core_ids=list(range(8)))
    rs = res.results
    f32 = np.float32
    y_prompt = np.concatenate([np.asarray(r["y_p"], f32).reshape(2, 256, D) for r in rs], 0)
    y_sample = np.stack([np.asarray(r["y_s"], f32) for r in rs], 0)
    na_kv = np.concatenate([np.asarray(r["o_na"], f32).reshape(2, 2, 2, 256, 8, 64) for r in rs], 0)
    gqa_kv = np.concatenate([np.asarray(r["o_g"], f32).reshape(2, 2, 2, 256, 2, 64) for r in rs], 0)
    swa_kv = np.concatenate([np.asarray(r["o_s"], f32).reshape(2, 2, 2, 256, 2, 64) for r in rs], 0)
    ml_C = np.concatenate([np.asarray(r["o_C"], f32) for r in rs], 0)
    ml_n = np.concatenate([np.asarray(r["o_n"], f32) for r in rs], 0)
    ml_m = np.concatenate([np.asarray(r["o_m"], f32) for r in rs], 0)
    return (y_prompt, y_sample, na_kv, gqa_kv, swa_kv, ml_C, ml_n, ml_m)
```

```python
import math
from contextlib import ExitStack

import numpy as np
import ml_dtypes

import concourse.bass as bass
import concourse.mybir as mybir
from concourse.alu_op_type import AluOpType as ALU
from concourse.bass_utils import run_bass_kernel_spmd

F32 = mybir.dt.float32
BF16 = mybir.dt.bfloat16
AF = mybir.ActivationFunctionType
AX = mybir.AxisListType

D = 1024
DEPTH = 2
NT = 36
NTS = 32
TOK = NT * 128
SEQ_S = 4096
GRID_W = 64
DH = 64
N_IN = 9232
D_FF = 2816
EPS = 1e-6
NEG = -30000.0
O_NAQ, O_NAK, O_NAV = 0, 512, 1024
O_GQ, O_GK, O_GV = 1536, 2048, 2176
O_SQ, O_SK, O_SV = 2304, 2816, 2944
O_MQ, O_MK, O_MV, O_MO, O_MI, O_MF = 3072, 3584, 4096, 4608, 5120, 5128
O_GATES = 5136

STREAMS = ("pe", "act", "dve", "pool", "sp")


class Res:
    __slots__ = ("name", "w", "r", "const")

    def __init__(self, name):
        self.name = name
        self.w = None
        self.r = []
        self.const = False


class Op:
    __slots__ = ("stream", "fn", "deps", "flag", "dma_key", "dma_val", "rank", "idx", "extra_waits")

    def __init__(self, stream, fn):
        self.stream = stream
        self.fn = fn
        self.deps = []
        self.flag = False
        self.dma_key = None
        self.dma_val = 0
        self.rank = 0
        self.idx = 0
        self.extra_waits = None


class Prog:
    def __init__(self):
        self.ops = []
        self.last = {s: None for s in STREAMS}
        self.dma_count = {}
        self.dma_last = {}
        self.key2sem = {}

    def _dep(self, o, p):
        if p is None or p is o:
            return
        if p.dma_key is None and o.dma_key is None and p.stream == "pe" and o.stream == "pe":
            return
        o.deps.append(p)
        p.flag = True

    def add(self, stream, fn, reads=(), writes=(), dma_key=None):
        o = Op(stream, fn)
        o.idx = len(self.ops)
        if dma_key is not None:
            ns = "sw" if stream == "pool" else "hw"
            if (ns, dma_key) not in self.key2sem:
                self.key2sem[(ns, dma_key)] = (ns, sum(1 for k in self.key2sem if k[0] == ns))
            dma_key = self.key2sem[(ns, dma_key)]
            o.dma_key = dma_key
            c = self.dma_count.get(dma_key, 0) + 16
            self.dma_count[dma_key] = c
            o.dma_val = c
            self.dma_last[dma_key] = o
        for r in reads:
            if r.const:
                continue
            self._dep(o, r.w)
            r.r.append(o)
        for w in writes:
            assert not w.const, w.name
            self._dep(o, w.w)
            for rd in w.r:
                self._dep(o, rd)
            w.w = o
            w.r = []
        self.ops.append(o)
        self.last[stream] = o
        return o

    def barrier(self):
        lasts = [self.last[s] for s in STREAMS if self.last[s] is not None]
        dmas = list(self.dma_last.values())
        for s in STREAMS:
            o = Op(s, None)
            o.idx = len(self.ops)
            for p in lasts + dmas:
                if p is not None:
                    o.deps.append(p)
                    p.flag = True
            self.ops.append(o)
        self.key2sem = {}

    def emit(self, nc, stack):
        cnt = {s: 0 for s in STREAMS}
        for o in self.ops:
            if o.dma_key is None and o.flag and o.fn is not None:
                cnt[o.stream] += 1
                o.rank = cnt[o.stream]
        esem = {s: stack.enter_context(nc.semaphore("e_" + s)) for s in STREAMS}
        dsem = {k: stack.enter_context(nc.semaphore("d_%d" % i)) for i, k in enumerate(self.dma_count)}
        self.n_sems = len(esem) + len(dsem)
        by_stream = {s: [] for s in STREAMS}
        for o in self.ops:
            by_stream[o.stream].append(o)
        key_prog = {}
        upgraded = {}
        for o in self.ops:
            for p in o.deps:
                if p.dma_key is not None:
                    upgraded[(o.idx, p.idx)] = max(p.dma_val, key_prog.get(p.dma_key, 0))
            if o.dma_key is not None:
                key_prog[o.dma_key] = o.dma_val

        def run_stream(s, eng):
            seen = {}
            for o in by_stream[s]:
                need = {}
                for p in o.deps:
                    if p.dma_key is not None:
                        sem = dsem[p.dma_key]
                        val = upgraded[(o.idx, p.idx)]
                        kk = ("d", p.dma_key)
                    else:
                        if p.fn is None:
                            continue
                        sem = esem[p.stream]
                        val = p.rank
                        kk = ("e", p.stream)
                    if seen.get(kk, 0) >= val:
                        continue
                    if kk not in need or need[kk][1] < val:
                        need[kk] = (sem, val)
                for kk, (sem, val) in need.items():
                    eng.wait_ge(sem, val)
                    seen[kk] = val
                if o.fn is None:
                    continue
                ins = o.fn(eng)
                if o.dma_key is not None:
                    ins.then_inc(dsem[o.dma_key], 16)
                elif o.flag:
                    ins.then_inc(esem[s], 1)

        block = stack.enter_context(nc.Block())

        @block.tensor
        def _(e):
            run_stream("pe", e)

        @block.scalar
        def _(e):
            run_stream("act", e)

        @block.vector
        def _(e):
            run_stream("dve", e)

        @block.gpsimd
        def _(e):
            run_stream("pool", e)

        @block.sync
        def _(e):
            run_stream("sp", e)


class Arena:
    def __init__(self, tensor, ncols):
        self.t = tensor
        self.ncols = ncols
        self.top = 0
        self.marks = []
        self.peak = 0

    def push(self):
        self.marks.append(self.top)

    def pop(self):
        self.top = self.marks.pop()

    def alloc(self, cols, dtype=F32, parts=128):
        n32 = cols if dtype == F32 else (cols + 1) // 2
        n32 = (n32 + 7) // 8 * 8
        off = self.top
        self.top += n32
        assert self.top <= self.ncols, "SBUF arena overflow: %d > %d" % (self.top, self.ncols)
        self.peak = max(self.peak, self.top)
        v = self.t[0:parts, off:off + n32]
        if dtype != F32:
            v = v.bitcast(dtype)[:, 0:cols]
        else:
            v = v[:, 0:cols]
        return v


class Rot:
    def __init__(self, items):
        self.items = items
        self.i = 0

    def next(self):
        it = self.items[self.i % len(self.items)]
        self.i += 1
        return it


def _h(ap, pattern, **kw):
    return ap.rearrange(pattern, **kw)


class Pipe:
    def __init__(self):
        self.active = []

    def push(self, gen):
        if gen is not None:
            self.active.insert(0, gen)
        self.step()

    def step(self):
        for g in list(self.active):
            try:
                next(g)
            except StopIteration:
                self.active.remove(g)

    def flush(self):
        while self.active:
            self.step()


def _na_kbs(a):
    rows = set()
    for r in (2 * a, 2 * a + 1):
        rs = min(max(r - 4, 0), 56)
        rows.update(range(rs, rs + 8))
    return sorted(set(r // 2 for r in rows))


def _na_class(a):
    return {0: 0, 1: 1, 30: 3, 31: 4}.get(a, 2)


def _na_tile_index():
    idx = {}
    for a in (0, 1, 2, 30, 31):
        for kb in reversed(_na_kbs(a)):
            key = (_na_class(a), kb - a)
            if key not in idx:
                idx[key] = len(idx)
    return idx


NA_TILE_IDX = _na_tile_index()
NA_NB = len(NA_TILE_IDX)


def _na_gather_indices():
    out = {}
    reps = {0: 0, 1: 1, 2: 2, 3: 30, 4: 31}
    for (cls, delta), tid in NA_TILE_IDX.items():
        a = reps[cls]
        kb = a + delta
        kk = np.arange(128)
        kr = 2 * kb + kk // 64
        kc = kk % 64
        qr = 2 * a + kk // 64
        qc = kk % 64
        rs = np.clip(qr - 4, 0, 56)
        cs = np.clip(qc - 8, 0, 48)
        valid = ((kr[:, None] >= rs[None, :]) & (kr[:, None] < rs[None, :] + 8) &
                 (kc[:, None] >= cs[None, :]) & (kc[:, None] < cs[None, :] + 16))
        dr = np.clip(kr[:, None] - qr[None, :] + 7, 0, 14)
        dc = np.clip(kc[:, None] - qc[None, :] + 15, 0, 30)
        out[tid] = (dr, dc, valid)
    return out


def _host_consts():
    c = {}
    c["identb"] = np.eye(128, dtype=np.float32).astype(ml_dtypes.bfloat16)
    c["identf"] = np.eye(128, dtype=np.float32)
    s = np.arange(128)
    le = (s[:, None] <= s[None, :]).astype(np.float32)
    ge = (s[:, None] >= s[None, :]).astype(np.float32)
    c["tri_f32"] = np.stack([le, ge, np.ones((128, 128), np.float32)], 0)
    c["tri_bf"] = np.stack([le, ge], 0).astype(ml_dtypes.bfloat16)
    c["swa_mask"] = np.stack([np.where(s[None, :] <= s[:, None], 0.0, NEG),
                              np.where(s[:, None] <= s[None, :], 0.0, NEG)], 0).astype(ml_dtypes.bfloat16)
    half = 32
    freqs = (10000.0 ** (-np.arange(0, half, 2, dtype=np.float32) / half)).astype(np.float32)
    t = np.arange(SEQ_S)
    ang_r = (t // GRID_W).astype(np.float32)[:, None] * freqs
    ang_c = (t % GRID_W).astype(np.float32)[:, None] * freqs
    cr, sr, cc, sc = np.cos(ang_r), np.sin(ang_r), np.cos(ang_c), np.sin(ang_c)
    c["rope_cos"] = np.concatenate([cr, cr, cc, cc], 1).astype(np.float32)
    c["rope_sin"] = np.concatenate([-sr, sr, -sc, sc], 1).astype(np.float32)
    return c


HOST_CONSTS = None


class KB:
    def __init__(self, debug=None, stop_after=None, layers=DEPTH):
        self.debug = debug or ()
        self.stop_after = stop_after
        self.layers = layers
        self.nc = bass.Bass("TRN2", target_bir_lowering=False)
        self.P = Prog()
        self.din = {}
        self.dout = {}
        self.scr = {}

    def _in(self, name, shape, dtype=F32):
        self.din[name] = self.nc.dram_tensor(name, list(shape), dtype, kind="ExternalInput").ap()

    def _out(self, name, shape, dtype=F32):
        self.dout[name] = self.nc.dram_tensor(name, list(shape), dtype, kind="ExternalOutput").ap()

    def _scr(self, name, shape, dtype):
        kind = "ExternalOutput" if name in self.debug else "Internal"
        self.scr[name] = self.nc.dram_tensor("scr_" + name, list(shape), dtype, kind=kind).ap()

    def declare(self):
        i = self._in
        i("xs", [4096, D]); i("xp", [512, D])
        i("cna", [2, 2, 256, 512]); i("cg", [2, 2, 256, 128]); i("cs", [2, 2, 256, 128])
        i("stC", [2, 2, 4, 128, 128]); i("stn", [2, 2, 4, 128]); i("stm", [2, 8])
        i("cvec", [2, D])
        i("w_mod", [2, D, 6 * D]); i("b_mod", [2, 6 * D]); i("norm1_g", [2, D]); i("norm2_g", [2, D])
        i("w_in", [2, D, N_IN]); i("b_in", [2, N_IN])
        i("gqa_q_g", [2, 64]); i("gqa_k_g", [2, 64]); i("swa_sink", [2, 8]); i("mlstm_norm_g", [2, 512])
        i("w_branch", [2, 4, 512, D]); i("w_out", [2, D, D]); i("w_ffn_in", [2, D, 2 * D_FF])
        i("w_ffn_out", [2, D_FF, D]); i("final_norm_g", [1, D])
        i("nabias", [2, 8, NA_NB, 128, 128])
        i("identb", [128, 128], BF16); i("identf", [128, 128]); i("tri_f32", [3, 128, 128])
        i("tri_bf", [2, 128, 128], BF16); i("swa_mask", [2, 128, 128], BF16)
        i("rope_cos", [4096, 64]); i("rope_sin", [4096, 64])
        o = self._out
        o("y_s", [4096, D]); o("y_p", [512, D])
        o("o_na", [2, 2, 2, 256, 512]); o("o_g", [2, 2, 2, 256, 128]); o("o_s", [2, 2, 2, 256, 128])
        o("o_C", [2, 2, 2, 4, 128, 128]); o("o_n", [2, 2, 2, 4, 128]); o("o_m", [2, 2, 2, 4])
        s = self._scr
        s("XRES", [TOK, D], F32)
        s("QT_na", [512, TOK], BF16); s("KT_na", [512, TOK], BF16); s("V_na", [TOK, 512], BF16)
        s("QT_g", [512, TOK], BF16); s("KT_g", [128, TOK], BF16); s("V_g", [TOK, 128], BF16)
        s("QT_s", [512, TOK], BF16); s("KT_s", [128, TOK], BF16); s("V_s", [TOK, 128], BF16)
        s("QT_m", [512, TOK], BF16); s("KT_m", [512, TOK], BF16); s("K_m", [TOK, 512], BF16)
        s("V_m", [TOK, 4 * 129], BF16); s("O_m", [TOK, 512], F32)
        s("GT", [4096, TOK], BF16)
        s("HF", [TOK, 512], F32)
        s("OT", [4, 512, TOK], BF16)
        s("H2T", [D, TOK], BF16)
        if "HT" in self.debug:
            s("HT", [D, TOK], BF16)
        s("MOD", [2, 128, 6 * D], F32)
        if "IFD" in self.debug:
            s("IFD", [128, NT * 16], F32)

    def dma(self, out, in_, reads=(), writes=(), key=None, q="sp", slow=False):
        assert key is not None
        if slow:
            fn = lambda e: e.dma_start(out=out, in_=in_, allow_slow_non_contiguous=True)
        else:
            fn = lambda e: e.dma_start(out=out, in_=in_)
        return self.P.add(q, fn, reads=reads, writes=writes, dma_key=key)

    def mm(self, out, lhsT, rhs, start, stop, reads=(), writes=(), **kw):
        return self.P.add("pe", lambda e: e.matmul(out, lhsT=lhsT, rhs=rhs, start=start, stop=stop, **kw),
                          reads=reads, writes=writes)

    def tr(self, out, in_, ident, reads=(), writes=()):
        return self.P.add("pe", lambda e: e.transpose(out=out, in_=in_, identity=ident), reads=reads, writes=writes)

    def act(self, out, in_, func, reads=(), writes=(), **kw):
        return self.P.add("act", lambda e: e.activation(out=out, in_=in_, func=func, **kw), reads=reads, writes=writes)

    def tt(self, out, in0, in1, op, reads=(), writes=(), eng="dve"):
        return self.P.add(eng, lambda e: e.tensor_tensor(out=out, in0=in0, in1=in1, op=op), reads=reads, writes=writes)

    def ts(self, out, in0, s1, op0, s2=None, op1=None, reads=(), writes=(), eng="dve"):
        if op1 is None:
            fn = lambda e: e.tensor_scalar(out=out, in0=in0, scalar1=s1, scalar2=None, op0=op0)
        else:
            fn = lambda e: e.tensor_scalar(out=out, in0=in0, scalar1=s1, scalar2=s2, op0=op0, op1=op1)
        return self.P.add(eng, fn, reads=reads, writes=writes)

    def stt(self, out, in0, scalar, in1, op0, op1, reads=(), writes=()):
        return self.P.add("dve", lambda e: e.scalar_tensor_tensor(out=out, in0=in0, scalar=scalar, in1=in1, op0=op0, op1=op1),
                          reads=reads, writes=writes)

    def cp(self, out, in_, reads=(), writes=(), eng="dve"):
        if eng == "act":
            return self.act(out, in_, AF.Copy, reads=reads, writes=writes)
        return self.P.add(eng, lambda e: e.tensor_copy(out=out, in_=in_), reads=reads, writes=writes)

    def red(self, out, in_, op, reads=(), writes=()):
        return self.P.add("dve", lambda e: e.tensor_reduce(out=out, in_=in_, axis=AX.X, op=op), reads=reads, writes=writes)

    def recip(self, out, in_, reads=(), writes=()):
        return self.P.add("dve", lambda e: e.reciprocal(out=out, in_=in_), reads=reads, writes=writes)

    def recip_fast(self, out, in_, reads=(), writes=()):
        return self.P.add("dve", lambda e: e.reciprocal_approx_fast(out=out, in_=in_), reads=reads, writes=writes)

    def memset(self, ap, val, writes=(), eng="dve"):
        return self.P.add(eng, lambda e: e.memset(ap, val), writes=writes)

    def rstd(self, out, ss, scale, reads=(), writes=(), tmp=None):
        self.act(tmp, ss, AF.Ln, reads=reads, writes=writes, scale=scale, bias=self.eps_col[0:ss.shape[0], :])
        self.act(out, tmp, AF.Exp, reads=writes, writes=writes, scale=-0.5)

    def alloc(self, cols, dtype=F32, name="t", parts=128):
        return self.A.alloc(cols, dtype, parts), Res(name)

    def warm(self, reads=(), n=24, bank=7):
        for i in range(n):
            self.mm(self.psb(bank), self.identb, self.wz, True, True, reads=list(reads), writes=[self.r_ps[bank]])

    def rot(self, n, cols, dtype=F32, name="r"):
        return Rot([self.alloc(cols, dtype, "%s%d" % (name, i)) for i in range(n)])

    def tile_rows(self, ti):
        if ti < NTS:
            return self.din["xs"][ti * 128:(ti + 1) * 128, :]
        return self.din["xp"][(ti - NTS) * 128:(ti - NTS + 1) * 128, :]

    def build(self):
        nc = self.nc
        self.declare()
        st = ExitStack()
        with st:
            arena_t = st.enter_context(nc.sbuf_tensor("arena", [128, 52000], F32))
            self.ps_t = st.enter_context(nc.psum_tensor("psum", [128, 8, 512], F32))
            self.A = Arena(arena_t, 52000)
            self.r_ps = [Res("ps%d" % i) for i in range(8)]
            self.r_xres = [Res("xres%d" % i) for i in range(NT)]
            self.body()
            self.P.barrier()
            self.P.emit(nc, st)
        return nc

    def psb(self, b):
        return self.ps_t[:, b, :]

    def psb_bf(self, b):
        return self.ps_t[:, b, :].bitcast(BF16)

    def done(self, name):
        return self.stop_after == name

    def body(self):
        self.preamble()
        for l in range(self.layers):
            self.A.push()
            self.A.push()
            self.phase_A(l)
            self.A.pop()
            if self.done("A%d" % l):
                return
            self.IF, self.r_IF = self.alloc(NT * 16, F32, "IF")
            self.A.push()
            self.hT, _ = self.alloc(8 * TOK, BF16, "hT")
            self.r_hT = [Res("hT%d" % i) for i in range(NT)]
            self.phase_B(l)
            if self.done("B%d" % l):
                return
            self.phase_C(l)
            self.P.barrier()
            self.A.pop()
            if self.done("C%d" % l):
                return
            if not getattr(self, "skip_D", False):
                self.phase_D(l)
            if self.done("D%d" % l):
                return
            if not getattr(self, "skip_E", False):
                self.phase_E(l)
            if self.done("E%d" % l):
                return
            self.phase_F(l)
            if self.done("F%d" % l):
                return
            self.phase_G(l, l == self.layers - 1)
            if self.done("G%d" % l):
                return
            self.A.pop()

    def preamble(self):
        A = self.A
        d = self.din
        self.identb, r = self.alloc(128, BF16, "identb")
        self.dma(self.identb, d["identb"][:, :], writes=[r], key="c0")
        self.r_const = r
        self.identf, _ = self.alloc(128, F32, "identf")
        self.dma(self.identf, d["identf"][:, :], writes=[r], key="c0")
        self.tri = []
        for k in range(3):
            t, _ = self.alloc(128, F32, "tri")
            self.dma(t, d["tri_f32"][k], writes=[r], key="c0")
            self.tri.append(t)
        self.trib = []
        for k in range(2):
            t, _ = self.alloc(128, BF16, "trib")
            self.dma(t, d["tri_bf"][k], writes=[r], key="c0")
            self.trib.append(t)
        self.swam = []
        for k in range(2):
            t, _ = self.alloc(128, BF16, "swam")
            self.dma(t, d["swa_mask"][k], writes=[r], key="c0")
            self.swam.append(t)
        self.wz, _ = self.alloc(512, BF16, "wz")
        self.memset(self.wz, 0.0, writes=[r])
        self.eps_col, _ = self.alloc(1, F32, "eps")
        self.memset(self.eps_col, EPS, writes=[r])
        self.one_col, _ = self.alloc(1, F32, "one")
        self.memset(self.one_col, 1.0, writes=[r])
        cvT, rc = self.alloc(16, F32, "cvT")
        self.dma(_h(cvT, "p (g k) -> p g k", g=2), _h(d["cvec"], "g (k p) -> p g k", p=128), writes=[rc], key="c1", slow=True)
        sv, rs = self.alloc(16, F32, "sv")
        self.act(sv, cvT, AF.Silu, reads=[rc], writes=[rs])
        self.cb = []
        for g in range(2):
            t, _ = self.alloc(8 * 128, BF16, "cb")
            self.cp(_h(t, "p (k m) -> p k m", k=8), sv[:, g * 8:(g + 1) * 8].unsqueeze(2).broadcast_to([128, 8, 128]),
                    reads=[rs], writes=[r])
            self.cb.append(t)
        self.gfin, _ = self.alloc(D, F32, "gfin")
        self.dma(self.gfin, d["final_norm_g"].broadcast_to([128, D]), writes=[r], key="c0")
        self.P.barrier()
        r.const = True

    def phase_A(self, l):
        d = self.din
        rc = self.r_const
        self.MOD = []
        self.r_MOD = []
        for g in range(2):
            t, r = self.alloc(6 * D, F32, "MOD%d" % g)
            self.MOD.append(t)
            self.r_MOD.append(r)
        self.A.push()
        wsl = self.rot(2, 8 * 512, BF16, "wmod")
        bsl = self.rot(2, 512, F32, "bmod")
        pr = Rot([0, 1, 2, 3])
        for cg in range(12):
            w, rw = wsl.next()
            b, rb = bsl.next()
            self.dma(_h(w, "p (k c) -> p k c", k=8), _h(d["w_mod"][l][:, cg * 512:(cg + 1) * 512], "(k p) c -> p k c", p=128),
                     writes=[rw], key=("wmod", id(rw)), q="pool")
            self.dma(b, d["b_mod"][l:l + 1, cg * 512:(cg + 1) * 512].broadcast_to([128, 512]), writes=[rb], key=("bmod", id(rb)))
            for g in range(2):
                pb = pr.next()
                for k in range(8):
                    self.mm(self.psb(pb), self.cb[g][:, k * 128:(k + 1) * 128], w[:, k * 512:(k + 1) * 512], k == 0, k == 7,
                            reads=[rw, rc], writes=[self.r_ps[pb]])
                self.tt(self.MOD[g][:, cg * 512:(cg + 1) * 512], self.psb(pb), b, ALU.add,
                        reads=[self.r_ps[pb], rb], writes=[self.r_MOD[g]])
        gt, rg = self.alloc(D, F32, "gtmp")
        for (col, nm) in ((1, "norm1_g"), (4, "norm2_g")):
            self.dma(gt, d[nm][l:l + 1, :].broadcast_to([128, D]), writes=[rg], key="gtmp")
            for g in range(2):
                v = self.MOD[g][:, col * D:(col + 1) * D]
                self.stt(v, v, 1.0, gt, ALU.add, ALU.mult, reads=[rg, self.r_MOD[g]], writes=[self.r_MOD[g]])
        for g in range(2):
            self.dma(self.scr["MOD"][g], self.MOD[g], reads=[self.r_MOD[g]], key="modst")
        self.P.barrier()
        self.A.pop()

    def load_mod(self, idxs):
        self.modt = {}
        self.r_modt, = [Res("modt")]
        for g in range(2):
            for idx in idxs:
                t, _ = self.alloc(D, F32, "mod%d_%d" % (g, idx))
                self.dma(t, self.scr["MOD"][g][:, idx * D:(idx + 1) * D], writes=[self.r_modt], key="modld")
                self.modt[(g, idx)] = t

    def mod(self, g, idx):
        return self.modt[(g, idx)]

    def norm_mod_T(self, xt, rx, g, gi, si, dst_fn, bufs):
        junk, rj = bufs["junk"].next()
        ssb, rss = bufs["ss"].next()
        t, rt = bufs["t"].next()
        hb, rhb = bufs["hb"].next()
        self.act(junk, xt, AF.Square, reads=[rx], writes=[rj, rss], accum_out=ssb[:, 0:1])
        self.rstd(ssb[:, 2:3], ssb[:, 0:1], 1.0 / D, reads=[rss], writes=[rss], tmp=ssb[:, 1:2])
        self.stt(t, xt, ssb[:, 2:3], self.mod(g, gi), ALU.mult, ALU.mult, reads=[rx, rss, self.r_modt], writes=[rt])
        self.tt(hb, t, self.mod(g, si), ALU.add, reads=[rt, self.r_modt], writes=[rhb], eng="pool")
        yield
        pb = bufs["pt"].next()
        pv = self.psb_bf(pb)
        for k in range(8):
            self.tr(pv[:, k * 128:(k + 1) * 128], hb[:, k * 128:(k + 1) * 128], self.identb, reads=[rhb], writes=[self.r_ps[pb]])
        yield
        dst_fn(pv, self.r_ps[pb])

    def phase_B(self, l):
        self.A.push()
        self.load_mod((0, 1))
        xr = self.rot(5, D, F32, "xt")
        bufs = {"junk": self.rot(2, D, BF16, "junk"), "ss": self.rot(4, 4, F32, "ss"), "t": self.rot(2, D, F32, "t"),
                "hb": self.rot(3, D, BF16, "hb"), "pt": Rot([5, 6, 7])}
        pipe = Pipe()
        hT3 = _h(self.hT, "p (k t) -> p k t", k=8)
        loaded = {}

        def load(ti):
            xt, rx = xr.next()
            src = self.tile_rows(ti) if l == 0 else self.scr["XRES"][ti * 128:(ti + 1) * 128, :]
            self.dma(xt, src, reads=[self.r_xres[ti]], writes=[rx], key=("xt", id(rx)))
            loaded[ti] = (xt, rx)

        load(0)
        load(1)
        load(2)
        for ti in range(NT):
            if ti + 3 < NT:
                load(ti + 3)
            xt, rx = loaded.pop(ti)
            g = 0 if ti < NTS else 1

            def dst(pv, rp, ti=ti):
                self.cp(hT3[:, :, ti * 128:(ti + 1) * 128], _h(pv, "p (k t) -> p k t", k=8), reads=[rp],
                        writes=[self.r_hT[ti]], eng="act")
            pipe.push(self.norm_mod_T(xt, rx, g, 1, 0, dst, bufs))
        pipe.flush()
        if "HT" in self.debug and l == 0:
            self.dma(_h(self.scr["HT"], "(k p) t -> p k t", p=128), hT3, reads=self.r_hT, key="dbg")
        self.P.barrier()
        self.A.pop()

    def phase_C(self, l):
        d = self.din
        S = self.scr
        rc = self.r_const
        self.A.push()
        wsl = self.rot(2, 8 * 512, BF16, "wsl")
        brs = self.rot(2, 512, F32, "brep")
        t1 = self.rot(10, 512, F32, "t1")
        t2 = self.rot(3, 512, F32, "t2")
        t3 = self.rot(3, 512, F32, "t3")
        junkr = self.rot(3, 512, F32, "junkc")
        ssm = self.rot(5, 32, F32, "ssm")
        qb = self.rot(3, 512, BF16, "qb")
        stT = self.rot(3, 512, BF16, "stT")
        stv = self.rot(3, 516, BF16, "stv")
        stg = self.rot(3, 512, F32, "stg")
        stgb = self.rot(3, 512, BF16, "stgb")
        stf = self.rot(3, 512, BF16, "stf")
        cs_r = self.rot(6, 128, F32, "cossin")
        gq_rep, rgq = self.alloc(64, F32, "gq")
        gk_rep, rgk = self.alloc(64, F32, "gk")
        bcol, rbc = self.alloc(72, F32, "bcol")
        mmr = Rot([0, 1])
        mmf = Rot([2, 3])
        wslf = self.rot(2, 8 * 512, BF16, "wslf")
        ptr = Rot([4, 5, 6])
        hT3 = _h(self.hT, "p (k t) -> p k t", k=8)
        IF3 = _h(self.IF, "p (c j) -> p c j", j=16)

        self.dma(gq_rep, d["gqa_q_g"][l:l + 1, :].broadcast_to([128, 64]), writes=[rgq], key="gq")
        self.ts(gq_rep, gq_rep, 0.125, ALU.mult, reads=[rgq], writes=[rgq])
        self.dma(gk_rep, d["gqa_k_g"][l:l + 1, :].broadcast_to([128, 64]), writes=[rgk], key="gk")
        self.dma(bcol[:, 0:40], _h(d["b_in"][l, 0:5120], "(j p) -> p j", p=128), writes=[rbc], key="bcol", slow=True)
        self.dma(bcol[:, 40:72], _h(d["b_in"][l, O_GATES:O_GATES + 4096], "(j p) -> p j", p=128), writes=[rbc], key="bcol", slow=True)

        def load_slab(c0, width):
            w, rw = wsl.next()
            self.dma(_h(w, "p (k c) -> p k c", k=8)[:, :, 0:width], _h(d["w_in"][l][:, c0:c0 + width], "(k p) c -> p k c", p=128),
                     writes=[rw], key=("wsl", id(rw)), q="pool")
            return w, rw

        def load_bias(c0, width, scale=None):
            b, rb = brs.next()
            self.dma(b[:, 0:width], d["b_in"][l:l + 1, c0:c0 + width].broadcast_to([128, width]), writes=[rb], key=("brep", id(rb)))
            if scale is not None:
                self.ts(b[:, 0:width], b[:, 0:width], scale, ALU.mult, reads=[rb], writes=[rb])
            return b, rb

        def load_cs(ti):
            t, r = cs_r.next()
            self.dma(t[:, 0:64], d["rope_cos"][ti * 128:(ti + 1) * 128, :], writes=[r], key=("cs", id(r)))
            self.dma(t[:, 64:128], d["rope_sin"][ti * 128:(ti + 1) * 128, :], writes=[r], key=("cs", id(r)))
            return t, r

        def rope(out, x, rx, H, cs, rcs, ro):
            W = H * 64
            a, ra = t2.next()
            b, rb = t3.next()
            x3 = _h(x[:, 0:W], "p (h e) -> p h e", h=H)
            cosb = cs[:, 0:64].unsqueeze(1).broadcast_to([128, H, 64])
            self.tt(_h(a[:, 0:W], "p (h e) -> p h e", h=H), x3, cosb, ALU.mult, reads=[rx, rcs], writes=[ra])
            x5 = _h(x[:, 0:W], "p (h f x j) -> p h f x j", h=H, f=2, x=2)
            b5 = _h(b[:, 0:W], "p (h f x j) -> p h f x j", h=H, f=2, x=2)
            s4 = _h(cs[:, 64:128], "p (f x j) -> p f x j", f=2, x=2)
            for xi in range(2):
                sb_ = s4[:, :, xi, :].unsqueeze(1).broadcast_to([128, H, 2, 16])
                self.tt(b5[:, :, :, xi, :], x5[:, :, :, 1 - xi, :], sb_, ALU.mult, reads=[rx, rcs], writes=[rb], eng="pool")
            self.tt(out[:, 0:W], a[:, 0:W], b[:, 0:W], ALU.add, reads=[ra, rb], writes=[ro])

        def headnorm_g(out, ro, t, rt, H, grep, rg):
            W = H * 64
            ss, rss = ssm.next()
            jk, rjk = junkr.next()
            self.act(jk[:, 0:W], t[:, 0:W], AF.Square, reads=[rt], writes=[rjk])
            yield
            self.red(ss[:, 0:H], _h(jk[:, 0:W], "p (h e) -> p h e", h=H), ALU.add, reads=[rjk], writes=[rss])
            self.rstd(ss[:, 16:16 + H], ss[:, 0:H], 1.0 / 64, reads=[rss], writes=[rss], tmp=ss[:, 8:8 + H])
            yield
            o3 = _h(out[:, 0:W], "p (h e) -> p h e", h=H)
            self.tt(o3, _h(t[:, 0:W], "p (h e) -> p h e", h=H), ss[:, 16:16 + H].unsqueeze(2).broadcast_to([128, H, 64]), ALU.mult,
                    reads=[rt, rss], writes=[ro])
            self.tt(o3, o3, grep.unsqueeze(1).broadcast_to([128, H, 64]), ALU.mult, reads=[ro, rg], writes=[ro])

        def to_fm_g(src, rsrc, W, dst, ti):
            nj = W // 128
            pb = ptr.next()
            pv = self.psb_bf(pb)
            for j in range(nj):
                self.tr(pv[:, j * 128:(j + 1) * 128], src[:, j * 128:(j + 1) * 128], self.identb, reads=[rsrc], writes=[self.r_ps[pb]])
            yield
            sg, rsg = stT.next()
            self.cp(sg[:, 0:W], pv[:, 0:W], reads=[self.r_ps[pb]], writes=[rsg], eng="act")
            self.dma(_h(dst[0:W, ti * 128:(ti + 1) * 128], "(j p) t -> p j t", p=128), _h(sg[:, 0:W], "p (j t) -> p j t", j=nj),
                     reads=[rsg], key=("stT", id(rsg)), q="act")

        def prompt_out(name, kv, ti, c0, width, src, rsrc):
            sq = (ti - NTS) // 2
            p0 = ((ti - NTS) % 2) * 128
            self.dma(self.dout[name][sq, l, kv, p0:p0 + 128, c0:c0 + width], src, reads=[rsrc], key=("po", id(rsrc)))

        def h_na_v(ps, rp, ti, b, rb):
            sv, rsv = stv.next()
            self.tt(sv[:, 0:512], ps, b, ALU.add, reads=[rp, rb], writes=[rsv])
            if ti >= NTS:
                f, rf = t1.next()
                self.tt(f, ps, b, ALU.add, reads=[rp, rb], writes=[rf])
            yield
            self.dma(S["V_na"][ti * 128:(ti + 1) * 128, :], sv[:, 0:512], reads=[rsv], key=("stv", id(rsv)))
            if ti >= NTS:
                prompt_out("o_na", 1, ti, 0, 512, f, rf)

        def h_na_k_prompt(ps, rp, ti, b, rb):
            f, rf = t1.next()
            self.tt(f, ps, b, ALU.add, reads=[rp, rb], writes=[rf])
            yield
            prompt_out("o_na", 0, ti, 0, 512, f, rf)

        def h_g_q(ps, rp, ti, b, rb):
            t, rt = t1.next()
            self.tt(t, ps, b, ALU.add, reads=[rp, rb], writes=[rt])
            if ti < NTS:
                cs, rcs = load_cs(ti)
            n, rn = t1.next()
            yield from headnorm_g(n, rn, t, rt, 8, gq_rep, rgq)
            q, rq = qb.next()
            if ti < NTS:
                rope(q, n, rn, 8, cs, rcs, rq)
            else:
                self.cp(q, n, reads=[rn], writes=[rq])
            yield
            yield from to_fm_g(q, rq, 512, S["QT_g"], ti)

        def kv_common(ps, rp, ti, b, rb, normed, kt_name, v_name, oname):
            t, rt = t1.next()
            self.tt(t[:, 0:256], ps, b[:, 0:256], ALU.add, reads=[rp, rb], writes=[rt])
            if ti < NTS:
                cs, rcs = load_cs(ti)
            sv, rsv = stv.next()
            self.cp(sv[:, 0:128], t[:, 128:256], reads=[rt], writes=[rsv], eng="pool")
            if normed:
                n, rn = t1.next()
                yield from headnorm_g(n, rn, t, rt, 2, gk_rep, rgk)
            else:
                n, rn = t, rt
                yield
            self.dma(S[v_name][ti * 128:(ti + 1) * 128, :], sv[:, 0:128], reads=[rsv], key=("stv", id(rsv)))
            q, rq = qb.next()
            if ti < NTS:
                rope(q, n, rn, 2, cs, rcs, rq)
            else:
                prompt_out(oname, 0, ti, 0, 128, n[:, 0:128], rn)
                prompt_out(oname, 1, ti, 0, 128, t[:, 128:256], rt)
                self.cp(q[:, 0:128], n[:, 0:128], reads=[rn], writes=[rq])
            yield
            yield from to_fm_g(q, rq, 128, S[kt_name], ti)

        def h_g_kv(ps, rp, ti, b, rb):
            yield from kv_common(ps, rp, ti, b, rb, True, "KT_g", "V_g", "o_g")

        def h_s_kv(ps, rp, ti, b, rb):
            yield from kv_common(ps, rp, ti, b, rb, False, "KT_s", "V_s", "o_s")

        def h_s_q(ps, rp, ti, b, rb):
            t, rt = t1.next()
            self.stt(t, ps, 0.125, b, ALU.mult, ALU.add, reads=[rp, rb], writes=[rt])
            if ti < NTS:
                cs, rcs = load_cs(ti)
            yield
            q, rq = qb.next()
            if ti < NTS:
                rope(q, t, rt, 8, cs, rcs, rq)
            else:
                self.cp(q, t, reads=[rt], writes=[rq])
            yield
            yield from to_fm_g(q, rq, 512, S["QT_s"], ti)

        def h_m_k(ps, rp, ti, b, rb):
            f, rf = stf.next()
            self.tt(f, ps, b, ALU.add, reads=[rp, rb], writes=[rf])
            yield
            self.dma(S["K_m"][ti * 128:(ti + 1) * 128, :], f, reads=[rf], key=("stf", id(rf)))

        def h_m_v(ps, rp, ti, b, rb):
            sv, rsv = stv.next()
            s3 = _h(sv[:, 0:516], "p (h e) -> p h e", h=4)
            self.tt(s3[:, :, 0:128], _h(ps, "p (h e) -> p h e", h=4), _h(b, "p (h e) -> p h e", h=4), ALU.add,
                    reads=[rp, rb], writes=[rsv])
            self.memset(s3[:, :, 128:129], 1.0, writes=[rsv], eng="pool")
            yield
            self.dma(S["V_m"][ti * 128:(ti + 1) * 128, :], sv[:, 0:516], reads=[rsv], key=("stv", id(rsv)))

        def h_m_o(ps, rp, ti, b, rb):
            t, rt = t1.next()
            self.tt(t, ps, b, ALU.add, reads=[rp, rb], writes=[rt])
            yield
            g, rg = stg.next()
            self.act(g, t, AF.Sigmoid, reads=[rt], writes=[rg])
            yield
            self.dma(S["O_m"][ti * 128:(ti + 1) * 128, :], g, reads=[rg], key=("stg", id(rg)), q="act")

        def h_m_if(ps, rp, ti, b, rb):
            self.tt(IF3[:, ti, :], ps, b[:, 0:16], ALU.add, reads=[rp, rb], writes=[self.r_IF])
            yield

        fm_groups = [(O_NAQ, 4, "q8", S["QT_na"], 0), (O_NAK, 4, "k", S["KT_na"], 4),
                     (O_MQ, 4, "k", S["QT_m"], 24), (O_MK, 4, "k", S["KT_m"], 28)]
        for i in range(8):
            fm_groups.append((O_GATES + i * 512, 4, "gate", S["GT"][i * 512:(i + 1) * 512, :], 40 + i * 4))
        onlyf = getattr(self, "only_fm", None)

        def fm_stream():
            for gi, (c0, nb, kind, dst, bc0) in enumerate(fm_groups):
                if onlyf is not None and gi not in onlyf:
                    continue
                w, rw = wslf.next()
                self.dma(_h(w, "p (k c) -> p k c", k=8), _h(d["w_in"][l][:, c0:c0 + 512], "(k p) c -> p k c", p=128),
                         writes=[rw], key=("wslf", id(rw)), q="pool")
                for cbl in range(nb):
                    bc = bcol[:, bc0 + cbl:bc0 + cbl + 1]
                    for tc in range(TOK // 512):
                        pb = mmf.next()
                        ps = self.psb(pb)
                        for k in range(8):
                            self.mm(ps, w[:, k * 512 + cbl * 128:k * 512 + (cbl + 1) * 128], hT3[:, k, tc * 512:(tc + 1) * 512],
                                    k == 0, k == 7, reads=[rw] + self.r_hT[tc * 4:(tc + 1) * 4], writes=[self.r_ps[pb]])
                        dsl = dst[cbl * 128:(cbl + 1) * 128, tc * 512:(tc + 1) * 512]
                        if kind == "gate":
                            g, rg = stgb.next()
                            self.act(g, ps, AF.Sigmoid, reads=[self.r_ps[pb], rbc], writes=[rg], bias=bc)
                            self.dma(dsl, g, reads=[rg], key=("stgb", id(rg)), q="act")
                        else:
                            f, rf = stf.next()
                            if kind == "q8":
                                self.ts(f, ps, bc, ALU.add, 0.125, ALU.mult, reads=[self.r_ps[pb], rbc], writes=[rf])
                            else:
                                self.ts(f, ps, bc, ALU.add, reads=[self.r_ps[pb], rbc], writes=[rf])
                            self.dma(dsl, f, reads=[rf], key=("stf", id(rf)))
                        yield

        all_t = list(range(NT))
        tm_groups = [
            (O_NAV, 512, h_na_v, all_t, None),
            (O_GQ, 512, h_g_q, all_t, None),
            (O_GK, 256, h_g_kv, all_t, None),
            (O_SQ, 512, h_s_q, all_t, 0.125),
            (O_SK, 256, h_s_kv, all_t, None),
            (O_MK, 512, h_m_k, all_t, None),
            (O_MV, 512, h_m_v, all_t, None),
            (O_MO, 512, h_m_o, all_t, None),
            (O_MI, 16, h_m_if, all_t, None),
            (O_NAK, 512, h_na_k_prompt, list(range(NTS, NT)), None),
        ]
        only = getattr(self, "only_groups", None)
        fm = fm_stream()
        n_tm = 0
        for gi, (c0, width, handler, tiles, bscale) in enumerate(tm_groups):
            if only is not None and gi not in only:
                continue
            w, rw = load_slab(c0, width)
            b, rb = load_bias(c0, width, bscale)
            if gi == 0:
                self.warm(reads=[rw, rb])
            pipe = Pipe()
            for ti in tiles:
                pb = mmr.next()
                ps = self.psb(pb)[:, 0:width]
                for k in range(8):
                    self.mm(ps, hT3[:, k, ti * 128:(ti + 1) * 128], w[:, k * 512:k * 512 + width], k == 0, k == 7,
                            reads=[rw, self.r_hT[ti]], writes=[self.r_ps[pb]])
                pipe.push(handler(ps, self.r_ps[pb], ti, b, rb))
                n_tm += 1
                for _ in range(2 if n_tm % 3 == 0 else 1):
                    next(fm, None)
            pipe.flush()

        if "IFD" in self.debug and l == 0:
            self.dma(S["IFD"], self.IF, reads=[self.r_IF], key="dbg")

        for _ in fm:
            pass
        self.A.pop()


def _core_inputs(inputs, core, consts, nab):
    f = np.ascontiguousarray
    m = {
        "xs": f(inputs["x_sample"][core]),
        "xp": f(inputs["x_prompt"][2 * core:2 * core + 2].reshape(512, D)),
        "cna": f(inputs["cache_na_kv"][core].reshape(2, 2, 256, 512)),
        "cg": f(inputs["cache_gqa_kv"][core].reshape(2, 2, 256, 128)),
        "cs": f(inputs["cache_swa_kv"][core].reshape(2, 2, 256, 128)),
        "stC": f(inputs["state_mlstm_C"][core]),
        "stn": f(inputs["state_mlstm_n"][core]),
        "stm": f(inputs["state_mlstm_m"][core].reshape(2, 8)),
        "cvec": f(np.stack([inputs["c"][core], inputs["c_ctx"]], 0)),
        "final_norm_g": f(inputs["final_norm_g"].reshape(1, D)),
        "nabias": nab,
    }
    for k in ("w_mod", "b_mod", "norm1_g", "norm2_g", "w_in", "b_in", "gqa_q_g", "gqa_k_g", "swa_sink",
              "mlstm_norm_g", "w_branch", "w_out", "w_ffn_in", "w_ffn_out"):
        m[k] = f(inputs[k])
    m.update(consts)
    return m


def _na_bias_tiles(na_rpb):
    gi = _na_gather_indices()
    out = np.empty((2, 8, NA_NB, 128, 128), np.float32)
    for tid, (dr, dc, valid) in gi.items():
        g = na_rpb[:, :, dr, dc]
        out[:, :, tid] = np.where(valid[None, None], g, np.float32(NEG))
    return out


def _attn_setup(self):
    self.at_srot = Rot([(0, 1), (2, 3)])
    self.at_obanks = (4, 5)
    self.at_pt = self.rot(3, 1024, BF16, "pt")
    self.at_osb = self.rot(3, 1024, F32, "osb")
    self.at_rec = self.rot(4, 512, F32, "rec")
    self.at_tmp = self.rot(4, 512, F32, "dtmp")
    self.at_pending = []


def _attn_job(self, *a, **kw):
    for _ in _attn_job_gen(self, *a, **kw):
        pass


def _attn_job_gen(self, kt, rkt, qt, rqt, nq, kbl, stage, rstage, sinks=None, rsink=None, filler=0, act_recip=True, after=None,
                  obanks=None, ptrot=None):
    oA, oB = obanks if obanks is not None else self.at_obanks
    psoA, psoB = self.psb(oA), self.psb(oB)
    nk = len(kbl)
    pend = None

    def pv(item):
        i, ptile, rpt = item
        kb = kbl[i]
        qlo, qhi = kb["qlo"], kb["qhi"]
        self.mm(psoA[:, qlo:qhi], kb["vx"][0], ptile[:, qlo:qhi], i == 0, i == nk - 1, reads=[rpt, kb["rvx"]], writes=[self.r_ps[oA]])
        self.mm(psoB[:, qlo:qhi], kb["vx"][1], ptile[:, 512 + qlo:512 + qhi], i == 0, i == nk - 1, reads=[rpt, kb["rvx"]],
                writes=[self.r_ps[oB]])
        if filler:
            self.warm(n=filler, bank=6)

    for i, kb in enumerate(kbl):
        banks = self.at_srot.next()
        ptile, rpt = (ptrot or self.at_pt).next()
        qlo, qhi = kb["qlo"], kb["qhi"]
        exs = kb.get("extras", ((), ()))
        for hh in range(2):
            pss = self.psb(banks[hh])
            self.mm(pss[:, qlo:qhi], kt[hh * 64:hh * 64 + 64, kb["kcol"]:kb["kcol"] + 128], qt[hh * 64:hh * 64 + 64, qlo:qhi],
                    True, len(exs[hh]) == 0, reads=[rkt, rqt], writes=[self.r_ps[banks[hh]]])
        for hh in range(2):
            pss = self.psb(banks[hh])
            ex = exs[hh]
            for ei, (col, tile, rtile) in enumerate(ex):
                self.mm(pss[:, col:col + tile.shape[1]], self.identb, tile, False, ei == len(ex) - 1,
                        reads=[rtile], writes=[self.r_ps[banks[hh]]], skip_group_check=True)
        if qlo == 0 and qhi == 512:
            self.act(ptile[:, 0:1024], self.ps_t[:, banks[0]:banks[0] + 2, :], AF.Exp,
                     reads=[self.r_ps[banks[0]], self.r_ps[banks[1]]], writes=[rpt])
        else:
            self.act(_h(ptile, "p (b q) -> p b q", b=2)[:, :, qlo:qhi], self.ps_t[:, banks[0]:banks[0] + 2, qlo:qhi], AF.Exp,
                     reads=[self.r_ps[banks[0]], self.r_ps[banks[1]]], writes=[rpt])
        if pend is not None:
            pv(pend)
        pend = (i, ptile, rpt)
        if i == min(1, nk - 2):
            while self.at_pending:
                self.at_pending.pop(0)()
        yield
    pv(pend)
    osb, rosb = self.at_osb.next()
    self.cp(osb[:, 0:nq], psoA[:, 0:nq], reads=[self.r_ps[oA]], writes=[rosb])
    self.cp(osb[:, 512:512 + nq], psoB[:, 0:nq], reads=[self.r_ps[oB]], writes=[rosb])

    def fin():
      for hh in range(2):
          o_ = osb[:, hh * 512:hh * 512 + nq]
          rec, rrec = self.at_rec.next()
          if sinks is not None:
              tmp, rtmp = self.at_tmp.next()
              self.ts(tmp[64:128, 0:nq], o_[64:128, :], sinks[hh][64:128, :], ALU.add, reads=[rosb, rsink], writes=[rtmp])
              den = tmp[64:128, 0:nq]
              rden = rtmp
          else:
              den = o_[64:128, :]
              rden = rosb
          if act_recip:
              self.act(rec[0:64, 0:nq], den, AF.Ln, reads=[rden], writes=[rrec])
              self.act(rec[0:64, 0:nq], rec[0:64, 0:nq], AF.Exp, reads=[rrec], writes=[rrec], scale=-1.0)
          else:
              self.recip(rec[0:64, 0:nq], den, reads=[rden], writes=[rrec])
          self.tt(stage[hh * 64:hh * 64 + 64, 0:nq], o_[0:64, :], rec[0:64, 0:nq], ALU.mult, reads=[rosb, rrec], writes=[rstage])
      if after is not None:
          after()
    self.at_pending.append(fin)


def _load_ctx_kT(self, src, ncols, dsts, l):
    tmpb, rtb = self.at_ctxb.next()
    self.dma(_h(tmpb[:, 0:256], "p (b e) -> p b e", b=2), _h(src, "(b p) e -> p b e", p=128), writes=[rtb],
             key=("ctxb", id(rtb)), q="pool")
    pb_ = self.at_ptr.next()
    pv = self.psb_bf(pb_)
    for b in range(2):
        self.tr(pv[:, b * 128:(b + 1) * 128], tmpb[:, b * 128:(b + 1) * 128], self.identb, reads=[rtb], writes=[self.r_ps[pb_]])
    for (dst, rdst, moves) in dsts:
        for (dlo, slo) in moves:
            self.cp(_h(dst[dlo:dlo + 64, 0:256], "p (b t) -> p b t", b=2), _h(pv[slo:slo + 64, 0:256], "p (b t) -> p b t", b=2),
                    reads=[self.r_ps[pb_]], writes=[rdst], eng="act")


def _phase_D(self, l):
    d = self.din
    S = self.scr
    self.A.push()
    _attn_setup(self)
    self.at_ctxb = self.rot(2, 256, BF16, "ctxb")
    self.at_ptr = Rot([6, 7])
    sinkexp, rsk = self.alloc(8, F32, "sinkexp")
    self.dma(sinkexp, d["swa_sink"][l:l + 1, :].broadcast_to([128, 8]), writes=[rsk], key="sink")
    self.act(sinkexp, sinkexp, AF.Exp, reads=[rsk], writes=[rsk])
    which = getattr(self, "only_attn", ("prompt", "gqa", "swa", "na"))

    def prompt_stream(obanks, ptrot, act_recip):
        ktp = self.rot(3, 256, BF16, "ktp")
        qtp = self.rot(3, 256, BF16, "qtp")
        vxp = self.rot(3, 2 * 2 * 128, BF16, "vxp")
        for (vx, rvx) in vxp.items:
            self.memset(vx, 1.0, writes=[rvx])
        stg = self.rot(3, 256, BF16, "stgp")
        for sq in range(2):
            c0 = 4096 + sq * 256
            for (bi, qn, kn, vn, nkv) in ((0, "QT_na", "KT_na", "V_na", 8), (1, "QT_g", "KT_g", "V_g", 2), (2, "QT_s", "KT_s", "V_s", 2)):
                for hp in range(4):
                    qt, rqt = qtp.next()
                    self.dma(qt, S[qn][hp * 128:(hp + 1) * 128, c0:c0 + 256], writes=[rqt], key=("qtp", id(rqt)))
                    kt, rkt = ktp.next()
                    vx, rvx = vxp.next()
                    vx4 = _h(vx, "p (b h e) -> p b h e", b=2, h=2)
                    if nkv == 8:
                        self.dma(kt, S[kn][hp * 128:(hp + 1) * 128, c0:c0 + 256], writes=[rkt], key=("ktp", id(rkt)))
                        for hh in range(2):
                            self.dma(vx4[:, :, hh, 0:64], _h(S[vn][c0:c0 + 256, hp * 128 + hh * 64:hp * 128 + (hh + 1) * 64], "(b p) e -> p b e", p=128),
                                     writes=[rvx], key=("vxp", id(rvx)))
                    else:
                        j = hp // 2
                        for half in range(2):
                            self.dma(kt[half * 64:(half + 1) * 64, :], S[kn][j * 64:(j + 1) * 64, c0:c0 + 256], writes=[rkt], key=("ktp", id(rkt)))
                            self.dma(vx4[:, :, half, 0:64], _h(S[vn][c0:c0 + 256, j * 64:(j + 1) * 64], "(b p) e -> p b e", p=128),
                                     writes=[rvx], key=("vxp", id(rvx)))
                    st_, rst = stg.next()
                    kbl = [dict(kcol=kb * 128, vx=(vx4[:, kb, 0, :], vx4[:, kb, 1, :]), rvx=rvx, qlo=0, qhi=256) for kb in range(2)]

                    def after(st_=st_, rst=rst, bi=bi, hp=hp, c0=c0):
                        self.dma(S["OT"][bi, hp * 128:(hp + 1) * 128, c0:c0 + 256], st_, reads=[rst], key=("stgp", id(rst)))
                    sk = (sinkexp[:, 2 * hp:2 * hp + 1], sinkexp[:, 2 * hp + 1:2 * hp + 2]) if bi == 2 else None
                    yield from _attn_job_gen(self, kt, rkt, qt, rqt, 256, kbl, st_, rst, sinks=sk, rsink=rsk, after=after,
                                             obanks=obanks, ptrot=ptrot, act_recip=act_recip)

    if "prompt" in which and "gqa" not in which:
        self.A.push()
        self.warm(bank=6)
        for _ in prompt_stream(None, None, True):
            pass
        while self.at_pending:
            self.at_pending.pop(0)()
        self.P.barrier()
        self.A.pop()

    if "gqa" in which or "swa" in which:
        self.A.push()
        mix = {}
        for (name, bi, qn, kn, vn, cn) in (("gqa", 1, "QT_g", "KT_g", "V_g", "cg"), ("swa", 2, "QT_s", "KT_s", "V_s", "cs")):
            if name not in which:
                continue
            ktd = [self.alloc(4352, BF16, "ktd%d" % j) for j in range(2)]
            vxs = [self.alloc(34 * 128, BF16, "vx%d" % j) for j in range(2)]
            qts = self.rot(2, 4096, BF16, "qts")
            stg = self.rot(2, 4096, BF16, "stgs")
            _load_ctx_kT(self, d[cn][l, 0], 128, [(ktd[0][0], ktd[0][1], [(0, 0), (64, 0)]), (ktd[1][0], ktd[1][1], [(0, 64), (64, 64)])], l)
            for j in range(2):
                kt, rkt = ktd[j]
                vx, rvx = vxs[j]
                for half in range(2):
                    self.dma(kt[half * 64:(half + 1) * 64, 256:4352], S[kn][j * 64:(j + 1) * 64, 0:4096], writes=[rkt], key=("ktd", id(rkt)))
                self.memset(vx, 1.0, writes=[rvx])
                vx3 = _h(vx, "p (b e) -> p b e", b=34)
                self.dma(vx3[:, 0:2, 0:64], _h(d[cn][l, 1][:, j * 64:(j + 1) * 64], "(b p) e -> p b e", p=128), writes=[rvx],
                         key=("vxs", id(rvx)), q="pool")
                for q4 in range(4):
                    self.dma(vx3[:, 2 + q4 * 8:2 + (q4 + 1) * 8, 0:64],
                             _h(S[vn][q4 * 1024:(q4 + 1) * 1024, j * 64:(j + 1) * 64], "(b p) e -> p b e", p=128), writes=[rvx], key=("vxs", id(rvx)))
            mix[name] = (bi, qn, ktd, vxs, qts, stg)

        ptrots = {"gqa": self.at_pt, "swa": self.rot(3, 1024, BF16, "pt_swa")}

        def stream(name, obanks, filler):
            bi, qn, ktd, vxs, qts, stg = mix[name]
            for hp in range(4):
                j = hp // 2
                kt, rkt = ktd[j]
                vx, rvx = vxs[j]
                vx3 = _h(vx, "p (b e) -> p b e", b=34)
                qt, rqt = qts.next()
                self.dma(qt, S[qn][hp * 128:(hp + 1) * 128, 0:4096], writes=[rqt], key=("qts", id(rqt)))
                st_, rst = stg.next()
                if name == "gqa":
                    self.warm(reads=[rkt, rqt, rvx], bank=6 if "swa" not in mix else obanks[0])
                for qc in range(8):
                    if name == "gqa":
                        kbl = [dict(kcol=kb * 128, vx=(vx3[:, kb, :], vx3[:, kb, :]), rvx=rvx, qlo=0, qhi=512) for kb in range(34)]
                        sk = None
                    else:
                        kbl = [dict(kcol=0, vx=(vx3[:, 0, :], vx3[:, 0, :]), rvx=rvx, qlo=0, qhi=512)]
                        for jb in range(4 * qc - 1, 4 * qc + 5):
                            if jb < 0 or jb > 31:
                                continue
                            qbs = [qb for qb in (jb - 1, jb, jb + 1) if 4 * qc <= qb <= 4 * qc + 3]
                            ex = []
                            if jb + 1 in qbs:
                                ex.append(((jb + 1 - 4 * qc) * 128, self.swam[0], self.r_const))
                            if jb - 1 in qbs:
                                ex.append(((jb - 1 - 4 * qc) * 128, self.swam[1], self.r_const))
                            kbl.append(dict(kcol=256 + jb * 128, vx=(vx3[:, 2 + jb, :], vx3[:, 2 + jb, :]), rvx=rvx,
                                            qlo=(min(qbs) - 4 * qc) * 128, qhi=(max(qbs) + 1 - 4 * qc) * 128, extras=(ex, ex)))
                        kbl.append(dict(kcol=128, vx=(vx3[:, 1, :], vx3[:, 1, :]), rvx=rvx, qlo=0, qhi=512))
                        sk = (sinkexp[:, 2 * hp:2 * hp + 1], sinkexp[:, 2 * hp + 1:2 * hp + 2])
                    aft = None
                    if qc == 7:
                        def aft(st_=st_, rst=rst, hp=hp, bi=bi):
                            self.dma(S["OT"][bi, hp * 128:(hp + 1) * 128, 0:4096], st_, reads=[rst], key=("stgs", id(rst)))
                    yield from _attn_job_gen(self, kt, rkt, qt[:, qc * 512:(qc + 1) * 512], rqt, 512, kbl,
                                             st_[:, qc * 512:(qc + 1) * 512], rst, sinks=sk, rsink=rsk, filler=filler,
                                             act_recip=False, after=aft, obanks=obanks, ptrot=ptrots[name])

        if "gqa" in mix and "swa" in mix:
            g_ = stream("gqa", (4, 5), getattr(self, "mix_filler", 0))

            def second():
                if "prompt" in which:
                    yield from prompt_stream((6, 7), ptrots["swa"], False)
                yield from stream("swa", (6, 7), 0)
            w_ = second()
            n_g = 0
            ratio = getattr(self, "mix_ratio", 3 if "prompt" in which else 4)
            w_alive = True
            for _ in g_:
                n_g += 1
                if w_alive and n_g % ratio == 0:
                    if next(w_, "done") == "done":
                        w_alive = False
            if w_alive:
                for _ in w_:
                    pass
        else:
            nm = "gqa" if "gqa" in mix else "swa"
            for _ in stream(nm, (4, 5), 1 if nm == "gqa" else 0):
                pass
        while self.at_pending:
            self.at_pending.pop(0)()
        self.P.barrier()
        self.A.pop()

    if "na" in which:
        self.A.push()
        ktn = self.rot(2, 4352, BF16, "ktn")
        qtn = self.rot(2, 4096, BF16, "qtn")
        vxn = self.rot(2, 34 * 2 * 128, BF16, "vxn")
        for (vx, rvx) in vxn.items:
            self.memset(vx, 1.0, writes=[rvx])
        bia = self.rot(2, 2 * NA_NB * 128, BF16, "nabias")
        stg = self.rot(2, 4096, BF16, "stgn")
        for hp in range(4):
            kt, rkt = ktn.next()
            qt, rqt = qtn.next()
            vx, rvx = vxn.next()
            bt, rbt = bia.next()
            vx4 = _h(vx, "p (b h e) -> p b h e", b=34, h=2)
            bt4 = _h(bt, "p (h n q) -> p h n q", h=2, n=NA_NB)
            _load_ctx_kT(self, d["cna"][l, 0][:, hp * 128:(hp + 1) * 128], 128, [(kt, rkt, [(0, 0), (64, 64)])], l)
            self.dma(kt[:, 256:4352], S["KT_na"][hp * 128:(hp + 1) * 128, 0:4096], writes=[rkt], key=("ktn", id(rkt)))
            self.dma(qt, S["QT_na"][hp * 128:(hp + 1) * 128, 0:4096], writes=[rqt], key=("qtn", id(rqt)))
            for hh in range(2):
                c_lo = hp * 128 + hh * 64
                self.dma(vx4[:, 0:2, hh, 0:64], _h(d["cna"][l, 1][:, c_lo:c_lo + 64], "(b p) e -> p b e", p=128),
                         writes=[rvx], key=("vxn", id(rvx)), q="pool")
                for q4 in range(4):
                    self.dma(vx4[:, 2 + q4 * 8:2 + (q4 + 1) * 8, hh, 0:64],
                             _h(S["V_na"][q4 * 1024:(q4 + 1) * 1024, c_lo:c_lo + 64], "(b p) e -> p b e", p=128),
                             writes=[rvx], key=("vxn", id(rvx)))
            for hh in range(2):
                self.dma(bt4[:, hh, :, :], _h(d["nabias"][l, hp * 2 + hh], "n k q -> k n q"), writes=[rbt], key=("nab", id(rbt)), q="pool")
            st_, rst = stg.next()
            self.warm(reads=[rkt, rqt, rvx, rbt], bank=6)
            fill = getattr(self, "filler", {"gqa": 1, "swa": 0, "na": 0})
            for qc in range(8):
                tiles = [4 * qc + i for i in range(4)]
                kbl = [dict(kcol=0, vx=(vx4[:, 0, 0, :], vx4[:, 0, 1, :]), rvx=rvx, qlo=0, qhi=512)]
                jbs = sorted(set(jb for a in tiles for jb in _na_kbs(a)))
                for jb in jbs:
                    as_ = [a for a in tiles if jb in _na_kbs(a)]
                    ids = [NA_TILE_IDX[(_na_class(a), jb - a)] for a in as_]
                    runs = []
                    for a, tid in zip(as_, ids):
                        if runs and runs[-1][1] + runs[-1][2] == tid:
                            runs[-1][2] += 1
                        else:
                            runs.append([a, tid, 1])
                    exs = tuple([((a0 - 4 * qc) * 128, _h(bt4[:, hh, t0:t0 + n, :], "p n q -> p (n q)"), rbt) for (a0, t0, n) in runs]
                                for hh in range(2))
                    kbl.append(dict(kcol=256 + jb * 128, vx=(vx4[:, 2 + jb, 0, :], vx4[:, 2 + jb, 1, :]), rvx=rvx,
                                    qlo=(min(as_) - 4 * qc) * 128, qhi=(max(as_) + 1 - 4 * qc) * 128, extras=exs))
                kbl.append(dict(kcol=128, vx=(vx4[:, 1, 0, :], vx4[:, 1, 1, :]), rvx=rvx, qlo=0, qhi=512))
                aft = None
                if qc == 7:
                    def aft(st_=st_, rst=rst, hp=hp):
                        self.dma(S["OT"][0, hp * 128:(hp + 1) * 128, 0:4096], st_, reads=[rst], key=("stgn", id(rst)))
                _attn_job(self, kt, rkt, qt[:, qc * 512:(qc + 1) * 512], rqt, 512, kbl,
                          st_[:, qc * 512:(qc + 1) * 512], rst, filler=fill["na"], after=aft)
        while self.at_pending:
            self.at_pending.pop(0)()
        self.P.barrier()
        self.A.pop()
    self.A.pop()


KB.phase_D = _phase_D


def _phase_E(self, l):
    d = self.din
    S = self.scr
    rc = self.r_const
    self.A.push()
    IF3 = _h(self.IF, "p (c j) -> p c j", j=16)
    NC4 = NT * 4
    lnk, rlnk = self.alloc(1, F32, "lnk")
    self.memset(lnk, -0.5 * math.log(128.0), writes=[rlnk])
    e1, re1 = self.alloc(NT * 8, F32, "e1")
    e13 = _h(e1, "p (c j) -> p c j", j=8)
    self.act(e13, IF3[:, :, 8:16], AF.Exp, reads=[self.r_IF], writes=[re1], scale=-1.0)
    self.act(e1, e1, AF.Ln, reads=[re1], writes=[re1], bias=self.one_col)
    nb, ntot, uu, eb, EG, ww, ug = [], [], [], [], [], [], []
    for dr in range(2):
        lfd, rl = self.alloc(NC4, F32, "lfd")
        self.cp(_h(lfd, "p (c j) -> p c j", j=4), e13[:, :, dr * 4:(dr + 1) * 4], reads=[re1], writes=[rl])
        self.mm(self.psb(dr)[:, 0:NC4], self.tri[dr], lfd, True, True, reads=[rl], writes=[self.r_ps[dr]])
        self.mm(self.psb(2 + dr)[:, 0:NC4], self.tri[2], lfd, True, True, reads=[rl], writes=[self.r_ps[2 + dr]])
        t_nb, r_nb = self.alloc(NC4, F32, "nb")
        t_nt, r_nt = self.alloc(NC4, F32, "ntot")
        self.cp(t_nb, self.psb(dr)[:, 0:NC4], reads=[self.r_ps[dr]], writes=[r_nb])
        self.cp(t_nt, self.psb(2 + dr)[:, 0:NC4], reads=[self.r_ps[2 + dr]], writes=[r_nt])
        dd, rdd = self.alloc(NC4, F32, "dd")
        self.tt(_h(dd, "p (c j) -> p c j", j=4), IF3[:, :, dr * 4:(dr + 1) * 4], _h(t_nb, "p (c j) -> p c j", j=4), ALU.add,
                reads=[self.r_IF, r_nb], writes=[rdd])
        t_u, r_u = self.alloc(NC4, F32, "u")
        self.act(t_u, dd, AF.Exp, reads=[rdd, rlnk], writes=[r_u], bias=lnk)
        t_eb, r_eb = self.alloc(NC4, F32, "eb")
        self.act(t_eb, t_nb, AF.Exp, reads=[r_nb], writes=[r_eb], scale=-1.0)
        t_eg, r_eg = self.alloc(NC4, F32, "EG")
        self.act(t_eg, t_nt, AF.Exp, reads=[r_nt], writes=[r_eg], scale=-1.0)
        t_w, r_w = self.alloc(NC4, F32, "w")
        self.tt(t_w, dd, t_nt, ALU.subtract, reads=[rdd, r_nt], writes=[r_w])
        t_ug, r_ug = self.alloc(NC4, F32, "ug")
        self.tt(t_ug, t_u, t_eg, ALU.mult, reads=[r_u, r_eg], writes=[r_ug])
        ug.append((t_ug, r_ug))
        nb.append((t_nb, r_nb)); ntot.append((t_nt, r_nt)); uu.append((t_u, r_u)); eb.append((t_eb, r_eb))
        EG.append((t_eg, r_eg)); ww.append((t_w, r_w))
    em0, rem0 = self.alloc(8, F32, "em0")
    self.dma(em0, d["stm"][l:l + 1, :].broadcast_to([128, 8]), writes=[rem0], key="em0")
    self.act(em0, em0, AF.Exp, reads=[rem0], writes=[rem0])
    gml, rgml = self.alloc(512, F32, "gml")
    self.dma(gml, d["mlstm_norm_g"][l:l + 1, :].broadcast_to([128, 512]), writes=[rgml], key="gml")
    self.P.barrier()

    Cst, rC = self.alloc(4 * 129, F32, "Cst")
    Cn, rCn = self.alloc(4 * 129, F32, "Cn")
    Cbf, rCb = self.alloc(4 * 129, BF16, "Cbf")
    qTr = self.rot(4, 512, BF16, "qTc")
    kTr = self.rot(4, 512, BF16, "kTc")
    kr = self.rot(4, 512, BF16, "kc")
    vr = self.rot(4, 516, BF16, "vc")
    hfr = self.rot(4, 512, F32, "hfc")
    ocr = self.rot(5, 512, F32, "oc")
    sTr = self.rot(3, 512, BF16, "sT")
    ktr = self.rot(3, 512, BF16, "kt")
    hst = self.rot(3, 512, F32, "hst")
    yr = self.rot(3, 512, F32, "y")
    ybr = self.rot(2, 512, BF16, "yb")
    stT = self.rot(2, 512, BF16, "stTm")
    sm = self.rot(6, 32, F32, "smm")
    junkr = self.rot(2, 128, F32, "junkm")
    psS = Rot([0, 1])
    ptr = Rot([6, 7])
    fin, rfin = self.alloc(64, F32, "fin")
    fco, rfco = self.alloc(4 * 129, F32, "fco")

    def Nv(h):
        return self.psb(2 + h // 2)[:, (h % 2) * 129:(h % 2) * 129 + 129], self.r_ps[2 + h // 2]

    def Uv(h):
        return self.psb(4 + h // 2)[:, (h % 2) * 129:(h % 2) * 129 + 129], self.r_ps[4 + h // 2]

    def run_pass(dr, tiles, sample, sq):
        t_u, r_u = uu[dr]
        t_eb, r_eb = eb[dr]
        t_eg, r_eg = EG[dr]
        if sample:
            C3 = _h(Cst, "p (h e) -> p h e", h=4)
            for h in range(4):
                self.dma(C3[:, h, 0:128], d["stC"][l, dr, h], writes=[rC], key="cinit")
                self.dma(C3[:, h, 128:129], _h(d["stn"][l, dr, h], "(p o) -> p o", o=1), writes=[rC], key="cinit", slow=True)
            for h in range(4):
                self.ts(C3[:, h, :], C3[:, h, :], em0[:, dr * 4 + h:dr * 4 + h + 1], ALU.mult, reads=[rC, rem0], writes=[rC])
        else:
            self.memset(Cst, 0.0, writes=[rC])
        self.cp(Cbf, Cst, reads=[rC], writes=[rCb], eng="act")
        loaded = {}

        def load(ti):
            qT, rq = qTr.next(); kT, rk = kTr.next(); kc, rkc = kr.next(); vc, rv = vr.next()
            cs = slice(ti * 128, (ti + 1) * 128)
            self.dma(_h(qT, "p (h t) -> p h t", h=4), _h(S["QT_m"][:, cs], "(h p) t -> p h t", p=128), writes=[rq], key=("qTc", id(rq)))
            self.dma(_h(kT, "p (h t) -> p h t", h=4), _h(S["KT_m"][:, cs], "(h p) t -> p h t", p=128), writes=[rk], key=("kTc", id(rk)))
            self.dma(kc, S["K_m"][cs, :], writes=[rkc], key=("kc", id(rkc)))
            self.dma(vc, S["V_m"][cs, :], writes=[rv], key=("vc", id(rv)))
            ex = None
            if dr == 1:
                hf, rhf = hfr.next(); oc, roc = ocr.next()
                self.dma(hf, S["HF"][cs, :], writes=[rhf], key=("hfc", id(rhf)))
                self.dma(oc, S["O_m"][cs, :], writes=[roc], key=("oc", id(roc)))
                ex = (hf, rhf, oc, roc)
            loaded[ti] = (qT, rq, kT, rk, kc, rkc, vc, rv, ex)

        t_ug, r_ug = ug[dr]

        def chunk(ti):
            qT, rq, kT, rk, kc, rkc, vc, rv, ex = loaded.pop(ti)
            sb = psS.next()
            pS = self.psb(sb)
            for h in range(4):
                self.mm(pS[:, h * 128:(h + 1) * 128], kT[:, h * 128:(h + 1) * 128], qT[:, h * 128:(h + 1) * 128], True, True,
                        reads=[rk, rq], writes=[self.r_ps[sb]])
            sT, rsT = sTr.next()
            kt, rkt = ktr.next()
            for h in range(4):
                ucol = t_u[:, ti * 4 + h:ti * 4 + h + 1]
                self.stt(sT[:, h * 128:(h + 1) * 128], pS[:, h * 128:(h + 1) * 128], ucol, self.trib[dr], ALU.mult, ALU.mult,
                         reads=[self.r_ps[sb], r_u], writes=[rsT])
                self.act(kt[:, h * 128:(h + 1) * 128], kc[:, h * 128:(h + 1) * 128], AF.Copy, reads=[rkc, r_ug], writes=[rkt],
                         scale=t_ug[:, ti * 4 + h:ti * 4 + h + 1])
            yield
            for h in range(4):
                nv, rn = Nv(h)
                self.mm(nv, sT[:, h * 128:(h + 1) * 128], vc[:, h * 129:(h + 1) * 129], True, False, reads=[rsT, rv], writes=[rn])
                self.mm(nv, qT[:, h * 128:(h + 1) * 128], Cbf[:, h * 129:(h + 1) * 129], False, True, reads=[rq, rCb], writes=[rn])
            for h in range(4):
                uv, ru = Uv(h)
                self.mm(uv, kt[:, h * 128:(h + 1) * 128], vc[:, h * 129:(h + 1) * 129], True, True, reads=[rkt, rv], writes=[ru])
            for h in range(4):
                uv, ru = Uv(h)
                gcol = t_eg[:, ti * 4 + h:ti * 4 + h + 1]
                cs_ = slice(h * 129, (h + 1) * 129)
                self.stt(Cst[:, cs_], Cst[:, cs_], gcol, uv, ALU.mult, ALU.add, reads=[ru, r_eg, rC], writes=[rC])
            self.cp(Cbf, Cst, reads=[rC], writes=[rCb], eng="act")
            s_, rs_ = sm.next()
            for hp in range(2):
                dv = _h(self.psb(2 + hp)[:, 0:258], "p (h e) -> p h e", e=129)[:, :, 128]
                self.tt(s_[:, 2 * hp:2 * hp + 2], dv, t_eb[:, ti * 4 + 2 * hp:ti * 4 + 2 * hp + 2], ALU.mult,
                        reads=[self.r_ps[2 + hp], r_eb], writes=[rs_])
            self.ts(s_[:, 4:8], s_[:, 0:4], -1.0, ALU.mult, reads=[rs_], writes=[rs_])
            self.stt(s_[:, 8:12], s_[:, 0:4], 1.0, s_[:, 4:8], ALU.max, ALU.max, reads=[rs_], writes=[rs_])
            self.recip(s_[:, 12:16], s_[:, 8:12], reads=[rs_], writes=[rs_])
            self.tt(s_[:, 16:20], s_[:, 12:16], t_eb[:, ti * 4:ti * 4 + 4], ALU.mult, reads=[rs_, r_eb], writes=[rs_])
            hs, rhs = hst.next()
            for h in range(4):
                nv, rn = Nv(h)
                self.act(hs[:, h * 128:(h + 1) * 128], nv[:, 0:128], AF.Copy, reads=[rn, rs_], writes=[rhs], scale=s_[:, 16 + h:17 + h])
            if dr == 1:
                self.tt(hs, hs, ex[0], ALU.add, reads=[rhs, ex[1]], writes=[rhs])
            if dr == 0:
                self.dma(S["HF"][ti * 128:(ti + 1) * 128, :], hs, reads=[rhs], key=("hst", id(rhs)), q="act")
                return
            yield
            y, ry = yr.next()
            self.tt(y, hs, ex[2], ALU.mult, reads=[rhs, ex[3]], writes=[ry])
            s2, rs2 = sm.next()
            jk, rjk = junkr.next()
            for h in range(4):
                self.act(jk, y[:, h * 128:(h + 1) * 128], AF.Square, reads=[ry], writes=[rjk, rs2], accum_out=s2[:, h:h + 1])
            self.rstd(s2[:, 8:12], s2[:, 0:4], 1.0 / 128, reads=[rs2], writes=[rs2], tmp=s2[:, 4:8])
            yield
            y3 = _h(y, "p (h e) -> p h e", h=4)
            self.tt(y3, y3, s2[:, 8:12].unsqueeze(2).broadcast_to([128, 4, 128]), ALU.mult, reads=[ry, rs2], writes=[ry])
            yb, ryb = ybr.next()
            self.tt(yb, y, gml, ALU.mult, reads=[ry, rgml], writes=[ryb])
            pb_ = ptr.next()
            pv = self.psb_bf(pb_)
            for j in range(4):
                self.tr(pv[:, j * 128:(j + 1) * 128], yb[:, j * 128:(j + 1) * 128], self.identb, reads=[ryb], writes=[self.r_ps[pb_]])
            yield
            sg, rsg = stT.next()
            self.cp(sg, pv[:, 0:512], reads=[self.r_ps[pb_]], writes=[rsg], eng="act")
            self.dma(_h(S["OT"][3][:, ti * 128:(ti + 1) * 128], "(j p) t -> p j t", p=128), _h(sg, "p (j t) -> p j t", j=4),
                     reads=[rsg], key=("stTm", id(rsg)), q="act")

        pipe = Pipe()
        load(tiles[0])
        if len(tiles) > 1:
            load(tiles[1])
        for idx, ti in enumerate(tiles):
            if idx + 2 < len(tiles):
                load(tiles[idx + 2])
            pipe.push(chunk(ti))
        pipe.flush()
        if not sample:
            t_w, r_w = ww[dr]
            t_nt, r_nt = ntot[dr]
            c0, c1 = tiles[0], tiles[1]
            self.cp(fin[:, 0:4], t_w[:, c0 * 4:c0 * 4 + 4], reads=[r_w], writes=[rfin])
            self.cp(fin[:, 4:8], t_w[:, c1 * 4:c1 * 4 + 4], reads=[r_w], writes=[rfin])
            self.ts(fin[:, 8:12], t_nt[:, c0 * 4:c0 * 4 + 4], -1.0, ALU.mult, reads=[r_nt], writes=[rfin])
            self.ts(fin[:, 12:16], t_nt[:, c1 * 4:c1 * 4 + 4], -1.0, ALU.mult, reads=[r_nt], writes=[rfin])
            pb_ = ptr.next()
            pf = self.psb(pb_)
            self.tr(pf[0:16, 0:128], fin[:, 0:16], self.identf, reads=[rfin], writes=[self.r_ps[pb_]])
            self.red(fin[0:16, 16:17], pf[0:16, 0:128], ALU.max, reads=[self.r_ps[pb_]], writes=[rfin])
            self.tr(pf[0:1, 128:144], fin[0:16, 16:17], self.identf[0:16, 0:16], reads=[rfin], writes=[self.r_ps[pb_]])
            row = fin[0:1, 32:48]
            self.cp(row, pf[0:1, 128:144], reads=[self.r_ps[pb_]], writes=[rfin])
            self.tt(fin[0:1, 48:52], fin[0:1, 40:44], fin[0:1, 44:48], ALU.add, reads=[rfin], writes=[rfin])
            self.tt(fin[0:1, 52:56], fin[0:1, 44:48], fin[0:1, 32:36], ALU.add, reads=[rfin], writes=[rfin])
            self.tt(fin[0:1, 48:52], fin[0:1, 48:52], fin[0:1, 52:56], ALU.max, reads=[rfin], writes=[rfin])
            self.tt(fin[0:1, 48:52], fin[0:1, 48:52], fin[0:1, 36:40], ALU.max, reads=[rfin], writes=[rfin])
            self.dma(self.dout["o_m"][sq, l, dr:dr + 1, :], fin[0:1, 48:52], reads=[rfin], key="fin")
            self.mm(pf[:, 256:260], self.tri[2][0:1, :], fin[0:1, 48:52], True, True, reads=[rfin], writes=[self.r_ps[pb_]])
            self.act(fin[:, 56:60], pf[:, 256:260], AF.Exp, reads=[self.r_ps[pb_]], writes=[rfin], scale=-1.0)
            for h in range(4):
                self.ts(fco[:, h * 129:(h + 1) * 129], Cst[:, h * 129:(h + 1) * 129], fin[:, 56 + h:57 + h], ALU.mult,
                        reads=[rC, rfin], writes=[rfco])
            f3 = _h(fco, "p (h e) -> p h e", h=4)
            for h in range(4):
                self.dma(self.dout["o_C"][sq, l, dr, h], f3[:, h, 0:128], reads=[rfco], key="fco")
                self.dma(_h(self.dout["o_n"][sq, l, dr, h], "(p o) -> p o", o=1), f3[:, h, 128:129], reads=[rfco], key="fco", slow=True)

    seqs = [(True, list(range(0, NTS)), None), (False, [NTS, NTS + 1], 0), (False, [NTS + 2, NTS + 3], 1)]
    for dr in range(2):
        for (sample, tiles, sq) in seqs:
            run_pass(dr, tiles if dr == 0 else tiles[::-1], sample, sq)
        self.P.barrier()
    self.A.pop()


KB.phase_E = _phase_E


def _phase_F(self, l):
    d = self.din
    S = self.scr
    self.A.push()
    wb, rwb = self.alloc(4 * 4 * D, BF16, "wb")
    wb4 = _h(wb, "p (i k c) -> p i k c", i=4, k=4)
    for i in range(4):
        self.dma(wb4[:, i, :, :], _h(d["w_branch"][l, i], "(k p) c -> p k c", p=128), writes=[rwb], key="wb", q="pool")
    wo, rwo = self.alloc(8 * D, BF16, "wo")
    wo3 = _h(wo, "p (k c) -> p k c", k=8)
    self.dma(wo3, _h(d["w_out"][l], "(k p) c -> p k c", p=128), writes=[rwo], key="wo", q="pool")
    otr = self.rot(6, 4 * 512, BF16, "ot")
    gtr = self.rot(4, 512, BF16, "gt")
    gbr = self.rot(6, 512, BF16, "gated")
    tmpr = self.rot(3, 512, F32, "tmpf")
    mTr = self.rot(2, 8 * 512, BF16, "mT")
    self.load_mod((2, 3, 4))
    xr = self.rot(2, D, F32, "xF")
    x1r = self.rot(4, D, F32, "x1F")
    bufs = {"junk": self.rot(2, D, BF16, "junkF"), "ss": self.rot(4, 4, F32, "ssF"), "t": self.rot(2, D, F32, "tF"),
            "hb": self.rot(3, D, BF16, "hbF"), "pt": Rot([6, 7])}
    h2s = self.rot(2, D, BF16, "h2s")
    mmr = Rot([0, 1])
    mmo = Rot([2, 3])
    accb = Rot([4, 5])
    pipe = Pipe()
    pending = []

    def tile_part2(tc, tt_, mT3_, rmT_):
        ti = tc * 4 + tt_
        g = 0 if ti < NTS else 1
        xt, rx = xr.next()
        src = self.tile_rows(ti) if l == 0 else S["XRES"][ti * 128:(ti + 1) * 128, :]
        self.dma(xt, src, reads=[self.r_xres[ti]], writes=[rx], key=("xF", id(rx)))
        x1, rx1 = x1r.next()
        for nh in range(2):
            hs = slice(nh * 512, (nh + 1) * 512)
            pb = mmo.next()
            ps = self.psb(pb)
            for k in range(8):
                self.mm(ps, mT3_[:, k, tt_ * 128:(tt_ + 1) * 128], wo3[:, k, hs], k == 0, k == 7, reads=[rmT_, rwo], writes=[self.r_ps[pb]])
            tmp, rtmp = tmpr.next()
            self.tt(tmp, ps, self.mod(g, 2)[:, hs], ALU.mult, reads=[self.r_ps[pb], self.r_modt], writes=[rtmp])
            self.tt(x1[:, hs], tmp, xt[:, hs], ALU.add, reads=[rtmp, rx], writes=[rx1], eng="pool")
        self.dma(S["XRES"][ti * 128:(ti + 1) * 128, :], x1, reads=[rx1], writes=[self.r_xres[ti]], key=("x1F", id(rx1)), q="pool")

        def dst(pv, rp, ti=ti):
            sg, rsg = h2s.next()
            self.cp(sg, pv, reads=[rp], writes=[rsg], eng="act")
            self.dma(_h(S["H2T"][:, ti * 128:(ti + 1) * 128], "(k p) t -> p k t", p=128), _h(sg, "p (k t) -> p k t", k=8),
                     reads=[rsg], key=("h2s", id(rsg)), q="act")
        pipe.push(self.norm_mod_T(x1, rx1, g, 4, 3, dst, bufs))

    for tc in range(TOK // 512):
        cs = slice(tc * 512, (tc + 1) * 512)
        mT_, rmT_ = mTr.next()
        mT3_ = _h(mT_, "p (k t) -> p k t", k=8)
        ots = []
        for i in range(4):
            ot, rot_ = otr.next()
            self.dma(_h(ot, "p (k t) -> p k t", k=4), _h(S["OT"][i][:, cs], "(k p) t -> p k t", p=128), writes=[rot_], key=("ot", id(rot_)))
            ots.append((_h(ot, "p (k t) -> p k t", k=4), rot_))
        if tc == 0:
            self.warm(reads=[rwb, rwo] + [o[1] for o in ots])
        for db in range(8):
            ab = accb.next()
            pacc = self.psb(ab)
            gated = []
            for i in range(4):
                pb = mmr.next()
                ps = self.psb(pb)
                for kc in range(4):
                    self.mm(ps, wb4[:, i, kc, db * 128:(db + 1) * 128], ots[i][0][:, kc, :], kc == 0, kc == 3,
                            reads=[rwb, ots[i][1]], writes=[self.r_ps[pb]])
                g, rg = gtr.next()
                self.dma(g, S["GT"][i * 1024 + db * 128:i * 1024 + (db + 1) * 128, cs], writes=[rg], key=("gt", id(rg)))
                tb_, rtb = gbr.next()
                self.tt(tb_, ps, g, ALU.mult, reads=[self.r_ps[pb], rg], writes=[rtb])
                gated.append((tb_, rtb))
                if i >= 1:
                    pt_, rpt_ = gated[i - 1]
                    self.mm(pacc, self.identb, pt_, i == 1, False, reads=[rpt_], writes=[self.r_ps[ab]])
            pt_, rpt_ = gated[3]
            self.mm(pacc, self.identb, pt_, False, True, reads=[rpt_], writes=[self.r_ps[ab]])
            self.cp(mT3_[:, db, :], pacc, reads=[self.r_ps[ab]], writes=[rmT_], eng="act")
            if db % 2 == 1 and pending:
                pending.pop(0)()
        while pending:
            pending.pop(0)()
        for tt_ in range(4):
            pending.append(lambda tc=tc, tt_=tt_, m3=mT3_, rm=rmT_: tile_part2(tc, tt_, m3, rm))
    while pending:
        pending.pop(0)()
    pipe.flush()
    self.P.barrier()
    self.A.pop()


def _phase_G(self, l, last):
    d = self.din
    S = self.scr
    self.A.push()
    NJ = D_FF // 128
    TB = 9
    JG = 4
    self.load_mod((5,))
    w2, rw2 = self.alloc(NJ * D, BF16, "w2")
    w23 = _h(w2, "p (j c) -> p j c", j=NJ)
    for q in range(2):
        self.dma(w23[:, q * 11:(q + 1) * 11, :], _h(d["w_ffn_out"][l][q * 1408:(q + 1) * 1408, :], "(j p) c -> p j c", p=128),
                 writes=[rw2], key="w2", q="pool")
    h2r = self.rot(1, 8 * TB * 128, BF16, "h2blk")
    actT, raT = self.alloc(NJ * TB * 128, BF16, "actT")
    a3 = _h(actT, "p (j t) -> p j t", j=NJ)
    wsr = self.rot(2, 8 * 2 * JG * 128, BF16, "w1sl")
    sgr = self.rot(3, 384, F32, "sg")
    x1r = self.rot(2, D, F32, "x1G")
    x2r = self.rot(2, D, F32, "x2G")
    tmpr = self.rot(2, 512, F32, "tmpG")
    junk, rj = self.alloc(D, BF16, "junkG")
    ssr = self.rot(2, 4, F32, "ssG")
    mmr = Rot([0, 1, 2, 3, 4, 5, 6, 7])
    jgroups = [list(range(j0, min(j0 + JG, NJ))) for j0 in range(0, NJ, JG)]
    for blk in range(NT // TB):
        t0 = blk * TB * 128
        h2, rh2 = h2r.next()
        h23 = _h(h2, "p (k t) -> p k t", k=8)
        self.dma(h23, _h(S["H2T"][:, t0:t0 + TB * 128], "(k p) t -> p k t", p=128), writes=[rh2], key=("h2blk", id(rh2)))
        for js in jgroups:
            nj = len(js)
            ws, rws = wsr.next()
            ws4 = _h(ws, "p (k u c) -> p k u c", k=8, u=2)
            c0 = js[0] * 128
            self.dma(ws4[:, :, 0, 0:nj * 128], _h(d["w_ffn_in"][l][:, c0:c0 + nj * 128], "(k p) c -> p k c", p=128), writes=[rws],
                     key=("w1sl", id(rws)), q="pool")
            self.dma(ws4[:, :, 1, 0:nj * 128], _h(d["w_ffn_in"][l][:, D_FF + c0:D_FF + c0 + nj * 128], "(k p) c -> p k c", p=128),
                     writes=[rws], key=("w1sl", id(rws)), q="pool")
            for ji, j in enumerate(js):
                for sub in range(TB * 128 // 384):
                    ss_ = slice(sub * 384, (sub + 1) * 384)
                    pg = mmr.next(); pu = mmr.next()
                    for k in range(8):
                        self.mm(self.psb(pg)[:, 0:384], ws4[:, k, 0, ji * 128:(ji + 1) * 128], h23[:, k, ss_], k == 0, k == 7,
                                reads=[rws, rh2], writes=[self.r_ps[pg]])
                    for k in range(8):
                        self.mm(self.psb(pu)[:, 0:384], ws4[:, k, 1, ji * 128:(ji + 1) * 128], h23[:, k, ss_], k == 0, k == 7,
                                reads=[rws, rh2], writes=[self.r_ps[pu]])
                    sg, rsg = sgr.next()
                    self.act(sg, self.psb(pg)[:, 0:384], AF.Silu, reads=[self.r_ps[pg]], writes=[rsg])
                    self.tt(a3[:, j, ss_], sg, self.psb(pu)[:, 0:384], ALU.mult, reads=[rsg, self.r_ps[pu]], writes=[raT])
        for tt_ in range(TB):
            ti = blk * TB + tt_
            g = 0 if ti < NTS else 1
            x1, rx1 = x1r.next()
            self.dma(x1, S["XRES"][ti * 128:(ti + 1) * 128, :], reads=[self.r_xres[ti]], writes=[rx1], key=("x1G", id(rx1)))
            x2, rx2 = x2r.next()
            for nh in range(2):
                hs = slice(nh * 512, (nh + 1) * 512)
                pb = mmr.next()
                ps = self.psb(pb)
                for j in range(NJ):
                    self.mm(ps, a3[:, j, tt_ * 128:(tt_ + 1) * 128], w23[:, j, hs], j == 0, j == NJ - 1, reads=[raT, rw2], writes=[self.r_ps[pb]])
                tmp, rtmp = tmpr.next()
                self.tt(tmp, ps, self.mod(g, 5)[:, hs], ALU.mult, reads=[self.r_ps[pb], self.r_modt], writes=[rtmp])
                self.tt(x2[:, hs], tmp, x1[:, hs], ALU.add, reads=[rtmp, rx1], writes=[rx2], eng="pool")
            if not last:
                self.dma(S["XRES"][ti * 128:(ti + 1) * 128, :], x2, reads=[rx2], writes=[self.r_xres[ti]], key=("x2G", id(rx2)))
            else:
                ssb, rss = ssr.next()
                self.act(junk, x2, AF.Square, reads=[rx2], writes=[rj, rss], accum_out=ssb[:, 0:1])
                self.rstd(ssb[:, 2:3], ssb[:, 0:1], 1.0 / D, reads=[rss], writes=[rss], tmp=ssb[:, 1:2])
                self.stt(x2, x2, ssb[:, 2:3], self.gfin, ALU.mult, ALU.mult, reads=[rx2, rss], writes=[rx2])
                if ti < NTS:
                    dsto = self.dout["y_s"][ti * 128:(ti + 1) * 128, :]
                else:
                    dsto = self.dout["y_p"][(ti - NTS) * 128:(ti - NTS + 1) * 128, :]
                self.dma(dsto, x2, reads=[rx2], key=("x2G", id(rx2)))
    self.P.barrier()
    self.A.pop()


KB.phase_F = _phase_F
KB.phase_G = _phase_G


_CACHE = {}


def _get_program():
    if "nc" not in _CACHE:
        kb = KB()
        _CACHE["nc"] = kb.build()
        _CACHE["kb"] = kb
    return _CACHE["nc"]


def kernel(**inputs):
    inputs = {k: np.asarray(v) for k, v in inputs.items()}
    nc = _get_program()
    consts = _host_consts()
    nab = _na_bias_tiles(inputs["na_rpb"].astype(np.float32))
    in_maps = [_core_inputs(inputs, c, consts, nab) for c in range(8)]
    res = run_bass_kernel_spmd(nc, in_maps, core_ids=list(range(8)))
    rs = res.results
    f32 = np.float32
    y_prompt = np.concatenate([np.asarray(r["y_p"], f32).reshape(2, 256, D) for r in rs], 0)
    y_sample = np.stack([np.asarray(r["y_s"], f32) for r in rs], 0)
    na_kv = np.concatenate([np.asarray(r["o_na"], f32).reshape(2, 2, 2, 256, 8, 64) for r in rs], 0)
    gqa_kv = np.concatenate([np.asarray(r["o_g"], f32).reshape(2, 2, 2, 256, 2, 64) for r in rs], 0)
    swa_kv = np.concatenate([np.asarray(r["o_s"], f32).reshape(2, 2, 2, 256, 2, 64) for r in rs], 0)
    ml_C = np.concatenate([np.asarray(r["o_C"], f32) for r in rs], 0)
    ml_n = np.concatenate([np.asarray(r["o_n"], f32) for r in rs], 0)
    ml_m = np.concatenate([np.asarray(r["o_m"], f32) for r in rs], 0)
    return (y_prompt, y_sample, na_kv, gqa_kv, swa_kv, ml_C, ml_n, ml_m)
```

```python
import math
from contextlib import ExitStack

import numpy as np
import ml_dtypes

import concourse.bass as bass
import concourse.mybir as mybir
from concourse.alu_op_type import AluOpType as ALU
from concourse.bass_utils import run_bass_kernel_spmd

F32 = mybir.dt.float32
BF16 = mybir.dt.bfloat16
AF = mybir.ActivationFunctionType
AX = mybir.AxisListType

D = 1024
DEPTH = 2
NT = 36
NTS = 32
TOK = NT * 128
SEQ_S = 4096
GRID_W = 64
DH = 64
N_IN = 9232
D_FF = 2816
EPS = 1e-6
NEG = -30000.0
O_NAQ, O_NAK, O_NAV = 0, 512, 1024
O_GQ, O_GK, O_GV = 1536, 2048, 2176
O_SQ, O_SK, O_SV = 2304, 2816, 2944
O_MQ, O_MK, O_MV, O_MO, O_MI, O_MF = 3072, 3584, 4096, 4608, 5120, 5128
O_GATES = 5136

STREAMS = ("pe", "act", "dve", "pool", "sp")


class Res:
    __slots__ = ("name", "w", "r", "const")

    def __init__(self, name):
        self.name = name
        self.w = None
        self.r = []
        self.const = False


class Op:
    __slots__ = ("stream", "fn", "deps", "flag", "dma_key", "dma_val", "rank", "idx", "extra_waits")

    def __init__(self, stream, fn):
        self.stream = stream
        self.fn = fn
        self.deps = []
        self.flag = False
        self.dma_key = None
        self.dma_val = 0
        self.rank = 0
        self.idx = 0
        self.extra_waits = None


class Prog:
    def __init__(self):
        self.ops = []
        self.last = {s: None for s in STREAMS}
        self.dma_count = {}
        self.dma_last = {}
        self.key2sem = {}

    def _dep(self, o, p):
        if p is None or p is o:
            return
        if p.dma_key is None and o.dma_key is None and p.stream == "pe" and o.stream == "pe":
            return
        o.deps.append(p)
        p.flag = True

    def add(self, stream, fn, reads=(), writes=(), dma_key=None):
        o = Op(stream, fn)
        o.idx = len(self.ops)
        if dma_key is not None:
            ns = "sw" if stream == "pool" else "hw"
            if (ns, dma_key) not in self.key2sem:
                self.key2sem[(ns, dma_key)] = (ns, sum(1 for k in self.key2sem if k[0] == ns))
            dma_key = self.key2sem[(ns, dma_key)]
            o.dma_key = dma_key
            c = self.dma_count.get(dma_key, 0) + 16
            self.dma_count[dma_key] = c
            o.dma_val = c
            self.dma_last[dma_key] = o
        for r in reads:
            if r.const:
                continue
            self._dep(o, r.w)
            r.r.append(o)
        for w in writes:
            assert not w.const, w.name
            self._dep(o, w.w)
            for rd in w.r:
                self._dep(o, rd)
            w.w = o
            w.r = []
        self.ops.append(o)
        self.last[stream] = o
        return o

    def barrier(self):
        lasts = [self.last[s] for s in STREAMS if self.last[s] is not None]
        dmas = list(self.dma_last.values())
        for s in STREAMS:
            o = Op(s, None)
            o.idx = len(self.ops)
            for p in lasts + dmas:
                if p is not None:
                    o.deps.append(p)
                    p.flag = True
            self.ops.append(o)
        self.key2sem = {}

    def emit(self, nc, stack):
        cnt = {s: 0 for s in STREAMS}
        for o in self.ops:
            if o.dma_key is None and o.flag and o.fn is not None:
                cnt[o.stream] += 1
                o.rank = cnt[o.stream]
        esem = {s: stack.enter_context(nc.semaphore("e_" + s)) for s in STREAMS}
        dsem = {k: stack.enter_context(nc.semaphore("d_%d" % i)) for i, k in enumerate(self.dma_count)}
        self.n_sems = len(esem) + len(dsem)
        by_stream = {s: [] for s in STREAMS}
        for o in self.ops:
            by_stream[o.stream].append(o)
        key_prog = {}
        upgraded = {}
        for o in self.ops:
            for p in o.deps:
                if p.dma_key is not None:
                    upgraded[(o.idx, p.idx)] = max(p.dma_val, key_prog.get(p.dma_key, 0))
            if o.dma_key is not None:
                key_prog[o.dma_key] = o.dma_val

        def run_stream(s, eng):
            seen = {}
            for o in by_stream[s]:
                need = {}
                for p in o.deps:
                    if p.dma_key is not None:
                        sem = dsem[p.dma_key]
                        val = upgraded[(o.idx, p.idx)]
                        kk = ("d", p.dma_key)
                    else:
                        if p.fn is None:
                            continue
                        sem = esem[p.stream]
                        val = p.rank
                        kk = ("e", p.stream)
                    if seen.get(kk, 0) >= val:
                        continue
                    if kk not in need or need[kk][1] < val:
                        need[kk] = (sem, val)
                for kk, (sem, val) in need.items():
                    eng.wait_ge(sem, val)
                    seen[kk] = val
                if o.fn is None:
                    continue
                ins = o.fn(eng)
                if o.dma_key is not None:
                    ins.then_inc(dsem[o.dma_key], 16)
                elif o.flag:
                    ins.then_inc(esem[s], 1)

        block = stack.enter_context(nc.Block())

        @block.tensor
        def _(e):
            run_stream("pe", e)

        @block.scalar
        def _(e):
            run_stream("act", e)

        @block.vector
        def _(e):
            run_stream("dve", e)

        @block.gpsimd
        def _(e):
            run_stream("pool", e)

        @block.sync
        def _(e):
            run_stream("sp", e)


class Arena:
    def __init__(self, tensor, ncols):
        self.t = tensor
        self.ncols = ncols
        self.top = 0
        self.marks = []
        self.peak = 0

    def push(self):
        self.marks.append(self.top)

    def pop(self):
        self.top = self.marks.pop()

    def alloc(self, cols, dtype=F32, parts=128):
        n32 = cols if dtype == F32 else (cols + 1) // 2
        n32 = (n32 + 7) // 8 * 8
        off = self.top
        self.top += n32
        assert self.top <= self.ncols, "SBUF arena overflow: %d > %d" % (self.top, self.ncols)
        self.peak = max(self.peak, self.top)
        v = self.t[0:parts, off:off + n32]
        if dtype != F32:
            v = v.bitcast(dtype)[:, 0:cols]
        else:
            v = v[:, 0:cols]
        return v


class Rot:
    def __init__(self, items):
        self.items = items
        self.i = 0

    def next(self):
        it = self.items[self.i % len(self.items)]
        self.i += 1
        return it


def _h(ap, pattern, **kw):
    return ap.rearrange(pattern, **kw)


class Pipe:
    def __init__(self):
        self.active = []

    def push(self, gen):
        if gen is not None:
            self.active.insert(0, gen)
        self.step()

    def step(self):
        for g in list(self.active):
            try:
                next(g)
            except StopIteration:
                self.active.remove(g)

    def flush(self):
        while self.active:
            self.step()


def _na_kbs(a):
    rows = set()
    for r in (2 * a, 2 * a + 1):
        rs = min(max(r - 4, 0), 56)
        rows.update(range(rs, rs + 8))
    return sorted(set(r // 2 for r in rows))


def _na_class(a):
    return {0: 0, 1: 1, 30: 3, 31: 4}.get(a, 2)


def _na_tile_index():
    idx = {}
    for a in (0, 1, 2, 30, 31):
        for kb in reversed(_na_kbs(a)):
            key = (_na_class(a), kb - a)
            if key not in idx:
                idx[key] = len(idx)
    return idx


NA_TILE_IDX = _na_tile_index()
NA_NB = len(NA_TILE_IDX)


def _na_gather_indices():
    out = {}
    reps = {0: 0, 1: 1, 2: 2, 3: 30, 4: 31}
    for (cls, delta), tid in NA_TILE_IDX.items():
        a = reps[cls]
        kb = a + delta
        kk = np.arange(128)
        kr = 2 * kb + kk // 64
        kc = kk % 64
        qr = 2 * a + kk // 64
        qc = kk % 64
        rs = np.clip(qr - 4, 0, 56)
        cs = np.clip(qc - 8, 0, 48)
        valid = ((kr[:, None] >= rs[None, :]) & (kr[:, None] < rs[None, :] + 8) &
                 (kc[:, None] >= cs[None, :]) & (kc[:, None] < cs[None, :] + 16))
        dr = np.clip(kr[:, None] - qr[None, :] + 7, 0, 14)
        dc = np.clip(kc[:, None] - qc[None, :] + 15, 0, 30)
        out[tid] = (dr, dc, valid)
    return out


def _host_consts():
    c = {}
    c["identb"] = np.eye(128, dtype=np.float32).astype(ml_dtypes.bfloat16)
    c["identf"] = np.eye(128, dtype=np.float32)
    s = np.arange(128)
    le = (s[:, None] <= s[None, :]).astype(np.float32)
    ge = (s[:, None] >= s[None, :]).astype(np.float32)
    c["tri_f32"] = np.stack([le, ge, np.ones((128, 128), np.float32)], 0)
    c["tri_bf"] = np.stack([le, ge], 0).astype(ml_dtypes.bfloat16)
    c["swa_mask"] = np.stack([np.where(s[None, :] <= s[:, None], 0.0, NEG),
                              np.where(s[:, None] <= s[None, :], 0.0, NEG)], 0).astype(ml_dtypes.bfloat16)
    half = 32
    freqs = (10000.0 ** (-np.arange(0, half, 2, dtype=np.float32) / half)).astype(np.float32)
    t = np.arange(SEQ_S)
    ang_r = (t // GRID_W).astype(np.float32)[:, None] * freqs
    ang_c = (t % GRID_W).astype(np.float32)[:, None] * freqs
    cr, sr, cc, sc = np.cos(ang_r), np.sin(ang_r), np.cos(ang_c), np.sin(ang_c)
    c["rope_cos"] = np.concatenate([cr, cr, cc, cc], 1).astype(np.float32)
    c["rope_sin"] = np.concatenate([-sr, sr, -sc, sc], 1).astype(np.float32)
    return c


HOST_CONSTS = None


class KB:
    def __init__(self, debug=None, stop_after=None, layers=DEPTH):
        self.debug = debug or ()
        self.stop_after = stop_after
        self.layers = layers
        self.nc = bass.Bass("TRN2", target_bir_lowering=False)
        self.P = Prog()
        self.din = {}
        self.dout = {}
        self.scr = {}

    def _in(self, name, shape, dtype=F32):
        self.din[name] = self.nc.dram_tensor(name, list(shape), dtype, kind="ExternalInput").ap()

    def _out(self, name, shape, dtype=F32):
        self.dout[name] = self.nc.dram_tensor(name, list(shape), dtype, kind="ExternalOutput").ap()

    def _scr(self, name, shape, dtype):
        kind = "ExternalOutput" if name in self.debug else "Internal"
        self.scr[name] = self.nc.dram_tensor("scr_" + name, list(shape), dtype, kind=kind).ap()

    def declare(self):
        i = self._in
        i("xs", [4096, D]); i("xp", [512, D])
        i("cna", [2, 2, 256, 512]); i("cg", [2, 2, 256, 128]); i("cs", [2, 2, 256, 128])
        i("stC", [2, 2, 4, 128, 128]); i("stn", [2, 2, 4, 128]); i("stm", [2, 8])
        i("cvec", [2, D])
        i("w_mod", [2, D, 6 * D]); i("b_mod", [2, 6 * D]); i("norm1_g", [2, D]); i("norm2_g", [2, D])
        i("w_in", [2, D, N_IN]); i("b_in", [2, N_IN])
        i("gqa_q_g", [2, 64]); i("gqa_k_g", [2, 64]); i("swa_sink", [2, 8]); i("mlstm_norm_g", [2, 512])
        i("w_branch", [2, 4, 512, D]); i("w_out", [2, D, D]); i("w_ffn_in", [2, D, 2 * D_FF])
        i("w_ffn_out", [2, D_FF, D]); i("final_norm_g", [1, D])
        i("nabias", [2, 8, NA_NB, 128, 128])
        i("identb", [128, 128], BF16); i("identf", [128, 128]); i("tri_f32", [3, 128, 128])
        i("tri_bf", [2, 128, 128], BF16); i("swa_mask", [2, 128, 128], BF16)
        i("rope_cos", [4096, 64]); i("rope_sin", [4096, 64])
        o = self._out
        o("y_s", [4096, D]); o("y_p", [512, D])
        o("o_na", [2, 2, 2, 256, 512]); o("o_g", [2, 2, 2, 256, 128]); o("o_s", [2, 2, 2, 256, 128])
        o("o_C", [2, 2, 2, 4, 128, 128]); o("o_n", [2, 2, 2, 4, 128]); o("o_m", [2, 2, 2, 4])
        s = self._scr
        s("XRES", [TOK, D], F32)
        s("QT_na", [512, TOK], BF16); s("KT_na", [512, TOK], BF16); s("V_na", [TOK, 512], BF16)
        s("QT_g", [512, TOK], BF16); s("KT_g", [128, TOK], BF16); s("V_g", [TOK, 128], BF16)
        s("QT_s", [512, TOK], BF16); s("KT_s", [128, TOK], BF16); s("V_s", [TOK, 128], BF16)
        s("QT_m", [512, TOK], BF16); s("KT_m", [512, TOK], BF16); s("K_m", [TOK, 512], BF16)
        s("V_m", [TOK, 4 * 129], BF16); s("O_m", [TOK, 512], F32)
        s("GT", [4096, TOK], BF16)
        s("HF", [TOK, 512], F32)
        s("OT", [4, 512, TOK], BF16)
        s("H2T", [D, TOK], BF16)
        if "HT" in self.debug:
            s("HT", [D, TOK], BF16)
        s("MOD", [2, 128, 6 * D], F32)
        if "IFD" in self.debug:
            s("IFD", [128, NT * 16], F32)

    def dma(self, out, in_, reads=(), writes=(), key=None, q="sp", slow=False):
        assert key is not None
        if slow:
            fn = lambda e: e.dma_start(out=out, in_=in_, allow_slow_non_contiguous=True)
        else:
            fn = lambda e: e.dma_start(out=out, in_=in_)
        return self.P.add(q, fn, reads=reads, writes=writes, dma_key=key)

    def mm(self, out, lhsT, rhs, start, stop, reads=(), writes=(), **kw):
        return self.P.add("pe", lambda e: e.matmul(out, lhsT=lhsT, rhs=rhs, start=start, stop=stop, **kw),
                          reads=reads, writes=writes)

    def tr(self, out, in_, ident, reads=(), writes=()):
        return self.P.add("pe", lambda e: e.transpose(out=out, in_=in_, identity=ident), reads=reads, writes=writes)

    def act(self, out, in_, func, reads=(), writes=(), **kw):
        return self.P.add("act", lambda e: e.activation(out=out, in_=in_, func=func, **kw), reads=reads, writes=writes)

    def tt(self, out, in0, in1, op, reads=(), writes=(), eng="dve"):
        return self.P.add(eng, lambda e: e.tensor_tensor(out=out, in0=in0, in1=in1, op=op), reads=reads, writes=writes)

    def ts(self, out, in0, s1, op0, s2=None, op1=None, reads=(), writes=(), eng="dve"):
        if op1 is None:
            fn = lambda e: e.tensor_scalar(out=out, in0=in0, scalar1=s1, scalar2=None, op0=op0)
        else:
            fn = lambda e: e.tensor_scalar(out=out, in0=in0, scalar1=s1, scalar2=s2, op0=op0, op1=op1)
        return self.P.add(eng, fn, reads=reads, writes=writes)

    def stt(self, out, in0, scalar, in1, op0, op1, reads=(), writes=()):
        return self.P.add("dve", lambda e: e.scalar_tensor_tensor(out=out, in0=in0, scalar=scalar, in1=in1, op0=op0, op1=op1),
                          reads=reads, writes=writes)

    def cp(self, out, in_, reads=(), writes=(), eng="dve"):
        if eng == "act":
            return self.act(out, in_, AF.Copy, reads=reads, writes=writes)
        return self.P.add(eng, lambda e: e.tensor_copy(out=out, in_=in_), reads=reads, writes=writes)

    def red(self, out, in_, op, reads=(), writes=()):
        return self.P.add("dve", lambda e: e.tensor_reduce(out=out, in_=in_, axis=AX.X, op=op), reads=reads, writes=writes)

    def recip(self, out, in_, reads=(), writes=()):
        return self.P.add("dve", lambda e: e.reciprocal(out=out, in_=in_), reads=reads, writes=writes)

    def recip_fast(self, out, in_, reads=(), writes=()):
        return self.P.add("dve", lambda e: e.reciprocal_approx_fast(out=out, in_=in_), reads=reads, writes=writes)

    def memset(self, ap, val, writes=(), eng="dve"):
        return self.P.add(eng, lambda e: e.memset(ap, val), writes=writes)

    def rstd(self, out, ss, scale, reads=(), writes=(), tmp=None):
        self.act(tmp, ss, AF.Ln, reads=reads, writes=writes, scale=scale, bias=self.eps_col[0:ss.shape[0], :])
        self.act(out, tmp, AF.Exp, reads=writes, writes=writes, scale=-0.5)

    def alloc(self, cols, dtype=F32, name="t", parts=128):
        return self.A.alloc(cols, dtype, parts), Res(name)

    def warm(self, reads=(), n=24, bank=7):
        for i in range(n):
            self.mm(self.psb(bank), self.identb, self.wz, True, True, reads=list(reads), writes=[self.r_ps[bank]])

    def rot(self, n, cols, dtype=F32, name="r"):
        return Rot([self.alloc(cols, dtype, "%s%d" % (name, i)) for i in range(n)])

    def tile_rows(self, ti):
        if ti < NTS:
            return self.din["xs"][ti * 128:(ti + 1) * 128, :]
        return self.din["xp"][(ti - NTS) * 128:(ti - NTS + 1) * 128, :]

    def build(self):
        nc = self.nc
        self.declare()
        st = ExitStack()
        with st:
            arena_t = st.enter_context(nc.sbuf_tensor("arena", [128, 52000], F32))
            self.ps_t = st.enter_context(nc.psum_tensor("psum", [128, 8, 512], F32))
            self.A = Arena(arena_t, 52000)
            self.r_ps = [Res("ps%d" % i) for i in range(8)]
            self.r_xres = [Res("xres%d" % i) for i in range(NT)]
            self.body()
            self.P.barrier()
            self.P.emit(nc, st)
        return nc

    def psb(self, b):
        return self.ps_t[:, b, :]

    def psb_bf(self, b):
        return self.ps_t[:, b, :].bitcast(BF16)

    def done(self, name):
        return self.stop_after == name

    def body(self):
        self.preamble()
        for l in range(self.layers):
            self.A.push()
            self.A.push()
            self.phase_A(l)
            self.A.pop()
            if self.done("A%d" % l):
                return
            self.IF, self.r_IF = self.alloc(NT * 16, F32, "IF")
            self.A.push()
            self.hT, _ = self.alloc(8 * TOK, BF16, "hT")
            self.r_hT = [Res("hT%d" % i) for i in range(NT)]
            self.phase_B(l)
            if self.done("B%d" % l):
                return
            self.phase_C(l)
            self.P.barrier()
            self.A.pop()
            if self.done("C%d" % l):
                return
            if not getattr(self, "skip_D", False):
                self.phase_D(l)
            if self.done("D%d" % l):
                return
            if not getattr(self, "skip_E", False):
                self.phase_E(l)
            if self.done("E%d" % l):
                return
            self.phase_F(l)
            if self.done("F%d" % l):
                return
            self.phase_G(l, l == self.layers - 1)
            if self.done("G%d" % l):
                return
            self.A.pop()

    def preamble(self):
        A = self.A
        d = self.din
        self.identb, r = self.alloc(128, BF16, "identb")
        self.dma(self.identb, d["identb"][:, :], writes=[r], key="c0")
        self.r_const = r
        self.identf, _ = self.alloc(128, F32, "identf")
        self.dma(self.identf, d["identf"][:, :], writes=[r], key="c0")
        self.tri = []
        for k in range(3):
            t, _ = self.alloc(128, F32, "tri")
            self.dma(t, d["tri_f32"][k], writes=[r], key="c0")
            self.tri.append(t)
        self.trib = []
        for k in range(2):
            t, _ = self.alloc(128, BF16, "trib")
            self.dma(t, d["tri_bf"][k], writes=[r], key="c0")
            self.trib.append(t)
        self.swam = []
        for k in range(2):
            t, _ = self.alloc(128, BF16, "swam")
            self.dma(t, d["swa_mask"][k], writes=[r], key="c0")
            self.swam.append(t)
        self.wz, _ = self.alloc(512, BF16, "wz")
        self.memset(self.wz, 0.0, writes=[r])
        self.eps_col, _ = self.alloc(1, F32, "eps")
        self.memset(self.eps_col, EPS, writes=[r])
        self.one_col, _ = self.alloc(1, F32, "one")
        self.memset(self.one_col, 1.0, writes=[r])
        cvT, rc = self.alloc(16, F32, "cvT")
        self.dma(_h(cvT, "p (g k) -> p g k", g=2), _h(d["cvec"], "g (k p) -> p g k", p=128), writes=[rc], key="c1", slow=True)
        sv, rs = self.alloc(16, F32, "sv")
        self.act(sv, cvT, AF.Silu, reads=[rc], writes=[rs])
        self.cb = []
        for g in range(2):
            t, _ = self.alloc(8 * 128, BF16, "cb")
            self.cp(_h(t, "p (k m) -> p k m", k=8), sv[:, g * 8:(g + 1) * 8].unsqueeze(2).broadcast_to([128, 8, 128]),
                    reads=[rs], writes=[r])
            self.cb.append(t)
        self.gfin, _ = self.alloc(D, F32, "gfin")
        self.dma(self.gfin, d["final_norm_g"].broadcast_to([128, D]), writes=[r], key="c0")
        self.P.barrier()
        r.const = True

    def phase_A(self, l):
        d = self.din
        rc = self.r_const
        self.MOD = []
        self.r_MOD = []
        for g in range(2):
            t, r = self.alloc(6 * D, F32, "MOD%d" % g)
            self.MOD.append(t)
            self.r_MOD.append(r)
        self.A.push()
        wsl = self.rot(2, 8 * 512, BF16, "wmod")
        bsl = self.rot(2, 512, F32, "bmod")
        pr = Rot([0, 1, 2, 3])
        for cg in range(12):
            w, rw = wsl.next()
            b, rb = bsl.next()
            self.dma(_h(w, "p (k c) -> p k c", k=8), _h(d["w_mod"][l][:, cg * 512:(cg + 1) * 512], "(k p) c -> p k c", p=128),
                     writes=[rw], key=("wmod", id(rw)), q="pool")
            self.dma(b, d["b_mod"][l:l + 1, cg * 512:(cg + 1) * 512].broadcast_to([128, 512]), writes=[rb], key=("bmod", id(rb)))
            for g in range(2):
                pb = pr.next()
                for k in range(8):
                    self.mm(self.psb(pb), self.cb[g][:, k * 128:(k + 1) * 128], w[:, k * 512:(k + 1) * 512], k == 0, k == 7,
                            reads=[rw, rc], writes=[self.r_ps[pb]])
                self.tt(self.MOD[g][:, cg * 512:(cg + 1) * 512], self.psb(pb), b, ALU.add,
                        reads=[self.r_ps[pb], rb], writes=[self.r_MOD[g]])
        gt, rg = self.alloc(D, F32, "gtmp")
        for (col, nm) in ((1, "norm1_g"), (4, "norm2_g")):
            self.dma(gt, d[nm][l:l + 1, :].broadcast_to([128, D]), writes=[rg], key="gtmp")
            for g in range(2):
                v = self.MOD[g][:, col * D:(col + 1) * D]
                self.stt(v, v, 1.0, gt, ALU.add, ALU.mult, reads=[rg, self.r_MOD[g]], writes=[self.r_MOD[g]])
        for g in range(2):
            self.dma(self.scr["MOD"][g], self.MOD[g], reads=[self.r_MOD[g]], key="modst")
        self.P.barrier()
        self.A.pop()

    def load_mod(self, idxs):
        self.modt = {}
        self.r_modt, = [Res("modt")]
        for g in range(2):
            for idx in idxs:
                t, _ = self.alloc(D, F32, "mod%d_%d" % (g, idx))
                self.dma(t, self.scr["MOD"][g][:, idx * D:(idx + 1) * D], writes=[self.r_modt], key="modld")
                self.modt[(g, idx)] = t

    def mod(self, g, idx):
        return self.modt[(g, idx)]

    def norm_mod_T(self, xt, rx, g, gi, si, dst_fn, bufs):
        junk, rj = bufs["junk"].next()
        ssb, rss = bufs["ss"].next()
        t, rt = bufs["t"].next()
        hb, rhb = bufs["hb"].next()
        self.act(junk, xt, AF.Square, reads=[rx], writes=[rj, rss], accum_out=ssb[:, 0:1])
        self.rstd(ssb[:, 2:3], ssb[:, 0:1], 1.0 / D, reads=[rss], writes=[rss], tmp=ssb[:, 1:2])
        self.stt(t, xt, ssb[:, 2:3], self.mod(g, gi), ALU.mult, ALU.mult, reads=[rx, rss, self.r_modt], writes=[rt])
        self.tt(hb, t, self.mod(g, si), ALU.add, reads=[rt, self.r_modt], writes=[rhb], eng="pool")
        yield
        pb = bufs["pt"].next()
        pv = self.psb_bf(pb)
        for k in range(8):
            self.tr(pv[:, k * 128:(k + 1) * 128], hb[:, k * 128:(k + 1) * 128], self.identb, reads=[rhb], writes=[self.r_ps[pb]])
        yield
        dst_fn(pv, self.r_ps[pb])

    def phase_B(self, l):
        self.A.push()
        self.load_mod((0, 1))
        xr = self.rot(5, D, F32, "xt")
        bufs = {"junk": self.rot(2, D, BF16, "junk"), "ss": self.rot(4, 4, F32, "ss"), "t": self.rot(2, D, F32, "t"),
                "hb": self.rot(3, D, BF16, "hb"), "pt": Rot([5, 6, 7])}
        pipe = Pipe()
        hT3 = _h(self.hT, "p (k t) -> p k t", k=8)
        loaded = {}

        def load(ti):
            xt, rx = xr.next()
            src = self.tile_rows(ti) if l == 0 else self.scr["XRES"][ti * 128:(ti + 1) * 128, :]
            self.dma(xt, src, reads=[self.r_xres[ti]], writes=[rx], key=("xt", id(rx)))
            loaded[ti] = (xt, rx)

        load(0)
        load(1)
        load(2)
        for ti in range(NT):
            if ti + 3 < NT:
                load(ti + 3)
            xt, rx = loaded.pop(ti)
            g = 0 if ti < NTS else 1

            def dst(pv, rp, ti=ti):
                self.cp(hT3[:, :, ti * 128:(ti + 1) * 128], _h(pv, "p (k t) -> p k t", k=8), reads=[rp],
                        writes=[self.r_hT[ti]], eng="act")
            pipe.push(self.norm_mod_T(xt, rx, g, 1, 0, dst, bufs))
        pipe.flush()
        if "HT" in self.debug and l == 0:
            self.dma(_h(self.scr["HT"], "(k p) t -> p k t", p=128), hT3, reads=self.r_hT, key="dbg")
        self.P.barrier()
        self.A.pop()

    def phase_C(self, l):
        d = self.din
        S = self.scr
        rc = self.r_const
        self.A.push()
        wsl = self.rot(2, 8 * 512, BF16, "wsl")
        brs = self.rot(2, 512, F32, "brep")
        t1 = self.rot(10, 512, F32, "t1")
        t2 = self.rot(3, 512, F32, "t2")
        t3 = self.rot(3, 512, F32, "t3")
        junkr = self.rot(3, 512, F32, "junkc")
        ssm = self.rot(5, 32, F32, "ssm")
        qb = self.rot(3, 512, BF16, "qb")
        stT = self.rot(3, 512, BF16, "stT")
        stv = self.rot(3, 516, BF16, "stv")
        stg = self.rot(3, 512, F32, "stg")
        stgb = self.rot(3, 512, BF16, "stgb")
        stf = self.rot(3, 512, BF16, "stf")
        cs_r = self.rot(6, 128, F32, "cossin")
        gq_rep, rgq = self.alloc(64, F32, "gq")
        gk_rep, rgk = self.alloc(64, F32, "gk")
        bcol, rbc = self.alloc(72, F32, "bcol")
        mmr = Rot([0, 1])
        mmf = Rot([2, 3])
        wslf = self.rot(2, 8 * 512, BF16, "wslf")
        ptr = Rot([4, 5, 6])
        hT3 = _h(self.hT, "p (k t) -> p k t", k=8)
        IF3 = _h(self.IF, "p (c j) -> p c j", j=16)

        self.dma(gq_rep, d["gqa_q_g"][l:l + 1, :].broadcast_to([128, 64]), writes=[rgq], key="gq")
        self.ts(gq_rep, gq_rep, 0.125, ALU.mult, reads=[rgq], writes=[rgq])
        self.dma(gk_rep, d["gqa_k_g"][l:l + 1, :].broadcast_to([128, 64]), writes=[rgk], key="gk")
        self.dma(bcol[:, 0:40], _h(d["b_in"][l, 0:5120], "(j p) -> p j", p=128), writes=[rbc], key="bcol", slow=True)
        self.dma(bcol[:, 40:72], _h(d["b_in"][l, O_GATES:O_GATES + 4096], "(j p) -> p j", p=128), writes=[rbc], key="bcol", slow=True)

        def load_slab(c0, width):
            w, rw = wsl.next()
            self.dma(_h(w, "p (k c) -> p k c", k=8)[:, :, 0:width], _h(d["w_in"][l][:, c0:c0 + width], "(k p) c -> p k c", p=128),
                     writes=[rw], key=("wsl", id(rw)), q="pool")
            return w, rw

        def load_bias(c0, width, scale=None):
            b, rb = brs.next()
            self.dma(b[:, 0:width], d["b_in"][l:l + 1, c0:c0 + width].broadcast_to([128, width]), writes=[rb], key=("brep", id(rb)))
            if scale is not None:
                self.ts(b[:, 0:width], b[:, 0:width], scale, ALU.mult, reads=[rb], writes=[rb])
            return b, rb

        def load_cs(ti):
            t, r = cs_r.next()
            self.dma(t[:, 0:64], d["rope_cos"][ti * 128:(ti + 1) * 128, :], writes=[r], key=("cs", id(r)))
            self.dma(t[:, 64:128], d["rope_sin"][ti * 128:(ti + 1) * 128, :], writes=[r], key=("cs", id(r)))
            return t, r

        def rope(out, x, rx, H, cs, rcs, ro):
            W = H * 64
            a, ra = t2.next()
            b, rb = t3.next()
            x3 = _h(x[:, 0:W], "p (h e) -> p h e", h=H)
            cosb = cs[:, 0:64].unsqueeze(1).broadcast_to([128, H, 64])
            self.tt(_h(a[:, 0:W], "p (h e) -> p h e", h=H), x3, cosb, ALU.mult, reads=[rx, rcs], writes=[ra])
            x5 = _h(x[:, 0:W], "p (h f x j) -> p h f x j", h=H, f=2, x=2)
            b5 = _h(b[:, 0:W], "p (h f x j) -> p h f x j", h=H, f=2, x=2)
            s4 = _h(cs[:, 64:128], "p (f x j) -> p f x j", f=2, x=2)
            for xi in range(2):
                sb_ = s4[:, :, xi, :].unsqueeze(1).broadcast_to([128, H, 2, 16])
                self.tt(b5[:, :, :, xi, :], x5[:, :, :, 1 - xi, :], sb_, ALU.mult, reads=[rx, rcs], writes=[rb], eng="pool")
            self.tt(out[:, 0:W], a[:, 0:W], b[:, 0:W], ALU.add, reads=[ra, rb], writes=[ro])

        def headnorm_g(out, ro, t, rt, H, grep, rg):
            W = H * 64
            ss, rss = ssm.next()
            jk, rjk = junkr.next()
            self.act(jk[:, 0:W], t[:, 0:W], AF.Square, reads=[rt], writes=[rjk])
            yield
            self.red(ss[:, 0:H], _h(jk[:, 0:W], "p (h e) -> p h e", h=H), ALU.add, reads=[rjk], writes=[rss])
            self.rstd(ss[:, 16:16 + H], ss[:, 0:H], 1.0 / 64, reads=[rss], writes=[rss], tmp=ss[:, 8:8 + H])
            yield
            o3 = _h(out[:, 0:W], "p (h e) -> p h e", h=H)
            self.tt(o3, _h(t[:, 0:W], "p (h e) -> p h e", h=H), ss[:, 16:16 + H].unsqueeze(2).broadcast_to([128, H, 64]), ALU.mult,
                    reads=[rt, rss], writes=[ro])
            self.tt(o3, o3, grep.unsqueeze(1).broadcast_to([128, H, 64]), ALU.mult, reads=[ro, rg], writes=[ro])

        def to_fm_g(src, rsrc, W, dst, ti):
            nj = W // 128
            pb = ptr.next()
            pv = self.psb_bf(pb)
            for j in range(nj):
                self.tr(pv[:, j * 128:(j + 1) * 128], src[:, j * 128:(j + 1) * 128], self.identb, reads=[rsrc], writes=[self.r_ps[pb]])
            yield
            sg, rsg = stT.next()
            self.cp(sg[:, 0:W], pv[:, 0:W], reads=[self.r_ps[pb]], writes=[rsg], eng="act")
            self.dma(_h(dst[0:W, ti * 128:(ti + 1) * 128], "(j p) t -> p j t", p=128), _h(sg[:, 0:W], "p (j t) -> p j t", j=nj),
                     reads=[rsg], key=("stT", id(rsg)), q="act")

        def prompt_out(name, kv, ti, c0, width, src, rsrc):
            sq = (ti - NTS) // 2
            p0 = ((ti - NTS) % 2) * 128
            self.dma(self.dout[name][sq, l, kv, p0:p0 + 128, c0:c0 + width], src, reads=[rsrc], key=("po", id(rsrc)))

        def h_na_v(ps, rp, ti, b, rb):
            sv, rsv = stv.next()
            self.tt(sv[:, 0:512], ps, b, ALU.add, reads=[rp, rb], writes=[rsv])
            if ti >= NTS:
                f, rf = t1.next()
                self.tt(f, ps, b, ALU.add, reads=[rp, rb], writes=[rf])
            yield
            self.dma(S["V_na"][ti * 128:(ti + 1) * 128, :], sv[:, 0:512], reads=[rsv], key=("stv", id(rsv)))
            if ti >= NTS:
                prompt_out("o_na", 1, ti, 0, 512, f, rf)

        def h_na_k_prompt(ps, rp, ti, b, rb):
            f, rf = t1.next()
            self.tt(f, ps, b, ALU.add, reads=[rp, rb], writes=[rf])
            yield
            prompt_out("o_na", 0, ti, 0, 512, f, rf)

        def h_g_q(ps, rp, ti, b, rb):
            t, rt = t1.next()
            self.tt(t, ps, b, ALU.add, reads=[rp, rb], writes=[rt])
            if ti < NTS:
                cs, rcs = load_cs(ti)
            n, rn = t1.next()
            yield from headnorm_g(n, rn, t, rt, 8, gq_rep, rgq)
            q, rq = qb.next()
            if ti < NTS:
                rope(q, n, rn, 8, cs, rcs, rq)
            else:
                self.cp(q, n, reads=[rn], writes=[rq])
            yield
            yield from to_fm_g(q, rq, 512, S["QT_g"], ti)

        def kv_common(ps, rp, ti, b, rb, normed, kt_name, v_name, oname):
            t, rt = t1.next()
            self.tt(t[:, 0:256], ps, b[:, 0:256], ALU.add, reads=[rp, rb], writes=[rt])
            if ti < NTS:
                cs, rcs = load_cs(ti)
            sv, rsv = stv.next()
            self.cp(sv[:, 0:128], t[:, 128:256], reads=[rt], writes=[rsv], eng="pool")
            if normed:
                n, rn = t1.next()
                yield from headnorm_g(n, rn, t, rt, 2, gk_rep, rgk)
            else:
                n, rn = t, rt
                yield
            self.dma(S[v_name][ti * 128:(ti + 1) * 128, :], sv[:, 0:128], reads=[rsv], key=("stv", id(rsv)))
            q, rq = qb.next()
            if ti < NTS:
                rope(q, n, rn, 2, cs, rcs, rq)
            else:
                prompt_out(oname, 0, ti, 0, 128, n[:, 0:128], rn)
                prompt_out(oname, 1, ti, 0, 128, t[:, 128:256], rt)
                self.cp(q[:, 0:128], n[:, 0:128], reads=[rn], writes=[rq])
            yield
            yield from to_fm_g(q, rq, 128, S[kt_name], ti)

        def h_g_kv(ps, rp, ti, b, rb):
            yield from kv_common(ps, rp, ti, b, rb, True, "KT_g", "V_g", "o_g")

        def h_s_kv(ps, rp, ti, b, rb):
            yield from kv_common(ps, rp, ti, b, rb, False, "KT_s", "V_s", "o_s")

        def h_s_q(ps, rp, ti, b, rb):
            t, rt = t1.next()
            self.stt(t, ps, 0.125, b, ALU.mult, ALU.add, reads=[rp, rb], writes=[rt])
            if ti < NTS:
                cs, rcs = load_cs(ti)
            yield
            q, rq = qb.next()
            if ti < NTS:
                rope(q, t, rt, 8, cs, rcs, rq)
            else:
                self.cp(q, t, reads=[rt], writes=[rq])
            yield
            yield from to_fm_g(q, rq, 512, S["QT_s"], ti)

        def h_m_k(ps, rp, ti, b, rb):
            f, rf = stf.next()
            self.tt(f, ps, b, ALU.add, reads=[rp, rb], writes=[rf])
            yield
            self.dma(S["K_m"][ti * 128:(ti + 1) * 128, :], f, reads=[rf], key=("stf", id(rf)))

        def h_m_v(ps, rp, ti, b, rb):
            sv, rsv = stv.next()
            s3 = _h(sv[:, 0:516], "p (h e) -> p h e", h=4)
            self.tt(s3[:, :, 0:128], _h(ps, "p (h e) -> p h e", h=4), _h(b, "p (h e) -> p h e", h=4), ALU.add,
                    reads=[rp, rb], writes=[rsv])
            self.memset(s3[:, :, 128:129], 1.0, writes=[rsv], eng="pool")
            yield
            self.dma(S["V_m"][ti * 128:(ti + 1) * 128, :], sv[:, 0:516], reads=[rsv], key=("stv", id(rsv)))

        def h_m_o(ps, rp, ti, b, rb):
            t, rt = t1.next()
            self.tt(t, ps, b, ALU.add, reads=[rp, rb], writes=[rt])
            yield
            g, rg = stg.next()
            self.act(g, t, AF.Sigmoid, reads=[rt], writes=[rg])
            yield
            self.dma(S["O_m"][ti * 128:(ti + 1) * 128, :], g, reads=[rg], key=("stg", id(rg)), q="act")

        def h_m_if(ps, rp, ti, b, rb):
            self.tt(IF3[:, ti, :], ps, b[:, 0:16], ALU.add, reads=[rp, rb], writes=[self.r_IF])
            yield

        fm_groups = [(O_NAQ, 4, "q8", S["QT_na"], 0), (O_NAK, 4, "k", S["KT_na"], 4),
                     (O_MQ, 4, "k", S["QT_m"], 24), (O_MK, 4, "k", S["KT_m"], 28)]
        for i in range(8):
            fm_groups.append((O_GATES + i * 512, 4, "gate", S["GT"][i * 512:(i + 1) * 512, :], 40 + i * 4))
        onlyf = getattr(self, "only_fm", None)

        def fm_stream():
            for gi, (c0, nb, kind, dst, bc0) in enumerate(fm_groups):
                if onlyf is not None and gi not in onlyf:
                    continue
                w, rw = wslf.next()
                self.dma(_h(w, "p (k c) -> p k c", k=8), _h(d["w_in"][l][:, c0:c0 + 512], "(k p) c -> p k c", p=128),
                         writes=[rw], key=("wslf", id(rw)), q="pool")
                for cbl in range(nb):
                    bc = bcol[:, bc0 + cbl:bc0 + cbl + 1]
                    for tc in range(TOK // 512):
                        pb = mmf.next()
                        ps = self.psb(pb)
                        for k in range(8):
                            self.mm(ps, w[:, k * 512 + cbl * 128:k * 512 + (cbl + 1) * 128], hT3[:, k, tc * 512:(tc + 1) * 512],
                                    k == 0, k == 7, reads=[rw] + self.r_hT[tc * 4:(tc + 1) * 4], writes=[self.r_ps[pb]])
                        dsl = dst[cbl * 128:(cbl + 1) * 128, tc * 512:(tc + 1) * 512]
                        if kind == "gate":
                            g, rg = stgb.next()
                            self.act(g, ps, AF.Sigmoid, reads=[self.r_ps[pb], rbc], writes=[rg], bias=bc)
                            self.dma(dsl, g, reads=[rg], key=("stgb", id(rg)), q="act")
                        else:
                            f, rf = stf.next()
                            if kind == "q8":
                                self.ts(f, ps, bc, ALU.add, 0.125, ALU.mult, reads=[self.r_ps[pb], rbc], writes=[rf])
                            else:
                                self.ts(f, ps, bc, ALU.add, reads=[self.r_ps[pb], rbc], writes=[rf])
                            self.dma(dsl, f, reads=[rf], key=("stf", id(rf)))
                        yield

        all_t = list(range(NT))
        tm_groups = [
            (O_NAV, 512, h_na_v, all_t, None),
            (O_GQ, 512, h_g_q, all_t, None),
            (O_GK, 256, h_g_kv, all_t, None),
            (O_SQ, 512, h_s_q, all_t, 0.125),
            (O_SK, 256, h_s_kv, all_t, None),
            (O_MK, 512, h_m_k, all_t, None),
            (O_MV, 512, h_m_v, all_t, None),
            (O_MO, 512, h_m_o, all_t, None),
            (O_MI, 16, h_m_if, all_t, None),
            (O_NAK, 512, h_na_k_prompt, list(range(NTS, NT)), None),
        ]
        only = getattr(self, "only_groups", None)
        fm = fm_stream()
        n_tm = 0
        for gi, (c0, width, handler, tiles, bscale) in enumerate(tm_groups):
            if only is not None and gi not in only:
                continue
            w, rw = load_slab(c0, width)
            b, rb = load_bias(c0, width, bscale)
            if gi == 0:
                self.warm(reads=[rw, rb])
            pipe = Pipe()
            for ti in tiles:
                pb = mmr.next()
                ps = self.psb(pb)[:, 0:width]
                for k in range(8):
                    self.mm(ps, hT3[:, k, ti * 128:(ti + 1) * 128], w[:, k * 512:k * 512 + width], k == 0, k == 7,
                            reads=[rw, self.r_hT[ti]], writes=[self.r_ps[pb]])
                pipe.push(handler(ps, self.r_ps[pb], ti, b, rb))
                n_tm += 1
                for _ in range(2 if n_tm % 3 == 0 else 1):
                    next(fm, None)
            pipe.flush()

        if "IFD" in self.debug and l == 0:
            self.dma(S["IFD"], self.IF, reads=[self.r_IF], key="dbg")

        for _ in fm:
            pass
        self.A.pop()


def _core_inputs(inputs, core, consts, nab):
    f = np.ascontiguousarray
    m = {
        "xs": f(inputs["x_sample"][core]),
        "xp": f(inputs["x_prompt"][2 * core:2 * core + 2].reshape(512, D)),
        "cna": f(inputs["cache_na_kv"][core].reshape(2, 2, 256, 512)),
        "cg": f(inputs["cache_gqa_kv"][core].reshape(2, 2, 256, 128)),
        "cs": f(inputs["cache_swa_kv"][core].reshape(2, 2, 256, 128)),
        "stC": f(inputs["state_mlstm_C"][core]),
        "stn": f(inputs["state_mlstm_n"][core]),
        "stm": f(inputs["state_mlstm_m"][core].reshape(2, 8)),
        "cvec": f(np.stack([inputs["c"][core], inputs["c_ctx"]], 0)),
        "final_norm_g": f(inputs["final_norm_g"].reshape(1, D)),
        "nabias": nab,
    }
    for k in ("w_mod", "b_mod", "norm1_g", "norm2_g", "w_in", "b_in", "gqa_q_g", "gqa_k_g", "swa_sink",
              "mlstm_norm_g", "w_branch", "w_out", "w_ffn_in", "w_ffn_out"):
        m[k] = f(inputs[k])
    m.update(consts)
    return m


def _na_bias_tiles(na_rpb):
    gi = _na_gather_indices()
    out = np.empty((2, 8, NA_NB, 128, 128), np.float32)
    for tid, (dr, dc, valid) in gi.items():
        g = na_rpb[:, :, dr, dc]
        out[:, :, tid] = np.where(valid[None, None], g, np.float32(NEG))
    return out


def _attn_setup(self):
    self.at_srot = Rot([(0, 1), (2, 3)])
    self.at_obanks = (4, 5)
    self.at_pt = self.rot(3, 1024, BF16, "pt")
    self.at_osb = self.rot(3, 1024, F32, "osb")
    self.at_rec = self.rot(4, 512, F32, "rec")
    self.at_tmp = self.rot(4, 512, F32, "dtmp")
    self.at_pending = []


def _attn_job(self, *a, **kw):
    for _ in _attn_job_gen(self, *a, **kw):
        pass


def _attn_job_gen(self, kt, rkt, qt, rqt, nq, kbl, stage, rstage, sinks=None, rsink=None, filler=0, act_recip=True, after=None,
                  obanks=None, ptrot=None):
    oA, oB = obanks if obanks is not None else self.at_obanks
    psoA, psoB = self.psb(oA), self.psb(oB)
    nk = len(kbl)
    pend = None

    def pv(item):
        i, ptile, rpt = item
        kb = kbl[i]
        qlo, qhi = kb["qlo"], kb["qhi"]
        self.mm(psoA[:, qlo:qhi], kb["vx"][0], ptile[:, qlo:qhi], i == 0, i == nk - 1, reads=[rpt, kb["rvx"]], writes=[self.r_ps[oA]])
        self.mm(psoB[:, qlo:qhi], kb["vx"][1], ptile[:, 512 + qlo:512 + qhi], i == 0, i == nk - 1, reads=[rpt, kb["rvx"]],
                writes=[self.r_ps[oB]])
        if filler:
            self.warm(n=filler, bank=6)

    for i, kb in enumerate(kbl):
        banks = self.at_srot.next()
        ptile, rpt = (ptrot or self.at_pt).next()
        qlo, qhi = kb["qlo"], kb["qhi"]
        exs = kb.get("extras", ((), ()))
        for hh in range(2):
            pss = self.psb(banks[hh])
            self.mm(pss[:, qlo:qhi], kt[hh * 64:hh * 64 + 64, kb["kcol"]:kb["kcol"] + 128], qt[hh * 64:hh * 64 + 64, qlo:qhi],
                    True, len(exs[hh]) == 0, reads=[rkt, rqt], writes=[self.r_ps[banks[hh]]])
        for hh in range(2):
            pss = self.psb(banks[hh])
            ex = exs[hh]
            for ei, (col, tile, rtile) in enumerate(ex):
                self.mm(pss[:, col:col + tile.shape[1]], self.identb, tile, False, ei == len(ex) - 1,
                        reads=[rtile], writes=[self.r_ps[banks[hh]]], skip_group_check=True)
        if qlo == 0 and qhi == 512:
            self.act(ptile[:, 0:1024], self.ps_t[:, banks[0]:banks[0] + 2, :], AF.Exp,
                     reads=[self.r_ps[banks[0]], self.r_ps[banks[1]]], writes=[rpt])
        else:
            self.act(_h(ptile, "p (b q) -> p b q", b=2)[:, :, qlo:qhi], self.ps_t[:, banks[0]:banks[0] + 2, qlo:qhi], AF.Exp,
                     reads=[self.r_ps[banks[0]], self.r_ps[banks[1]]], writes=[rpt])
        if pend is not None:
            pv(pend)
        pend = (i, ptile, rpt)
        if i == min(1, nk - 2):
            while self.at_pending:
                self.at_pending.pop(0)()
        yield
    pv(pend)
    osb, rosb = self.at_osb.next()
    self.cp(osb[:, 0:nq], psoA[:, 0:nq], reads=[self.r_ps[oA]], writes=[rosb])
    self.cp(osb[:, 512:512 + nq], psoB[:, 0:nq], reads=[self.r_ps[oB]], writes=[rosb])

    def fin():
      for hh in range(2):
          o_ = osb[:, hh * 512:hh * 512 + nq]
          rec, rrec = self.at_rec.next()
          if sinks is not None:
              tmp, rtmp = self.at_tmp.next()
              self.ts(tmp[64:128, 0:nq], o_[64:128, :], sinks[hh][64:128, :], ALU.add, reads=[rosb, rsink], writes=[rtmp])
              den = tmp[64:128, 0:nq]
              rden = rtmp
          else:
              den = o_[64:128, :]
              rden = rosb
          if act_recip:
              self.act(rec[0:64, 0:nq], den, AF.Ln, reads=[rden], writes=[rrec])
              self.act(rec[0:64, 0:nq], rec[0:64, 0:nq], AF.Exp, reads=[rrec], writes=[rrec], scale=-1.0)
          else:
              self.recip(rec[0:64, 0:nq], den, reads=[rden], writes=[rrec])
          self.tt(stage[hh * 64:hh * 64 + 64, 0:nq], o_[0:64, :], rec[0:64, 0:nq], ALU.mult, reads=[rosb, rrec], writes=[rstage])
      if after is not None:
          after()
    self.at_pending.append(fin)


def _load_ctx_kT(self, src, ncols, dsts, l):
    tmpb, rtb = self.at_ctxb.next()
    self.dma(_h(tmpb[:, 0:256], "p (b e) -> p b e", b=2), _h(src, "(b p) e -> p b e", p=128), writes=[rtb],
             key=("ctxb", id(rtb)), q="pool")
    pb_ = self.at_ptr.next()
    pv = self.psb_bf(pb_)
    for b in range(2):
        self.tr(pv[:, b * 128:(b + 1) * 128], tmpb[:, b * 128:(b + 1) * 128], self.identb, reads=[rtb], writes=[self.r_ps[pb_]])
    for (dst, rdst, moves) in dsts:
        for (dlo, slo) in moves:
            self.cp(_h(dst[dlo:dlo + 64, 0:256], "p (b t) -> p b t", b=2), _h(pv[slo:slo + 64, 0:256], "p (b t) -> p b t", b=2),
                    reads=[self.r_ps[pb_]], writes=[rdst], eng="act")


def _phase_D(self, l):
    d = self.din
    S = self.scr
    self.A.push()
    _attn_setup(self)
    self.at_ctxb = self.rot(2, 256, BF16, "ctxb")
    self.at_ptr = Rot([6, 7])
    sinkexp, rsk = self.alloc(8, F32, "sinkexp")
    self.dma(sinkexp, d["swa_sink"][l:l + 1, :].broadcast_to([128, 8]), writes=[rsk], key="sink")
    self.act(sinkexp, sinkexp, AF.Exp, reads=[rsk], writes=[rsk])
    which = getattr(self, "only_attn", ("prompt", "gqa", "swa", "na"))

    def prompt_stream(obanks, ptrot, act_recip):
        ktp = self.rot(3, 256, BF16, "ktp")
        qtp = self.rot(3, 256, BF16, "qtp")
        vxp = self.rot(3, 2 * 2 * 128, BF16, "vxp")
        for (vx, rvx) in vxp.items:
            self.memset(vx, 1.0, writes=[rvx])
        stg = self.rot(3, 256, BF16, "stgp")
        for sq in range(2):
            c0 = 4096 + sq * 256
            for (bi, qn, kn, vn, nkv) in ((0, "QT_na", "KT_na", "V_na", 8), (1, "QT_g", "KT_g", "V_g", 2), (2, "QT_s", "KT_s", "V_s", 2)):
                for hp in range(4):
                    qt, rqt = qtp.next()
                    self.dma(qt, S[qn][hp * 128:(hp + 1) * 128, c0:c0 + 256], writes=[rqt], key=("qtp", id(rqt)))
                    kt, rkt = ktp.next()
                    vx, rvx = vxp.next()
                    vx4 = _h(vx, "p (b h e) -> p b h e", b=2, h=2)
                    if nkv == 8:
                        self.dma(kt, S[kn][hp * 128:(hp + 1) * 128, c0:c0 + 256], writes=[rkt], key=("ktp", id(rkt)))
                        for hh in range(2):
                            self.dma(vx4[:, :, hh, 0:64], _h(S[vn][c0:c0 + 256, hp * 128 + hh * 64:hp * 128 + (hh + 1) * 64], "(b p) e -> p b e", p=128),
                                     writes=[rvx], key=("vxp", id(rvx)))
                    else:
                        j = hp // 2
                        for half in range(2):
                            self.dma(kt[half * 64:(half + 1) * 64, :], S[kn][j * 64:(j + 1) * 64, c0:c0 + 256], writes=[rkt], key=("ktp", id(rkt)))
                            self.dma(vx4[:, :, half, 0:64], _h(S[vn][c0:c0 + 256, j * 64:(j + 1) * 64], "(b p) e -> p b e", p=128),
                                     writes=[rvx], key=("vxp", id(rvx)))
                    st_, rst = stg.next()
                    kbl = [dict(kcol=kb * 128, vx=(vx4[:, kb, 0, :], vx4[:, kb, 1, :]), rvx=rvx, qlo=0, qhi=256) for kb in range(2)]

                    def after(st_=st_, rst=rst, bi=bi, hp=hp, c0=c0):
                        self.dma(S["OT"][bi, hp * 128:(hp + 1) * 128, c0:c0 + 256], st_, reads=[rst], key=("stgp", id(rst)))
                    sk = (sinkexp[:, 2 * hp:2 * hp + 1], sinkexp[:, 2 * hp + 1:2 * hp + 2]) if bi == 2 else None
                    yield from _attn_job_gen(self, kt, rkt, qt, rqt, 256, kbl, st_, rst, sinks=sk, rsink=rsk, after=after,
                                             obanks=obanks, ptrot=ptrot, act_recip=act_recip)

    if "prompt" in which and "gqa" not in which:
        self.A.push()
        self.warm(bank=6)
        for _ in prompt_stream(None, None, True):
            pass
        while self.at_pending:
            self.at_pending.pop(0)()
        self.P.barrier()
        self.A.pop()

    def na_stream(obanks, ptrot, act_recip, depth, alias_res, warm_bank):
        ar = list(alias_res)
        ktn = self.rot(1 if depth == 1 else 2, 4352, BF16, "ktn")
        qtn = self.rot(2, 4096, BF16, "qtn")
        vxn = self.rot(depth, 34 * 2 * 128, BF16, "vxn")
        for (vx, rvx) in vxn.items:
            self.memset(vx, 1.0, writes=[rvx] + ar)
        bia = self.rot(depth, 2 * NA_NB * 128, BF16, "nabias")
        stg = self.rot(depth, 4096, BF16, "stgn")
        for hp in range(4):
            kt, rkt = ktn.next()
            qt, rqt = qtn.next()
            vx, rvx = vxn.next()
            bt, rbt = bia.next()
            vx4 = _h(vx, "p (b h e) -> p b h e", b=34, h=2)
            bt4 = _h(bt, "p (h n q) -> p h n q", h=2, n=NA_NB)
            self.dma(kt[:, 256:4352], S["KT_na"][hp * 128:(hp + 1) * 128, 0:4096], writes=[rkt] + ar, key=("ktn", id(rkt)))
            _load_ctx_kT(self, d["cna"][l, 0][:, hp * 128:(hp + 1) * 128], 128, [(kt, rkt, [(0, 0), (64, 64)])], l)
            self.dma(qt, S["QT_na"][hp * 128:(hp + 1) * 128, 0:4096], writes=[rqt] + ar, key=("qtn", id(rqt)))
            for hh in range(2):
                c_lo = hp * 128 + hh * 64
                self.dma(vx4[:, 0:2, hh, 0:64], _h(d["cna"][l, 1][:, c_lo:c_lo + 64], "(b p) e -> p b e", p=128),
                         writes=[rvx], key=("vxn", id(rvx)), q="pool")
                for q4 in range(4):
                    self.dma(vx4[:, 2 + q4 * 8:2 + (q4 + 1) * 8, hh, 0:64],
                             _h(S["V_na"][q4 * 1024:(q4 + 1) * 1024, c_lo:c_lo + 64], "(b p) e -> p b e", p=128),
                             writes=[rvx], key=("vxn", id(rvx)))
            for hh in range(2):
                self.dma(bt4[:, hh, :, :], _h(d["nabias"][l, hp * 2 + hh], "n k q -> k n q"), writes=[rbt] + ar, key=("nab", id(rbt)), q="pool")
            st_, rst = stg.next()
            if hp == 0 and ar:
                self.memset(st_[:, 0:2], 0.0, writes=[rst] + ar)
            if warm_bank is not None:
                self.warm(reads=[rkt, rqt, rvx, rbt], bank=warm_bank)
            for qc in range(8):
                tiles = [4 * qc + i for i in range(4)]
                kbl = [dict(kcol=0, vx=(vx4[:, 0, 0, :], vx4[:, 0, 1, :]), rvx=rvx, qlo=0, qhi=512)]
                jbs = sorted(set(jb for a in tiles for jb in _na_kbs(a)))
                for jb in jbs:
                    as_ = [a for a in tiles if jb in _na_kbs(a)]
                    ids = [NA_TILE_IDX[(_na_class(a), jb - a)] for a in as_]
                    runs = []
                    for a, tid in zip(as_, ids):
                        if runs and runs[-1][1] + runs[-1][2] == tid:
                            runs[-1][2] += 1
                        else:
                            runs.append([a, tid, 1])
                    exs = tuple([((a0 - 4 * qc) * 128, _h(bt4[:, hh, t0:t0 + n, :], "p n q -> p (n q)"), rbt) for (a0, t0, n) in runs]
                                for hh in range(2))
                    kbl.append(dict(kcol=256 + jb * 128, vx=(vx4[:, 2 + jb, 0, :], vx4[:, 2 + jb, 1, :]), rvx=rvx,
                                    qlo=(min(as_) - 4 * qc) * 128, qhi=(max(as_) + 1 - 4 * qc) * 128, extras=exs))
                kbl.append(dict(kcol=128, vx=(vx4[:, 1, 0, :], vx4[:, 1, 1, :]), rvx=rvx, qlo=0, qhi=512))
                aft = None
                if qc == 7:
                    def aft(st_=st_, rst=rst, hp=hp):
                        self.dma(S["OT"][0, hp * 128:(hp + 1) * 128, 0:4096], st_, reads=[rst], key=("stgn", id(rst)))
                yield from _attn_job_gen(self, kt, rkt, qt[:, qc * 512:(qc + 1) * 512], rqt, 512, kbl,
                                         st_[:, qc * 512:(qc + 1) * 512], rst, after=aft, obanks=obanks, ptrot=ptrot,
                                         act_recip=act_recip)

    mixed_na = ("na" in which and "gqa" in which and "swa" in which and getattr(self, "mix_na", True))

    if "gqa" in which or "swa" in which:
        self.A.push()
        mix = {}
        swa_top0 = swa_top1 = None
        swa_res = []
        for (name, bi, qn, kn, vn, cn) in (("gqa", 1, "QT_g", "KT_g", "V_g", "cg"), ("swa", 2, "QT_s", "KT_s", "V_s", "cs")):
            if name not in which:
                continue
            if name == "swa":
                swa_top0 = self.A.top
            ktd = [self.alloc(4352, BF16, "ktd%d" % j) for j in range(2)]
            vxs = [self.alloc(34 * 128, BF16, "vx%d" % j) for j in range(2)]
            qts = self.rot(2, 4096, BF16, "qts")
            stg = self.rot(2, 4096, BF16, "stgs")
            _load_ctx_kT(self, d[cn][l, 0], 128, [(ktd[0][0], ktd[0][1], [(0, 0), (64, 0)]), (ktd[1][0], ktd[1][1], [(0, 64), (64, 64)])], l)
            for j in range(2):
                kt, rkt = ktd[j]
                vx, rvx = vxs[j]
                for half in range(2):
                    self.dma(kt[half * 64:(half + 1) * 64, 256:4352], S[kn][j * 64:(j + 1) * 64, 0:4096], writes=[rkt], key=("ktd", id(rkt)))
                self.memset(vx, 1.0, writes=[rvx])
                vx3 = _h(vx, "p (b e) -> p b e", b=34)
                self.dma(vx3[:, 0:2, 0:64], _h(d[cn][l, 1][:, j * 64:(j + 1) * 64], "(b p) e -> p b e", p=128), writes=[rvx],
                         key=("vxs", id(rvx)), q="pool")
                for q4 in range(4):
                    self.dma(vx3[:, 2 + q4 * 8:2 + (q4 + 1) * 8, 0:64],
                             _h(S[vn][q4 * 1024:(q4 + 1) * 1024, j * 64:(j + 1) * 64], "(b p) e -> p b e", p=128), writes=[rvx], key=("vxs", id(rvx)))
            mix[name] = (bi, qn, ktd, vxs, qts, stg)
            if name == "swa":
                swa_top1 = self.A.top
                swa_res = [r for (_, r) in ktd] + [r for (_, r) in vxs] + [r for (_, r) in qts.items] + [r for (_, r) in stg.items]

        ptrots = {"gqa": self.at_pt, "swa": self.rot(3, 1024, BF16, "pt_swa")}

        def stream(name, obanks, filler):
            bi, qn, ktd, vxs, qts, stg = mix[name]
            for hp in range(4):
                j = hp // 2
                kt, rkt = ktd[j]
                vx, rvx = vxs[j]
                vx3 = _h(vx, "p (b e) -> p b e", b=34)
                qt, rqt = qts.next()
                self.dma(qt, S[qn][hp * 128:(hp + 1) * 128, 0:4096], writes=[rqt], key=("qts", id(rqt)))
                st_, rst = stg.next()
                if name == "gqa":
                    self.warm(reads=[rkt, rqt, rvx], bank=6 if "swa" not in mix else obanks[0])
                for qc in range(8):
                    if name == "gqa":
                        kbl = [dict(kcol=kb * 128, vx=(vx3[:, kb, :], vx3[:, kb, :]), rvx=rvx, qlo=0, qhi=512) for kb in range(34)]
                        sk = None
                    else:
                        kbl = [dict(kcol=0, vx=(vx3[:, 0, :], vx3[:, 0, :]), rvx=rvx, qlo=0, qhi=512)]
                        for jb in range(4 * qc - 1, 4 * qc + 5):
                            if jb < 0 or jb > 31:
                                continue
                            qbs = [qb for qb in (jb - 1, jb, jb + 1) if 4 * qc <= qb <= 4 * qc + 3]
                            ex = []
                            if jb + 1 in qbs:
                                ex.append(((jb + 1 - 4 * qc) * 128, self.swam[0], self.r_const))
                            if jb - 1 in qbs:
                                ex.append(((jb - 1 - 4 * qc) * 128, self.swam[1], self.r_const))
                            kbl.append(dict(kcol=256 + jb * 128, vx=(vx3[:, 2 + jb, :], vx3[:, 2 + jb, :]), rvx=rvx,
                                            qlo=(min(qbs) - 4 * qc) * 128, qhi=(max(qbs) + 1 - 4 * qc) * 128, extras=(ex, ex)))
                        kbl.append(dict(kcol=128, vx=(vx3[:, 1, :], vx3[:, 1, :]), rvx=rvx, qlo=0, qhi=512))
                        sk = (sinkexp[:, 2 * hp:2 * hp + 1], sinkexp[:, 2 * hp + 1:2 * hp + 2])
                    aft = None
                    if qc == 7:
                        def aft(st_=st_, rst=rst, hp=hp, bi=bi):
                            self.dma(S["OT"][bi, hp * 128:(hp + 1) * 128, 0:4096], st_, reads=[rst], key=("stgs", id(rst)))
                    yield from _attn_job_gen(self, kt, rkt, qt[:, qc * 512:(qc + 1) * 512], rqt, 512, kbl,
                                             st_[:, qc * 512:(qc + 1) * 512], rst, sinks=sk, rsink=rsk, filler=filler,
                                             act_recip=False, after=aft, obanks=obanks, ptrot=ptrots[name])

        if "gqa" in mix and "swa" in mix:
            g_ = stream("gqa", (4, 5), getattr(self, "mix_filler", 0))

            def second():
                if "prompt" in which:
                    yield from prompt_stream((6, 7), ptrots["swa"], False)
                yield from stream("swa", (6, 7), 0)
                if mixed_na:
                    while self.at_pending:
                        self.at_pending.pop(0)()
                    keep = self.A.top
                    self.A.top = swa_top0
                    gen = na_stream((6, 7), ptrots["swa"], False, 1, swa_res, None)
                    first = True
                    for _ in gen:
                        if first:
                            assert self.A.top <= swa_top1, (self.A.top, swa_top1)
                            self.A.top = keep
                            first = False
                        yield
            w_ = second()
            n_g = 0
            ratio = getattr(self, "mix_ratio", 2 if mixed_na else (3 if "prompt" in which else 4))
            w_alive = True
            for _ in g_:
                n_g += 1
                if w_alive and n_g % ratio == 0:
                    if next(w_, "done") == "done":
                        w_alive = False
            if w_alive:
                for _ in w_:
                    pass
        else:
            nm = "gqa" if "gqa" in mix else "swa"
            for _ in stream(nm, (4, 5), 1 if nm == "gqa" else 0):
                pass
        while self.at_pending:
            self.at_pending.pop(0)()
        self.P.barrier()
        self.A.pop()

    if "na" in which and not mixed_na:
        self.A.push()
        for _ in na_stream(None, None, True, 2, [], 6):
            pass
        while self.at_pending:
            self.at_pending.pop(0)()
        self.P.barrier()
        self.A.pop()
    self.A.pop()


KB.phase_D = _phase_D


def _phase_E(self, l):
    d = self.din
    S = self.scr
    rc = self.r_const
    self.A.push()
    IF3 = _h(self.IF, "p (c j) -> p c j", j=16)
    NC4 = NT * 4
    lnk, rlnk = self.alloc(1, F32, "lnk")
    self.memset(lnk, -0.5 * math.log(128.0), writes=[rlnk])
    e1, re1 = self.alloc(NT * 8, F32, "e1")
    e13 = _h(e1, "p (c j) -> p c j", j=8)
    self.act(e13, IF3[:, :, 8:16], AF.Exp, reads=[self.r_IF], writes=[re1], scale=-1.0)
    self.act(e1, e1, AF.Ln, reads=[re1], writes=[re1], bias=self.one_col)
    nb, ntot, uu, eb, EG, ww, ug = [], [], [], [], [], [], []
    for dr in range(2):
        lfd, rl = self.alloc(NC4, F32, "lfd")
        self.cp(_h(lfd, "p (c j) -> p c j", j=4), e13[:, :, dr * 4:(dr + 1) * 4], reads=[re1], writes=[rl])
        self.mm(self.psb(dr)[:, 0:NC4], self.tri[dr], lfd, True, True, reads=[rl], writes=[self.r_ps[dr]])
        self.mm(self.psb(2 + dr)[:, 0:NC4], self.tri[2], lfd, True, True, reads=[rl], writes=[self.r_ps[2 + dr]])
        t_nb, r_nb = self.alloc(NC4, F32, "nb")
        t_nt, r_nt = self.alloc(NC4, F32, "ntot")
        self.cp(t_nb, self.psb(dr)[:, 0:NC4], reads=[self.r_ps[dr]], writes=[r_nb])
        self.cp(t_nt, self.psb(2 + dr)[:, 0:NC4], reads=[self.r_ps[2 + dr]], writes=[r_nt])
        dd, rdd = self.alloc(NC4, F32, "dd")
        self.tt(_h(dd, "p (c j) -> p c j", j=4), IF3[:, :, dr * 4:(dr + 1) * 4], _h(t_nb, "p (c j) -> p c j", j=4), ALU.add,
                reads=[self.r_IF, r_nb], writes=[rdd])
        t_u, r_u = self.alloc(NC4, F32, "u")
        self.act(t_u, dd, AF.Exp, reads=[rdd, rlnk], writes=[r_u], bias=lnk)
        t_eb, r_eb = self.alloc(NC4, F32, "eb")
        self.act(t_eb, t_nb, AF.Exp, reads=[r_nb], writes=[r_eb], scale=-1.0)
        t_eg, r_eg = self.alloc(NC4, F32, "EG")
        self.act(t_eg, t_nt, AF.Exp, reads=[r_nt], writes=[r_eg], scale=-1.0)
        t_w, r_w = self.alloc(NC4, F32, "w")
        self.tt(t_w, dd, t_nt, ALU.subtract, reads=[rdd, r_nt], writes=[r_w])
        t_ug, r_ug = self.alloc(NC4, F32, "ug")
        self.tt(t_ug, t_u, t_eg, ALU.mult, reads=[r_u, r_eg], writes=[r_ug])
        ug.append((t_ug, r_ug))
        nb.append((t_nb, r_nb)); ntot.append((t_nt, r_nt)); uu.append((t_u, r_u)); eb.append((t_eb, r_eb))
        EG.append((t_eg, r_eg)); ww.append((t_w, r_w))
    em0, rem0 = self.alloc(8, F32, "em0")
    self.dma(em0, d["stm"][l:l + 1, :].broadcast_to([128, 8]), writes=[rem0], key="em0")
    self.act(em0, em0, AF.Exp, reads=[rem0], writes=[rem0])
    gml, rgml = self.alloc(512, F32, "gml")
    self.dma(gml, d["mlstm_norm_g"][l:l + 1, :].broadcast_to([128, 512]), writes=[rgml], key="gml")
    self.P.barrier()

    Cst, rC = self.alloc(4 * 129, F32, "Cst")
    Cn, rCn = self.alloc(4 * 129, F32, "Cn")
    Cbf, rCb = self.alloc(4 * 129, BF16, "Cbf")
    qTr = self.rot(4, 512, BF16, "qTc")
    kTr = self.rot(4, 512, BF16, "kTc")
    kr = self.rot(4, 512, BF16, "kc")
    vr = self.rot(4, 516, BF16, "vc")
    hfr = self.rot(4, 512, F32, "hfc")
    ocr = self.rot(5, 512, F32, "oc")
    sTr = self.rot(3, 512, BF16, "sT")
    ktr = self.rot(3, 512, BF16, "kt")
    hst = self.rot(3, 512, F32, "hst")
    yr = self.rot(3, 512, F32, "y")
    ybr = self.rot(2, 512, BF16, "yb")
    stT = self.rot(2, 512, BF16, "stTm")
    sm = self.rot(6, 32, F32, "smm")
    junkr = self.rot(2, 128, F32, "junkm")
    psS = Rot([0, 1])
    ptr = Rot([6, 7])
    fin, rfin = self.alloc(64, F32, "fin")
    fco, rfco = self.alloc(4 * 129, F32, "fco")

    def Nv(h):
        return self.psb(2 + h // 2)[:, (h % 2) * 129:(h % 2) * 129 + 129], self.r_ps[2 + h // 2]

    def Uv(h):
        return self.psb(4 + h // 2)[:, (h % 2) * 129:(h % 2) * 129 + 129], self.r_ps[4 + h // 2]

    def run_pass(dr, tiles, sample, sq):
        t_u, r_u = uu[dr]
        t_eb, r_eb = eb[dr]
        t_eg, r_eg = EG[dr]
        if sample:
            C3 = _h(Cst, "p (h e) -> p h e", h=4)
            for h in range(4):
                self.dma(C3[:, h, 0:128], d["stC"][l, dr, h], writes=[rC], key="cinit")
                self.dma(C3[:, h, 128:129], _h(d["stn"][l, dr, h], "(p o) -> p o", o=1), writes=[rC], key="cinit", slow=True)
            for h in range(4):
                self.ts(C3[:, h, :], C3[:, h, :], em0[:, dr * 4 + h:dr * 4 + h + 1], ALU.mult, reads=[rC, rem0], writes=[rC])
        else:
            self.memset(Cst, 0.0, writes=[rC])
        self.cp(Cbf, Cst, reads=[rC], writes=[rCb], eng="act")
        loaded = {}

        def load(ti):
            qT, rq = qTr.next(); kT, rk = kTr.next(); kc, rkc = kr.next(); vc, rv = vr.next()
            cs = slice(ti * 128, (ti + 1) * 128)
            self.dma(_h(qT, "p (h t) -> p h t", h=4), _h(S["QT_m"][:, cs], "(h p) t -> p h t", p=128), writes=[rq], key=("qTc", id(rq)))
            self.dma(_h(kT, "p (h t) -> p h t", h=4), _h(S["KT_m"][:, cs], "(h p) t -> p h t", p=128), writes=[rk], key=("kTc", id(rk)))
            self.dma(kc, S["K_m"][cs, :], writes=[rkc], key=("kc", id(rkc)))
            self.dma(vc, S["V_m"][cs, :], writes=[rv], key=("vc", id(rv)))
            ex = None
            if dr == 1:
                hf, rhf = hfr.next(); oc, roc = ocr.next()
                self.dma(hf, S["HF"][cs, :], writes=[rhf], key=("hfc", id(rhf)))
                self.dma(oc, S["O_m"][cs, :], writes=[roc], key=("oc", id(roc)))
                ex = (hf, rhf, oc, roc)
            loaded[ti] = (qT, rq, kT, rk, kc, rkc, vc, rv, ex)

        t_ug, r_ug = ug[dr]

        def chunk(ti):
            qT, rq, kT, rk, kc, rkc, vc, rv, ex = loaded.pop(ti)
            sb = psS.next()
            pS = self.psb(sb)
            for h in range(4):
                self.mm(pS[:, h * 128:(h + 1) * 128], kT[:, h * 128:(h + 1) * 128], qT[:, h * 128:(h + 1) * 128], True, True,
                        reads=[rk, rq], writes=[self.r_ps[sb]])
            sT, rsT = sTr.next()
            kt, rkt = ktr.next()
            for h in range(4):
                ucol = t_u[:, ti * 4 + h:ti * 4 + h + 1]
                self.stt(sT[:, h * 128:(h + 1) * 128], pS[:, h * 128:(h + 1) * 128], ucol, self.trib[dr], ALU.mult, ALU.mult,
                         reads=[self.r_ps[sb], r_u], writes=[rsT])
                self.act(kt[:, h * 128:(h + 1) * 128], kc[:, h * 128:(h + 1) * 128], AF.Copy, reads=[rkc, r_ug], writes=[rkt],
                         scale=t_ug[:, ti * 4 + h:ti * 4 + h + 1])
            yield
            for h in range(4):
                nv, rn = Nv(h)
                self.mm(nv, sT[:, h * 128:(h + 1) * 128], vc[:, h * 129:(h + 1) * 129], True, False, reads=[rsT, rv], writes=[rn])
                self.mm(nv, qT[:, h * 128:(h + 1) * 128], Cbf[:, h * 129:(h + 1) * 129], False, True, reads=[rq, rCb], writes=[rn])
            for h in range(4):
                uv, ru = Uv(h)
                self.mm(uv, kt[:, h * 128:(h + 1) * 128], vc[:, h * 129:(h + 1) * 129], True, True, reads=[rkt, rv], writes=[ru])
            for h in range(4):
                uv, ru = Uv(h)
                gcol = t_eg[:, ti * 4 + h:ti * 4 + h + 1]
                cs_ = slice(h * 129, (h + 1) * 129)
                self.stt(Cst[:, cs_], Cst[:, cs_], gcol, uv, ALU.mult, ALU.add, reads=[ru, r_eg, rC], writes=[rC])
            self.cp(Cbf, Cst, reads=[rC], writes=[rCb], eng="act")
            s_, rs_ = sm.next()
            for hp in range(2):
                dv = _h(self.psb(2 + hp)[:, 0:258], "p (h e) -> p h e", e=129)[:, :, 128]
                self.tt(s_[:, 2 * hp:2 * hp + 2], dv, t_eb[:, ti * 4 + 2 * hp:ti * 4 + 2 * hp + 2], ALU.mult,
                        reads=[self.r_ps[2 + hp], r_eb], writes=[rs_])
            self.ts(s_[:, 4:8], s_[:, 0:4], -1.0, ALU.mult, reads=[rs_], writes=[rs_])
            self.stt(s_[:, 8:12], s_[:, 0:4], 1.0, s_[:, 4:8], ALU.max, ALU.max, reads=[rs_], writes=[rs_])
            self.recip(s_[:, 12:16], s_[:, 8:12], reads=[rs_], writes=[rs_])
            self.tt(s_[:, 16:20], s_[:, 12:16], t_eb[:, ti * 4:ti * 4 + 4], ALU.mult, reads=[rs_, r_eb], writes=[rs_])
            hs, rhs = hst.next()
            for h in range(4):
                nv, rn = Nv(h)
                self.act(hs[:, h * 128:(h + 1) * 128], nv[:, 0:128], AF.Copy, reads=[rn, rs_], writes=[rhs], scale=s_[:, 16 + h:17 + h])
            if dr == 1:
                self.tt(hs, hs, ex[0], ALU.add, reads=[rhs, ex[1]], writes=[rhs])
            if dr == 0:
                self.dma(S["HF"][ti * 128:(ti + 1) * 128, :], hs, reads=[rhs], key=("hst", id(rhs)), q="act")
                return
            yield
            y, ry = yr.next()
            self.tt(y, hs, ex[2], ALU.mult, reads=[rhs, ex[3]], writes=[ry])
            s2, rs2 = sm.next()
            jk, rjk = junkr.next()
            for h in range(4):
                self.act(jk, y[:, h * 128:(h + 1) * 128], AF.Square, reads=[ry], writes=[rjk, rs2], accum_out=s2[:, h:h + 1])
            self.rstd(s2[:, 8:12], s2[:, 0:4], 1.0 / 128, reads=[rs2], writes=[rs2], tmp=s2[:, 4:8])
            yield
            y3 = _h(y, "p (h e) -> p h e", h=4)
            self.tt(y3, y3, s2[:, 8:12].unsqueeze(2).broadcast_to([128, 4, 128]), ALU.mult, reads=[ry, rs2], writes=[ry])
            yb, ryb = ybr.next()
            self.tt(yb, y, gml, ALU.mult, reads=[ry, rgml], writes=[ryb])
            pb_ = ptr.next()
            pv = self.psb_bf(pb_)
            for j in range(4):
                self.tr(pv[:, j * 128:(j + 1) * 128], yb[:, j * 128:(j + 1) * 128], self.identb, reads=[ryb], writes=[self.r_ps[pb_]])
            yield
            sg, rsg = stT.next()
            self.cp(sg, pv[:, 0:512], reads=[self.r_ps[pb_]], writes=[rsg], eng="act")
            self.dma(_h(S["OT"][3][:, ti * 128:(ti + 1) * 128], "(j p) t -> p j t", p=128), _h(sg, "p (j t) -> p j t", j=4),
                     reads=[rsg], key=("stTm", id(rsg)), q="act")

        pipe = Pipe()
        load(tiles[0])
        if len(tiles) > 1:
            load(tiles[1])
        for idx, ti in enumerate(tiles):
            if idx + 2 < len(tiles):
                load(tiles[idx + 2])
            pipe.push(chunk(ti))
        pipe.flush()
        if not sample:
            t_w, r_w = ww[dr]
            t_nt, r_nt = ntot[dr]
            c0, c1 = tiles[0], tiles[1]
            self.cp(fin[:, 0:4], t_w[:, c0 * 4:c0 * 4 + 4], reads=[r_w], writes=[rfin])
            self.cp(fin[:, 4:8], t_w[:, c1 * 4:c1 * 4 + 4], reads=[r_w], writes=[rfin])
            self.ts(fin[:, 8:12], t_nt[:, c0 * 4:c0 * 4 + 4], -1.0, ALU.mult, reads=[r_nt], writes=[rfin])
            self.ts(fin[:, 12:16], t_nt[:, c1 * 4:c1 * 4 + 4], -1.0, ALU.mult, reads=[r_nt], writes=[rfin])
            pb_ = ptr.next()
            pf = self.psb(pb_)
            self.tr(pf[0:16, 0:128], fin[:, 0:16], self.identf, reads=[rfin], writes=[self.r_ps[pb_]])
            self.red(fin[0:16, 16:17], pf[0:16, 0:128], ALU.max, reads=[self.r_ps[pb_]], writes=[rfin])
            self.tr(pf[0:1, 128:144], fin[0:16, 16:17], self.identf[0:16, 0:16], reads=[rfin], writes=[self.r_ps[pb_]])
            row = fin[0:1, 32:48]
            self.cp(row, pf[0:1, 128:144], reads=[self.r_ps[pb_]], writes=[rfin])
            self.tt(fin[0:1, 48:52], fin[0:1, 40:44], fin[0:1, 44:48], ALU.add, reads=[rfin], writes=[rfin])
            self.tt(fin[0:1, 52:56], fin[0:1, 44:48], fin[0:1, 32:36], ALU.add, reads=[rfin], writes=[rfin])
            self.tt(fin[0:1, 48:52], fin[0:1, 48:52], fin[0:1, 52:56], ALU.max, reads=[rfin], writes=[rfin])
            self.tt(fin[0:1, 48:52], fin[0:1, 48:52], fin[0:1, 36:40], ALU.max, reads=[rfin], writes=[rfin])
            self.dma(self.dout["o_m"][sq, l, dr:dr + 1, :], fin[0:1, 48:52], reads=[rfin], key="fin")
            self.mm(pf[:, 256:260], self.tri[2][0:1, :], fin[0:1, 48:52], True, True, reads=[rfin], writes=[self.r_ps[pb_]])
            self.act(fin[:, 56:60], pf[:, 256:260], AF.Exp, reads=[self.r_ps[pb_]], writes=[rfin], scale=-1.0)
            for h in range(4):
                self.ts(fco[:, h * 129:(h + 1) * 129], Cst[:, h * 129:(h + 1) * 129], fin[:, 56 + h:57 + h], ALU.mult,
                        reads=[rC, rfin], writes=[rfco])
            f3 = _h(fco, "p (h e) -> p h e", h=4)
            for h in range(4):
                self.dma(self.dout["o_C"][sq, l, dr, h], f3[:, h, 0:128], reads=[rfco], key="fco")
                self.dma(_h(self.dout["o_n"][sq, l, dr, h], "(p o) -> p o", o=1), f3[:, h, 128:129], reads=[rfco], key="fco", slow=True)

    seqs = [(True, list(range(0, NTS)), None), (False, [NTS, NTS + 1], 0), (False, [NTS + 2, NTS + 3], 1)]
    for dr in range(2):
        for (sample, tiles, sq) in seqs:
            run_pass(dr, tiles if dr == 0 else tiles[::-1], sample, sq)
        self.P.barrier()
    self.A.pop()


KB.phase_E = _phase_E


def _phase_F(self, l):
    d = self.din
    S = self.scr
    self.A.push()
    wb, rwb = self.alloc(4 * 4 * D, BF16, "wb")
    wb4 = _h(wb, "p (i k c) -> p i k c", i=4, k=4)
    for i in range(4):
        self.dma(wb4[:, i, :, :], _h(d["w_branch"][l, i], "(k p) c -> p k c", p=128), writes=[rwb], key="wb", q="pool")
    wo, rwo = self.alloc(8 * D, BF16, "wo")
    wo3 = _h(wo, "p (k c) -> p k c", k=8)
    self.dma(wo3, _h(d["w_out"][l], "(k p) c -> p k c", p=128), writes=[rwo], key="wo", q="pool")
    otr = self.rot(6, 4 * 512, BF16, "ot")
    gtr = self.rot(4, 512, BF16, "gt")
    gbr = self.rot(6, 512, BF16, "gated")
    tmpr = self.rot(3, 512, F32, "tmpf")
    mTr = self.rot(2, 8 * 512, BF16, "mT")
    self.load_mod((2, 3, 4))
    xr = self.rot(2, D, F32, "xF")
    x1r = self.rot(4, D, F32, "x1F")
    bufs = {"junk": self.rot(2, D, BF16, "junkF"), "ss": self.rot(4, 4, F32, "ssF"), "t": self.rot(2, D, F32, "tF"),
            "hb": self.rot(3, D, BF16, "hbF"), "pt": Rot([6, 7])}
    h2s = self.rot(2, D, BF16, "h2s")
    mmr = Rot([0, 1])
    mmo = Rot([2, 3])
    accb = Rot([4, 5])
    pipe = Pipe()
    pending = []

    def tile_part2(tc, tt_, mT3_, rmT_):
        ti = tc * 4 + tt_
        g = 0 if ti < NTS else 1
        xt, rx = xr.next()
        src = self.tile_rows(ti) if l == 0 else S["XRES"][ti * 128:(ti + 1) * 128, :]
        self.dma(xt, src, reads=[self.r_xres[ti]], writes=[rx], key=("xF", id(rx)))
        x1, rx1 = x1r.next()
        for nh in range(2):
            hs = slice(nh * 512, (nh + 1) * 512)
            pb = mmo.next()
            ps = self.psb(pb)
            for k in range(8):
                self.mm(ps, mT3_[:, k, tt_ * 128:(tt_ + 1) * 128], wo3[:, k, hs], k == 0, k == 7, reads=[rmT_, rwo], writes=[self.r_ps[pb]])
            tmp, rtmp = tmpr.next()
            self.tt(tmp, ps, self.mod(g, 2)[:, hs], ALU.mult, reads=[self.r_ps[pb], self.r_modt], writes=[rtmp])
            self.tt(x1[:, hs], tmp, xt[:, hs], ALU.add, reads=[rtmp, rx], writes=[rx1], eng="pool")
        self.dma(S["XRES"][ti * 128:(ti + 1) * 128, :], x1, reads=[rx1], writes=[self.r_xres[ti]], key=("x1F", id(rx1)), q="pool")

        def dst(pv, rp, ti=ti):
            sg, rsg = h2s.next()
            self.cp(sg, pv, reads=[rp], writes=[rsg], eng="act")
            self.dma(_h(S["H2T"][:, ti * 128:(ti + 1) * 128], "(k p) t -> p k t", p=128), _h(sg, "p (k t) -> p k t", k=8),
                     reads=[rsg], key=("h2s", id(rsg)), q="act")
        pipe.push(self.norm_mod_T(x1, rx1, g, 4, 3, dst, bufs))

    for tc in range(TOK // 512):
        cs = slice(tc * 512, (tc + 1) * 512)
        mT_, rmT_ = mTr.next()
        mT3_ = _h(mT_, "p (k t) -> p k t", k=8)
        ots = []
        for i in range(4):
            ot, rot_ = otr.next()
            self.dma(_h(ot, "p (k t) -> p k t", k=4), _h(S["OT"][i][:, cs], "(k p) t -> p k t", p=128), writes=[rot_], key=("ot", id(rot_)))
            ots.append((_h(ot, "p (k t) -> p k t", k=4), rot_))
        if tc == 0:
            self.warm(reads=[rwb, rwo] + [o[1] for o in ots])
        for db in range(8):
            ab = accb.next()
            pacc = self.psb(ab)
            gated = []
            for i in range(4):
                pb = mmr.next()
                ps = self.psb(pb)
                for kc in range(4):
                    self.mm(ps, wb4[:, i, kc, db * 128:(db + 1) * 128], ots[i][0][:, kc, :], kc == 0, kc == 3,
                            reads=[rwb, ots[i][1]], writes=[self.r_ps[pb]])
                g, rg = gtr.next()
                self.dma(g, S["GT"][i * 1024 + db * 128:i * 1024 + (db + 1) * 128, cs], writes=[rg], key=("gt", id(rg)))
                tb_, rtb = gbr.next()
                self.tt(tb_, ps, g, ALU.mult, reads=[self.r_ps[pb], rg], writes=[rtb])
                gated.append((tb_, rtb))
                if i >= 1:
                    pt_, rpt_ = gated[i - 1]
                    self.mm(pacc, self.identb, pt_, i == 1, False, reads=[rpt_], writes=[self.r_ps[ab]])
            pt_, rpt_ = gated[3]
            self.mm(pacc, self.identb, pt_, False, True, reads=[rpt_], writes=[self.r_ps[ab]])
            self.cp(mT3_[:, db, :], pacc, reads=[self.r_ps[ab]], writes=[rmT_], eng="act")
            if db % 2 == 1 and pending:
                pending.pop(0)()
        while pending:
            pending.pop(0)()
        for tt_ in range(4):
            pending.append(lambda tc=tc, tt_=tt_, m3=mT3_, rm=rmT_: tile_part2(tc, tt_, m3, rm))
    while pending:
        pending.pop(0)()
    pipe.flush()
    self.P.barrier()
    self.A.pop()


def _phase_G(self, l, last):
    d = self.din
    S = self.scr
    self.A.push()
    NJ = D_FF // 128
    TB = 9
    JG = 4
    self.load_mod((5,))
    w2, rw2 = self.alloc(NJ * D, BF16, "w2")
    w23 = _h(w2, "p (j c) -> p j c", j=NJ)
    for q in range(2):
        self.dma(w23[:, q * 11:(q + 1) * 11, :], _h(d["w_ffn_out"][l][q * 1408:(q + 1) * 1408, :], "(j p) c -> p j c", p=128),
                 writes=[rw2], key="w2", q="pool")
    h2r = self.rot(1, 8 * TB * 128, BF16, "h2blk")
    actT, raT = self.alloc(NJ * TB * 128, BF16, "actT")
    a3 = _h(actT, "p (j t) -> p j t", j=NJ)
    wsr = self.rot(2, 8 * 2 * JG * 128, BF16, "w1sl")
    sgr = self.rot(3, 384, F32, "sg")
    x1r = self.rot(2, D, F32, "x1G")
    x2r = self.rot(2, D, F32, "x2G")
    tmpr = self.rot(2, 512, F32, "tmpG")
    junk, rj = self.alloc(D, BF16, "junkG")
    ssr = self.rot(2, 4, F32, "ssG")
    mmr = Rot([0, 1, 2, 3, 4, 5, 6, 7])
    jgroups = [list(range(j0, min(j0 + JG, NJ))) for j0 in range(0, NJ, JG)]
    for blk in range(NT // TB):
        t0 = blk * TB * 128
        h2, rh2 = h2r.next()
        h23 = _h(h2, "p (k t) -> p k t", k=8)
        self.dma(h23, _h(S["H2T"][:, t0:t0 + TB * 128], "(k p) t -> p k t", p=128), writes=[rh2], key=("h2blk", id(rh2)))
        for js in jgroups:
            nj = len(js)
            ws, rws = wsr.next()
            ws4 = _h(ws, "p (k u c) -> p k u c", k=8, u=2)
            c0 = js[0] * 128
            self.dma(ws4[:, :, 0, 0:nj * 128], _h(d["w_ffn_in"][l][:, c0:c0 + nj * 128], "(k p) c -> p k c", p=128), writes=[rws],
                     key=("w1sl", id(rws)), q="pool")
            self.dma(ws4[:, :, 1, 0:nj * 128], _h(d["w_ffn_in"][l][:, D_FF + c0:D_FF + c0 + nj * 128], "(k p) c -> p k c", p=128),
                     writes=[rws], key=("w1sl", id(rws)), q="pool")
            for ji, j in enumerate(js):
                for sub in range(TB * 128 // 384):
                    ss_ = slice(sub * 384, (sub + 1) * 384)
                    pg = mmr.next(); pu = mmr.next()
                    for k in range(8):
                        self.mm(self.psb(pg)[:, 0:384], ws4[:, k, 0, ji * 128:(ji + 1) * 128], h23[:, k, ss_], k == 0, k == 7,
                                reads=[rws, rh2], writes=[self.r_ps[pg]])
                    for k in range(8):
                        self.mm(self.psb(pu)[:, 0:384], ws4[:, k, 1, ji * 128:(ji + 1) * 128], h23[:, k, ss_], k == 0, k == 7,
                                reads=[rws, rh2], writes=[self.r_ps[pu]])
                    sg, rsg = sgr.next()
                    self.act(sg, self.psb(pg)[:, 0:384], AF.Silu, reads=[self.r_ps[pg]], writes=[rsg])
                    self.tt(a3[:, j, ss_], sg, self.psb(pu)[:, 0:384], ALU.mult, reads=[rsg, self.r_ps[pu]], writes=[raT])
        for tt_ in range(TB):
            ti = blk * TB + tt_
            g = 0 if ti < NTS else 1
            x1, rx1 = x1r.next()
            self.dma(x1, S["XRES"][ti * 128:(ti + 1) * 128, :], reads=[self.r_xres[ti]], writes=[rx1], key=("x1G", id(rx1)))
            x2, rx2 = x2r.next()
            for nh in range(2):
                hs = slice(nh * 512, (nh + 1) * 512)
                pb = mmr.next()
                ps = self.psb(pb)
                for j in range(NJ):
                    self.mm(ps, a3[:, j, tt_ * 128:(tt_ + 1) * 128], w23[:, j, hs], j == 0, j == NJ - 1, reads=[raT, rw2], writes=[self.r_ps[pb]])
                tmp, rtmp = tmpr.next()
                self.tt(tmp, ps, self.mod(g, 5)[:, hs], ALU.mult, reads=[self.r_ps[pb], self.r_modt], writes=[rtmp])
                self.tt(x2[:, hs], tmp, x1[:, hs], ALU.add, reads=[rtmp, rx1], writes=[rx2], eng="pool")
            if not last:
                self.dma(S["XRES"][ti * 128:(ti + 1) * 128, :], x2, reads=[rx2], writes=[self.r_xres[ti]], key=("x2G", id(rx2)))
            else:
                ssb, rss = ssr.next()
                self.act(junk, x2, AF.Square, reads=[rx2], writes=[rj, rss], accum_out=ssb[:, 0:1])
                self.rstd(ssb[:, 2:3], ssb[:, 0:1], 1.0 / D, reads=[rss], writes=[rss], tmp=ssb[:, 1:2])
                self.stt(x2, x2, ssb[:, 2:3], self.gfin, ALU.mult, ALU.mult, reads=[rx2, rss], writes=[rx2])
                if ti < NTS:
                    dsto = self.dout["y_s"][ti * 128:(ti + 1) * 128, :]
                else:
                    dsto = self.dout["y_p"][(ti - NTS) * 128:(ti - NTS + 1) * 128, :]
                self.dma(dsto, x2, reads=[rx2], key=("x2G", id(rx2)))
    self.P.barrier()
    self.A.pop()


KB.phase_F = _phase_F
KB.phase_G = _phase_G


_CACHE = {}


def _get_program():
    if "nc" not in _CACHE:
        kb = KB()
        _CACHE["nc"] = kb.build()
        _CACHE["kb"] = kb
    return _CACHE["nc"]


def kernel(**inputs):
    inputs = {k: np.asarray(v) for k, v in inputs.items()}
    nc = _get_program()
    consts = _host_consts()
    nab = _na_bias_tiles(inputs["na_rpb"].astype(np.float32))
    in_maps = [_core_inputs(inputs, c, consts, nab) for c in range(8)]
    res = run_bass_kernel_spmd(nc, in_maps, core_ids=list(range(8)))
    rs = res.results
    f32 = np.float32
    y_prompt = np.concatenate([np.asarray(r["y_p"], f32).reshape(2, 256, D) for r in rs], 0)
    y_sample = np.stack([np.asarray(r["y_s"], f32) for r in rs], 0)
    na_kv = np.concatenate([np.asarray(r["o_na"], f32).reshape(2, 2, 2, 256, 8, 64) for r in rs], 0)
    gqa_kv = np.concatenate([np.asarray(r["o_g"], f32).reshape(2, 2, 2, 256, 2, 64) for r in rs], 0)
    swa_kv = np.concatenate([np.asarray(r["o_s"], f32).reshape(2, 2, 2, 256, 2, 64) for r in rs], 0)
    ml_C = np.concatenate([np.asarray(r["o_C"], f32) for r in rs], 0)
    ml_n = np.concatenate([np.asarray(r["o_n"], f32) for r in rs], 0)
    ml_m = np.concatenate([np.asarray(r["o_m"], f32) for r in rs], 0)
    return (y_prompt, y_sample, na_kv, gqa_kv, swa_kv, ml_C, ml_n, ml_m)
```
